# Optimizing a Trainium2 kernel written in Bass

```python
import jax, jax.numpy as jnp
from jax import lax
import numpy as np

D_MODEL = 1024
BATCH = 4
SEQ = 8192
DEPTH = 1

CHUNK = 64
SGU_BLOCK = 128
SGU_WIDTH = D_MODEL
SGU_GROUP = 128
SGU_GROUPS = SGU_WIDTH // SGU_GROUP
HGRN_EXPAND = 128
HGRN_HEADS = D_MODEL // HGRN_EXPAND
HGRN_KEY = HGRN_HEADS * HGRN_EXPAND
HGRN_HEAD_V = D_MODEL // HGRN_HEADS
HGRN_VALUE = HGRN_HEADS * HGRN_HEAD_V
N_BRANCHES = 2
FFN_HIDDEN = -(-8 * D_MODEL // (3 * 256)) * 256
IN_WIDTH = 2 * SGU_WIDTH + 2 * HGRN_KEY + 2 * HGRN_VALUE + N_BRANCHES * D_MODEL
SPLIT_AT = [int(s) for s in np.cumsum([SGU_WIDTH, SGU_WIDTH, HGRN_KEY, HGRN_KEY,
                                       HGRN_VALUE, HGRN_VALUE, D_MODEL, D_MODEL])[:-1]]
EPS = 1e-6

kernel_name = "hybrid_sgu_hgrn2_sandwich_block"


def rms_norm(x, gain):
    xf = x.astype(jnp.float32)
    y = xf * lax.rsqrt(jnp.mean(jnp.square(xf), axis=-1, keepdims=True) + EPS)
    return (y * gain.astype(jnp.float32)).astype(x.dtype)


def layer_norm(x, gain, bias):
    xf = x.astype(jnp.float32)
    mu = jnp.mean(xf, axis=-1, keepdims=True)
    var = jnp.mean(jnp.square(xf - mu), axis=-1, keepdims=True)
    y = (xf - mu) * lax.rsqrt(var + EPS)
    return (y * gain.astype(jnp.float32) + bias.astype(jnp.float32)).astype(x.dtype)


def spatial_gating(u, v, w_s, b_s, gain, bias):
    B_, S_, _ = v.shape
    v = layer_norm(v, gain, bias)
    c = jnp.arange(SGU_BLOCK) // CHUNK
    mask = c[None, :] <= c[:, None]
    w = jnp.where(mask[None], w_s, jnp.zeros_like(w_s))
    vb = v.reshape(B_, S_ // SGU_BLOCK, SGU_BLOCK, SGU_GROUPS, SGU_GROUP)
    mixed = jnp.einsum('gts,bnsgc->bntgc', w, vb) + b_s.T[None, None, :, :, None]
    return u * mixed.reshape(B_, S_, SGU_WIDTH)


def chunk_recurrence(q, k, v, logf):
    B_, S_, H, dk = q.shape
    dv = v.shape[-1]
    n = S_ // CHUNK

    def to_chunks(t):
        return t.reshape(B_, n, CHUNK, H, t.shape[-1]).transpose(1, 0, 3, 2, 4)

    causal = jnp.tril(jnp.ones((CHUNK, CHUNK), dtype=bool))[..., None]

    def step(state, inp):
        qc, kc, vc, gc = inp
        b = jnp.cumsum(gc, axis=2)
        decay = jnp.exp(jnp.where(causal, b[:, :, :, None, :] - b[:, :, None, :, :], -jnp.inf))
        scores = jnp.einsum('bhtc,bhtsc,bhsc->bhts', qc, decay, kc)
        out = (jnp.einsum('bhts,bhsv->bhtv', scores, vc)
               + jnp.einsum('bhtc,bhcv->bhtv', qc * jnp.exp(b), state))
        b_end = b[:, :, -1:, :]
        state = (jnp.exp(b[:, :, -1, :])[..., None] * state
                 + jnp.einsum('bhsc,bhsv->bhcv', kc * jnp.exp(b_end - b), vc))
        return state, out

    state0 = jnp.zeros((B_, H, dk, dv), jnp.float32)
    _, out = lax.scan(step, state0, (to_chunks(q), to_chunks(k), to_chunks(v), to_chunks(logf)))
    return out.transpose(1, 0, 3, 2, 4).reshape(B_, S_, H, dv)


def hgrn2(zq, zf, zi, zg, lb, norm_gain):
    B_, S_, _ = zq.shape
    H = HGRN_HEADS
    q = jax.nn.silu(zq.astype(jnp.float32)).reshape(B_, S_, H, HGRN_EXPAND)
    f = lb + (1.0 - lb) * jax.nn.sigmoid(zf.astype(jnp.float32))
    logf = jnp.log(f).reshape(B_, S_, H, HGRN_EXPAND)
    k = (1.0 - f).reshape(B_, S_, H, HGRN_EXPAND)
    v = zi.astype(jnp.float32).reshape(B_, S_, H, HGRN_HEAD_V)
    o = chunk_recurrence(q, k, v, logf)
    o = rms_norm(o, norm_gain.reshape(H, HGRN_HEAD_V)).reshape(B_, S_, HGRN_VALUE)
    o = o * jax.nn.silu(zg.astype(jnp.float32))
    return o.astype(zq.dtype)


def setup_inputs(seed: int = 0) -> dict:
    key = jax.random.key(seed)
    ks = jax.random.split(key, 17)

    def nrm(k, shape, scale):
        return jax.random.normal(k, shape, jnp.float32) * scale

    def gain(k, n):
        return 1.0 + nrm(k, (DEPTH, n), 0.05)

    return {
        "x": nrm(ks[0], (BATCH, SEQ, D_MODEL), 1.0),
        "pre_mix_gain": gain(ks[1], D_MODEL),
        "w_in": nrm(ks[2], (DEPTH, D_MODEL, IN_WIDTH), D_MODEL ** -0.5),
        "sgu_norm_gain": gain(ks[3], SGU_WIDTH),
        "sgu_norm_bias": nrm(ks[4], (DEPTH, SGU_WIDTH), 0.02),
        "w_spatial": nrm(ks[5], (DEPTH, SGU_GROUPS, SGU_BLOCK, SGU_BLOCK), 0.5 * SGU_BLOCK ** -0.5),
        "b_spatial": 1.0 + nrm(ks[6], (DEPTH, SGU_GROUPS, SGU_BLOCK), 0.1),
        "lb_logits": nrm(ks[7], (DEPTH + 1, HGRN_KEY), 0.5),
        "hgrn_norm_gain": gain(ks[8], HGRN_VALUE),
        "w_proj_sgu": nrm(ks[9], (DEPTH, SGU_WIDTH, D_MODEL), SGU_WIDTH ** -0.5),
        "w_proj_hgrn": nrm(ks[10], (DEPTH, HGRN_VALUE, D_MODEL), HGRN_VALUE ** -0.5),
        "w_out": nrm(ks[11], (DEPTH, D_MODEL, D_MODEL), D_MODEL ** -0.5),
        "post_mix_gain": gain(ks[12], D_MODEL),
        "pre_ffn_gain": gain(ks[13], D_MODEL),
        "w_ffn_up": nrm(ks[14], (DEPTH, D_MODEL, 2 * FFN_HIDDEN), D_MODEL ** -0.5),
        "w_ffn_down": nrm(ks[15], (DEPTH, FFN_HIDDEN, D_MODEL), FFN_HIDDEN ** -0.5),
        "post_ffn_gain": gain(ks[16], D_MODEL),
    }


def reference(x, pre_mix_gain, w_in, sgu_norm_gain, sgu_norm_bias, w_spatial, b_spatial,
              lb_logits, hgrn_norm_gain, w_proj_sgu, w_proj_hgrn, w_out, post_mix_gain,
              pre_ffn_gain, w_ffn_up, w_ffn_down, post_ffn_gain):
    lower_bounds = jnp.cumsum(jax.nn.softmax(lb_logits.astype(jnp.float32), axis=0), axis=0)
    for l in range(DEPTH):
        h = rms_norm(x, pre_mix_gain[l])
        z = h @ w_in[l]
        zu, zv, zq, zf, zi, zg, gate_a, gate_b = jnp.split(z, SPLIT_AT, axis=-1)
        y_a = spatial_gating(jax.nn.gelu(zu, approximate=False), jax.nn.gelu(zv, approximate=False),
                             w_spatial[l], b_spatial[l], sgu_norm_gain[l], sgu_norm_bias[l])
        y_b = hgrn2(zq, zf, zi, zg, lower_bounds[l], hgrn_norm_gain[l])
        merged = (jax.nn.sigmoid(gate_a) * (y_a @ w_proj_sgu[l])
                  + jax.nn.sigmoid(gate_b) * (y_b @ w_proj_hgrn[l]))
        x = x + rms_norm(merged @ w_out[l], post_mix_gain[l])
        h = rms_norm(x, pre_ffn_gain[l])
        g, up = jnp.split(h @ w_ffn_up[l], 2, axis=-1)
        x = x + rms_norm((jax.nn.silu(g) * up) @ w_ffn_down[l], post_ffn_gain[l])
    return x
```

```python
import numpy as np
import concourse.bass as bass
import concourse.mybir as mybir
from concourse.bass_utils import run_bass_kernel_spmd

F32 = mybir.dt.float32
BF16 = mybir.dt.bfloat16
AF = mybir.ActivationFunctionType
ALU = mybir.AluOpType

D = 1024
NTOK = 4096
T = 512
NT = NTOK // T
FH = 2816
EPS = 1e-6
NCOL = 48


class Eng:
    def __init__(self, nc, name, h, kind):
        self.h = h
        self.kind = kind
        self.sem = nc.alloc_semaphore("se_" + name)
        self.cnt = 0
        self.waited = {}


class DSem:
    def __init__(self, nc, name):
        self.sem = nc.alloc_semaphore("sd_" + name)
        self.cnt = 0


class Buf:
    def __init__(self, name):
        self.name = name
        self.wr = {}
        self.rd = {}
        self.ov = []


def overlap(*bufs):
    for a in bufs:
        for b in bufs:
            if a is not b and b not in a.ov:
                a.ov.append(b)


class K:
    def __init__(self, nc):
        self.nc = nc
        self.pe = Eng(nc, "pe", nc.tensor, "pe")
        self.act = Eng(nc, "act", nc.scalar, "c")
        self.dve = Eng(nc, "dve", nc.vector, "c")
        self.pool = Eng(nc, "pool", nc.gpsimd, "c")
        self.sp = Eng(nc, "sp", nc.sync, "q")
        self.sems = [e.sem for e in (self.pe, self.act, self.dve, self.pool, self.sp)]
        self.semobj = {}

    def dsem(self, name):
        d = DSem(self.nc, name)
        self.sems.append(d.sem)
        return d

    def _deps(self, eng, reads, writes):
        need = {}

        def add(dct, same_ok):
            for key, (sem, val, src) in dct.items():
                if src is eng:
                    if eng.kind == "pe":
                        continue
                if need.get(key, (None, 0))[1] < val:
                    need[key] = (sem, val)

        for b in reads:
            for o in [b] + b.ov:
                add(o.wr, True)
        for b in writes:
            for o in [b] + b.ov:
                add(o.wr, False)
                add(o.rd, False)
        for key, (sem, val) in need.items():
            if eng.waited.get(key, 0) < val:
                eng.h.wait_ge(sem, val)
                eng.waited[key] = val

    def _mark(self, tok, reads, writes):
        key = id(tok[0])
        for b in writes:
            b.wr[key] = tok
        for b in reads:
            b.rd[key] = tok

    def op(self, eng, fn, reads=(), writes=()):
        self._deps(eng, reads, writes)
        ins = fn(eng.h)
        eng.cnt += 1
        ins.then_inc(eng.sem, 1)
        self._mark((eng.sem, eng.cnt, eng), reads, writes)

    def dma(self, q, ds, out, in_, reads=(), writes=()):
        self._deps(q, reads, writes)
        ins = q.h.dma_start(out=out, in_=in_)
        ds.cnt += 16
        ins.then_inc(ds.sem, 16)
        self._mark((ds.sem, ds.cnt, None), reads, writes)


def build_nc(ntiles_main=NT, ntiles_pre=NT, stop=99):
    nc = bass.Bass("TRN2", target_bir_lowering=False)

    def din(name, shape, dt=F32):
        return nc.dram_tensor(name, shape, dt, kind="ExternalInput").ap()

    x_main = din("x_main", [NTOK, D])
    x_prev = din("x_prev", [NTOK, D])
    w_in = din("w_in", [D, 8192])
    w_pa = din("w_pa", [D, D])
    w_pb = din("w_pb", [D, D])
    w_out = din("w_out", [D, D])
    w_up = din("w_up", [D, 2 * FH])
    w_dn = din("w_dn", [FH, D])
    cols_d = din("cols", [128, NCOL])
    rows_d = din("rows_bc", [128, 2 * D])
    wspT_d = din("wspT", [128, 8 * 128])
    cst_d = din("consts", [128, 128 + 256 + 512 + 128])
    sguA_d = din("sguA", [2, D])
    bsp_d = din("bsp", [1, D])
    y_out = nc.dram_tensor("y_out", [NTOK, D], F32, kind="ExternalOutput").ap()

    def dscr(name, shape):
        return nc.dram_tensor(name, shape, BF16, kind="Internal").ap()

    wb_in = dscr("wb_in", [D, 8192])
    wb_pa = dscr("wb_pa", [D, D])
    wb_pb = dscr("wb_pb", [D, D])
    wb_out = dscr("wb_out", [D, D])
    wb_up = dscr("wb_up", [D, 2 * FH])
    wb_dn = dscr("wb_dn", [FH, D])

    k = K(nc)
    PE, ACT, DVE, POOL, SP = k.pe, k.act, k.dve, k.pool, k.sp

    def sb(name, shape, dt):
        return nc.alloc_sbuf_tensor("s_" + name, shape, dt)

    cols = sb("cols", [128, NCOL], F32)
    der = sb("der", [128, 64], F32)
    rows = sb("rows", [128, 2 * D], F32)
    cst = sb("cst", [128, 1024], F32)
    identb = sb("identb", [128, 128], BF16)
    onesf = sb("onesf", [128, 128], F32)
    wspb = sb("wspb", [128, 8, 128], BF16)
    cg = sb("cg", [128, 8, 128], F32)
    X = [sb(f"x{s}", [128, D], F32) for s in range(4)]
    wspf = X[1].rearrange("p (g t) -> p g t", t=128)
    sguA = X[2][0:2, :]
    sguB = X[3][0:2, :]
    HT = sb("hT", [128, 8, T], BF16)
    RR = sb("RR", [128, 12288], BF16)
    R4 = sb("R4", [128, 8, T], BF16)
    R5 = sb("R5", [128, 8, T], BF16)
    S32 = sb("S32", [128, 8, 128], F32)
    SBF = sb("SBF", [128, 8, 9, 128], BF16)
    decs = sb("decs", [128, 8, 8], F32)
    scm = [sb(f"scm{i}", [128, 256], BF16) for i in range(2)]
    Bt = [sb(f"Bt{i}", [128, T], F32) for i in range(4)]
    TA = [sb(f"ta{i}", [128, T], F32) for i in range(3)]
    P1 = [sb(f"P1_{i}", [128, T], F32) for i in range(4)]
    P2 = [sb(f"P2_{i}", [128, T], F32) for i in range(4)]
    P3 = [sb(f"P3_{i}", [128, T], F32) for i in range(4)]
    bia = sb("bia", [128, 4], F32)
    kppT = [sb(f"kppT{i}", [128, T], BF16) for i in range(4)]
    junk = sb("junk", [128, D], BF16)
    hb = [sb(f"hb{i}", [128, D], BF16) for i in range(2)]
    BT = [sb(f"BT{i}", [128, D], F32) for i in range(2)]
    stat = sb("stat", [128, 64], F32)
    RING = sb("RING", [128, 4 * 5632], BF16)

    U = RR[:, 0:4096].rearrange("p (k n) -> p k n", n=T)
    VLN = RR[:, 4096:8192].rearrange("p (s n) -> p s n", n=D)
    QP = RR[:, 4096:6144].rearrange("p (k n) -> p k n", n=T)
    KP = RR[:, 6144:8192].rearrange("p (k n) -> p k n", n=T)
    KPP = RR[:, 8192:10240].rearrange("p (s n) -> p s n", n=T)
    VI = RR[:, 10240:12288].rearrange("p (s n) -> p s n", n=T)
    GB = U
    FACT = RR[:, 0:11264].rearrange("p (k n) -> p k n", n=T)

    MMt = [nc.alloc_psum_tensor(f"mm{i}", [128, T], F32) for i in range(3)]
    STt2 = [nc.alloc_psum_tensor(f"st{i}", [128, 4, 128], F32) for i in range(2)]
    STt = STt2[0]
    TPt = nc.alloc_psum_tensor("tp", [128, 8, 128], BF16)
    OTt = [nc.alloc_psum_tensor(f"ot{i}", [128, T], F32) for i in range(2)]

    b_cols, b_der, b_rows, b_cst, b_identb, b_onesf = (Buf(n) for n in ("cols", "der", "rows", "cst", "identb", "onesf"))
    b_wspf, b_wspb, b_cg, b_sguA, b_sguB = (Buf(n) for n in ("wspf", "wspb", "cg", "sguA", "sguB"))
    b_X = [Buf(f"x{s}") for s in range(4)]
    overlap(b_X[1], b_wspf)
    overlap(b_X[2], b_sguA)
    overlap(b_X[3], b_sguB)
    b_HT = Buf("hT")
    b_U, b_VLN, b_QP, b_KP, b_KPP, b_VI, b_FACT = (Buf(n) for n in ("U", "VLN", "QP", "KP", "KPP", "VI", "FACT"))
    overlap(b_U, b_FACT)
    overlap(b_VLN, b_QP, b_FACT)
    overlap(b_VLN, b_KP, b_FACT)
    overlap(b_KPP, b_FACT)
    overlap(b_VI, b_FACT)
    b_R4, b_R5 = Buf("R4"), Buf("R5")
    b_S32 = [Buf(f"S32_{h}") for h in range(8)]
    b_SBF = [Buf(f"SBF_{h}") for h in range(8)]
    b_decs = [Buf(f"decs{h}") for h in range(8)]
    b_scm = [Buf("scm0"), Buf("scm1")]
    b_Bt = [Buf(f"Bt{i}") for i in range(4)]
    b_TA = [Buf(f"ta{i}") for i in range(3)]
    b_P1 = [Buf(f"P1_{i}") for i in range(4)]
    b_P2 = [Buf(f"P2_{i}") for i in range(4)]
    b_P3 = [Buf(f"P3_{i}") for i in range(4)]
    b_bia = [Buf(f"bia{i}") for i in range(4)]
    b_kppT = [Buf(f"kppT{i}") for i in range(4)]
    b_junk = Buf("junk")
    b_hb = [Buf("hb0"), Buf("hb1")]
    b_BT = [Buf("BT0"), Buf("BT1")]
    b_stat = Buf("stat")
    b_ring = [Buf(f"ring{i}") for i in range(4)]
    b_stb = [Buf("stb0"), Buf("stb1")]
    b_MM = [Buf(f"mm{i}") for i in range(3)]
    b_ST2 = [Buf(f"st{i}") for i in range(2)]
    b_ST = [b_ST2[0]]
    b_TP = Buf("tp")
    b_OT = [Buf("ot0"), Buf("ot1")]
    b_wb = {n: Buf("wb_" + n) for n in ("in", "pa", "pb", "out", "up", "dn")}

    d_const = k.dsem("const")
    d_ring = [k.dsem(f"ring{i}") for i in range(4)]
    d_stl = [k.dsem(f"stl{i}") for i in range(3)]
    d_conv = [k.dsem("conv0"), k.dsem("conv1")]
    d_xl = [k.dsem(f"xl{s}") for s in range(4)]
    d_xs = [k.dsem(f"xs{s}") for s in range(4)]

    def finish():
        for s in range(4):
            if d_xs[s].cnt:
                nc.sync.wait_ge(d_xs[s].sem, d_xs[s].cnt)
        nc.all_engine_barrier()
        for sm in k.sems:
            nc.gpsimd.sem_clear(sm)
        return nc

    for sm in k.sems:
        nc.gpsimd.sem_clear(sm)
    nc.all_engine_barrier()

    k.dma(SP, d_const, cols[:], cols_d[:, :], writes=[b_cols])
    k.dma(SP, d_const, rows[:], rows_d[:, :], writes=[b_rows])
    k.dma(SP, d_const, cst[:], cst_d[:, :], writes=[b_cst])
    k.dma(SP, d_const, X[1][:], wspT_d[:, :], writes=[b_wspf])
    k.dma(SP, d_const, sguA, sguA_d[:, :], writes=[b_sguA])
    k.dma(SP, d_const, sguB[1:2, :], bsp_d[:, :], writes=[b_sguB])
    for b_ in (b_cols, b_rows, b_cst, b_wspf, b_sguA, b_sguB):
        b_.wr[id(d_const.sem)] = (d_const.sem, d_const.cnt, None)
    ident_f = cst[:, 0:128]
    mask_sc = cst[:, 128:384]
    mask_scan = cst[:, 384:896]
    mask_sgu = cst[:, 896:1024]

    k.op(DVE, lambda e: e.tensor_copy(out=identb[:], in_=ident_f), reads=[b_cst], writes=[b_identb])
    k.op(POOL, lambda e: e.memset(onesf[:], 1.0), writes=[b_onesf])
    k.op(POOL, lambda e: e.memset(der[:, 40:48], -0.5), writes=[b_der])
    k.op(POOL, lambda e: e.memset(S32.rearrange("p h v -> p (h v)"), 0.0), writes=b_S32)
    k.op(POOL, lambda e: e.memset(SBF.rearrange("p h j v -> p (h j v)"), 0.0), writes=b_SBF)
    k.op(DVE, lambda e: e.tensor_tensor(out=der[:, 24:32], in0=cols[:, 40:48], in1=cols[:, 32:40], op=ALU.subtract),
         reads=[b_cols], writes=[b_der])
    k.op(ACT, lambda e: e.activation(out=der[:, 24:32], in_=der[:, 24:32], func=AF.Exp), reads=[b_der], writes=[b_der])
    k.op(DVE, lambda e: e.tensor_scalar(out=der[:, 32:40], in0=der[:, 24:32], scalar1=1.0, scalar2=None, op0=ALU.add),
         reads=[b_der], writes=[b_der])
    k.op(DVE, lambda e: e.reciprocal(out=der[:, 0:8], in_=der[:, 32:40]), reads=[b_der], writes=[b_der])
    k.op(DVE, lambda e: e.tensor_tensor(out=der[:, 8:16], in0=der[:, 24:32], in1=der[:, 0:8], op=ALU.mult),
         reads=[b_der], writes=[b_der])
    k.op(ACT, lambda e: e.activation(out=der[:, 16:24], in_=der[:, 8:16], func=AF.Ln), reads=[b_der], writes=[b_der])
    LB = lambda h: der[:, h:h + 1]
    LNOML = lambda h: der[:, 16 + h:17 + h]
    NEGH = der[:, 40:41]

    w3 = wspf
    ones512 = onesf[:, 0:1].to_broadcast([128, T])
    k.op(DVE, lambda e: e.tensor_tensor(out=w3, in0=w3, in1=mask_sgu.unsqueeze(1).to_broadcast([128, 8, 128]), op=ALU.mult),
         reads=[b_wspf, b_cst], writes=[b_wspf])
    k.op(POOL, lambda e: e.tensor_copy(out=wspb[:], in_=wspf), reads=[b_wspf], writes=[b_wspb])

    def f_wsum2(e):
        ins = None
        for g in range(8):
            dst = MMt[g // 4][0:1, (g % 4) * 128:(g % 4 + 1) * 128]
            ins = e.matmul(dst, lhsT=onesf[:, 0:1], rhs=wspf[:, g, :], start=True, stop=True)
        return ins
    k.op(PE, f_wsum2, reads=[b_onesf, b_wspf], writes=[b_MM[0], b_MM[1]])
    k.op(DVE, lambda e: e.tensor_copy(out=sguB[0:1, 0:512], in_=MMt[0][0:1, :]), reads=[b_MM[0]], writes=[b_sguB])
    k.op(DVE, lambda e: e.tensor_copy(out=sguB[0:1, 512:1024], in_=MMt[1][0:1, :]), reads=[b_MM[1]], writes=[b_sguB])

    def f_cg(e):
        ins = None
        for g in range(8):
            dst = MMt[g // 4][:, (g % 4) * 128:(g % 4 + 1) * 128]
            ins = e.matmul(dst, lhsT=sguA[:, g * 128:(g + 1) * 128], rhs=sguB[:, g * 128:(g + 1) * 128], start=True, stop=True)
        return ins
    k.op(PE, f_cg, reads=[b_sguA, b_sguB], writes=[b_MM[0], b_MM[1]])
    k.op(DVE, lambda e: e.tensor_copy(out=cg[:, 0:4, :].rearrange("p g t -> p (g t)"), in_=MMt[0][:]), reads=[b_MM[0]], writes=[b_cg])
    k.op(DVE, lambda e: e.tensor_copy(out=cg[:, 4:8, :].rearrange("p g t -> p (g t)"), in_=MMt[1][:]), reads=[b_MM[1]], writes=[b_cg])

    if stop == 0:
        return finish()
    stf = [R4.rearrange("p k n -> p (k n)").bitcast(F32), R5.rearrange("p k n -> p (k n)").bitcast(F32), RR[:, 0:4096].bitcast(F32)]
    stb = [RR[:, 4096:6144], RR[:, 6144:8192]]
    b_stf = [Buf(f"stf{i}") for i in range(3)]
    overlap(b_stf[0], b_R4)
    overlap(b_stf[1], b_R5)
    overlap(b_stf[2], b_U, b_FACT)
    overlap(b_stb[0], b_VLN, b_QP, b_FACT)
    overlap(b_stb[1], b_VLN, b_KP, b_FACT)
    conv = []

    def conv_win(c0):
        for kc in range(8):
            conv.append((w_in[kc * 128:(kc + 1) * 128, c0:c0 + 2048], wb_in[kc * 128:(kc + 1) * 128, c0:c0 + 2048], cols[:, kc:kc + 1], "in"))
    conv_win(2048)
    conv_win(4096)
    n_phase0 = len(conv)
    conv_win(0)
    conv_win(6144)
    for nm, src, dst in (("pa", w_pa, wb_pa), ("pb", w_pb, wb_pb), ("out", w_out, wb_out)):
        for kc in range(8):
            conv.append((src[kc * 128:(kc + 1) * 128, :], dst[kc * 128:(kc + 1) * 128, :], None, nm))
    for kc in range(8):
        for c0, cw in ((0, 2048), (2048, 2048), (4096, 1536)):
            conv.append((w_up[kc * 128:(kc + 1) * 128, c0:c0 + cw], wb_up[kc * 128:(kc + 1) * 128, c0:c0 + cw], cols[:, 8 + kc:9 + kc], "up"))
    for kc in range(22):
        conv.append((w_dn[kc * 128:(kc + 1) * 128, :], wb_dn[kc * 128:(kc + 1) * 128, :], None, "dn"))
    nconv = len(conv)
    cv = {"next": 0, "loaded": 0, "stored": 0}

    def conv_load(i):
        src = conv[i][0]
        w = src.shape[1]
        k.dma(SP, d_stl[i % 3], stf[i % 3][:, 0:w], src, writes=[b_stf[i % 3]])

    def conv_cast(i):
        src, dst, gc, nm = conv[i]
        w = src.shape[1]
        j = i % 2
        if i % 2 == 0:
            if gc is None:
                k.op(DVE, lambda e: e.tensor_copy(out=stb[j][:, 0:w], in_=stf[i % 3][:, 0:w]), reads=[b_stf[i % 3]], writes=[b_stb[j]])
            else:
                k.op(DVE, lambda e: e.tensor_scalar(out=stb[j][:, 0:w], in0=stf[i % 3][:, 0:w], scalar1=gc, scalar2=None, op0=ALU.mult),
                     reads=[b_stf[i % 3], b_cols], writes=[b_stb[j]])
        else:
            if gc is None:
                k.op(ACT, lambda e: e.activation(out=stb[j][:, 0:w], in_=stf[i % 3][:, 0:w], func=AF.Copy), reads=[b_stf[i % 3]], writes=[b_stb[j]])
            else:
                k.op(ACT, lambda e: e.activation(out=stb[j][:, 0:w], in_=stf[i % 3][:, 0:w], func=AF.Copy, scale=gc),
                     reads=[b_stf[i % 3], b_cols], writes=[b_stb[j]])

    def conv_store(i):
        src, dst, gc, nm = conv[i]
        w = src.shape[1]
        j = i % 2
        k.dma(ACT, d_conv[j], dst, stb[j][:, 0:w], reads=[b_stb[j]], writes=[b_wb[nm]])

    def conv_run(upto, flush=False):
        upto = min(upto, nconv)
        while cv["next"] < upto:
            i = cv["next"]
            while cv["loaded"] < min(i + 3, nconv):
                conv_load(cv["loaded"])
                cv["loaded"] += 1
            conv_cast(i)
            if i - 1 >= cv["stored"]:
                conv_store(cv["stored"])
                cv["stored"] += 1
            cv["next"] += 1
        if flush:
            while cv["stored"] < cv["next"]:
                conv_store(cv["stored"])
                cv["stored"] += 1

    conv_run(n_phase0, flush=True)

    if stop == 1:
        return finish()
    ring_ctr = [0]

    def ring_slot():
        i = ring_ctr[0] % 4
        ring_ctr[0] += 1
        return i

    def slot_view(i, kk):
        return RING[:, i * 5632:i * 5632 + kk * 512].rearrange("p (k n) -> p k n", n=512)

    def load_k8(wb, nm, c0):
        i = ring_slot()
        k.dma(SP, d_ring[i], slot_view(i, 8), wb[:, c0:c0 + 512].rearrange("(k p) n -> p k n", p=128),
              reads=[b_wb[nm]], writes=[b_ring[i]])
        return i

    def load_up(blk):
        i = ring_slot()
        v = slot_view(i, 8)
        j = 2 * blk
        k.dma(SP, d_ring[i], v[:, :, 0:256], wb_up[:, j * 128:j * 128 + 256].rearrange("(k p) n -> p k n", p=128),
              reads=[b_wb["up"]], writes=[b_ring[i]])
        k.dma(SP, d_ring[i], v[:, :, 256:512], wb_up[:, FH + j * 128:FH + j * 128 + 256].rearrange("(k p) n -> p k n", p=128),
              reads=[b_wb["up"]], writes=[b_ring[i]])
        return i

    def load_dn(kh, nh):
        i = ring_slot()
        k.dma(SP, d_ring[i], slot_view(i, 11), wb_dn[kh * 1408:(kh + 1) * 1408, nh * 512:(nh + 1) * 512].rearrange("(k p) n -> p k n", p=128),
              reads=[b_wb["dn"]], writes=[b_ring[i]])
        return i

    mm_ctr = [0]

    def mm_bank():
        i = mm_ctr[0] % 3
        mm_ctr[0] += 1
        return i

    ta_ctr = [0]

    def ta():
        i = ta_ctr[0] % 3
        ta_ctr[0] += 1
        return TA[i], b_TA[i]

    st_ctr = [0]
    misc = {"stat": 0, "hb": 0, "bt": 0, "kppT": 0, "sc": 0, "ot": 0}

    def nxt(name, n):
        v = misc[name] % n
        misc[name] += 1
        return v

    b_stats = [Buf(f"stat{i}") for i in range(32)]

    def stat_cols(n):
        i = misc["stat"] % 32
        misc["stat"] += 1
        return stat[:, 2 * i:2 * i + n], b_stats[i]

    def fm_group(slot, chunk, rhs, rbufs):
        bi = mm_bank()
        v = slot_view(slot, 8)

        def f(e):
            ins = None
            for kc in range(8):
                ins = e.matmul(MMt[bi][:], lhsT=v[:, kc, chunk * 128:(chunk + 1) * 128], rhs=rhs[:, kc, :], start=(kc == 0), stop=(kc == 7))
            return ins
        k.op(PE, f, reads=[b_ring[slot]] + rbufs, writes=[b_MM[bi]])
        return bi

    def tm_group(slot, lhs, s, rbufs):
        bi = mm_bank()
        v = slot_view(slot, 8)

        def f(e):
            ins = None
            for kc in range(8):
                ins = e.matmul(MMt[bi][:], lhsT=lhs[:, kc, s * 128:(s + 1) * 128], rhs=v[:, kc, :], start=(kc == 0), stop=(kc == 7))
            return ins
        k.op(PE, f, reads=[b_ring[slot]] + rbufs, writes=[b_MM[bi]])
        return bi

    def rstd_small(ss_ap, ss_buf, n, scale):
        m, bm = stat_cols(n)
        r, br = stat_cols(n)
        k.op(POOL, lambda e: e.tensor_scalar(out=m, in0=ss_ap, scalar1=scale, scalar2=EPS, op0=ALU.mult, op1=ALU.add),
             reads=[ss_buf], writes=[bm])
        k.op(POOL, lambda e: e.tensor_tensor(out=r, in0=m, in1=NEGH, op=ALU.pow),
             reads=[bm, b_der], writes=[br])
        return r, br

    def norm_to_hT(s):
        norm_b(norm_a(s))

    def norm_a(s):
        ss, bss = stat_cols(1)
        k.op(ACT, lambda e: e.activation(out=junk[:], in_=X[s][:], func=AF.Square, accum_out=ss),
             reads=[b_X[s]], writes=[b_junk, bss])
        r, br = rstd_small(ss, bss, 1, 1.0 / D)
        hi = nxt("hb", 2)
        k.op(DVE, lambda e: e.tensor_scalar(out=hb[hi][:], in0=X[s][:], scalar1=r, scalar2=None, op0=ALU.mult),
             reads=[b_X[s], br], writes=[b_hb[hi]])
        return (s, hi)

    def norm_b(sh):
        s, hi = sh

        def f(e):
            ins = None
            for kc in range(8):
                ins = e.transpose(TPt[:, kc, :], hb[hi][:, kc * 128:(kc + 1) * 128], identb[:])
            return ins
        k.op(PE, f, reads=[b_hb[hi], b_identb], writes=[b_TP])
        k.op(DVE, lambda e: e.tensor_copy(out=HT[:, :, s * 128:(s + 1) * 128], in_=TPt[:]), reads=[b_TP], writes=[b_HT])

    def load_x(src, t, s):
        k.dma(SP, d_xl[s], X[s][:], src[t * T + s * 128:t * T + (s + 1) * 128, :], writes=[b_X[s]])

    def sbf_slot(gt, j):
        return (8 * gt + j) % 9

    def hgrn_group(gi, gt, main, last_pre=False):
        heads = [4 * gi + i for i in range(4)]
        sl_f = load_k8(wb_in, "in", 3072 + gi * 512)
        sl_i = load_k8(wb_in, "in", 4096 + gi * 512)
        if main:
            sl_g = load_k8(wb_in, "in", 5120 + gi * 512)
            sl_q = load_k8(wb_in, "in", 2048 + gi * 512)
        for hl, h in enumerate(heads):
            bi = fm_group(sl_f, hl, HT, [b_HT])
            z = MMt[bi]
            E, bE = ta()
            k.op(ACT, lambda e: e.activation(out=E[:], in_=z[:], func=AF.Exp, scale=-1.0), reads=[b_MM[bi]], writes=[bE])
            k.op(ACT, lambda e: e.activation(out=P2[hl][:], in_=E[:], func=AF.Ln, scale=LB(h), bias=1.0), reads=[bE, b_der], writes=[b_P2[hl]])
            k.op(ACT, lambda e: e.activation(out=P3[hl][:], in_=E[:], func=AF.Ln, scale=1.0, bias=1.0), reads=[bE], writes=[b_P3[hl]])
            k.op(DVE, lambda e: e.scalar_tensor_tensor(out=P1[hl][:], in0=z[:], scalar=-1.0, in1=P3[hl][:], op0=ALU.mult, op1=ALU.subtract),
                 reads=[b_MM[bi], b_P3[hl]], writes=[b_P1[hl]])
        for s in range(4):
            bi = tm_group(sl_i, HT, s, [b_HT])
            k.op(DVE, lambda e: e.tensor_copy(out=VI[:, s, :], in_=MMt[bi][:]), reads=[b_MM[bi]], writes=[b_VI])
        if main:
            for hp in range(0, 4, 2):
                st = []
                for hl in (hp, hp + 1):
                    bi = fm_group(sl_g, hl, HT, [b_HT])
                    E, bE = ta()
                    st.append((hl, heads[hl], bi, E, bE))
                    k.op(ACT, lambda e: e.activation(out=E[:], in_=MMt[bi][:], func=AF.Exp, scale=-1.0), reads=[b_MM[bi]], writes=[bE])
                for hl, h, bi, E, bE in st:
                    k.op(ACT, lambda e: e.activation(out=E[:], in_=E[:], func=AF.Ln, bias=1.0), reads=[bE], writes=[bE])
                for hl, h, bi, E, bE in st:
                    k.op(ACT, lambda e: e.activation(out=E[:], in_=E[:], func=AF.Exp, scale=-1.0), reads=[bE], writes=[bE])
                for hl, h, bi, E, bE in st:
                    k.op(DVE, lambda e: e.tensor_tensor(out=R5[:, h, :], in0=MMt[bi][:], in1=E[:], op=ALU.mult), reads=[b_MM[bi], bE], writes=[b_R5])
        for hl, h in enumerate(heads):
            k.op(POOL, lambda e: e.tensor_tensor(out=P2[hl][:], in0=P2[hl][:], in1=P3[hl][:], op=ALU.subtract),
                 reads=[b_P2[hl], b_P3[hl]], writes=[b_P2[hl]])
            B, bB = Bt[hl], b_Bt[hl]
            if main:
                k.op(DVE, lambda e: e.tensor_tensor_scan(out=B[:], data0=mask_scan, data1=P2[hl][:], initial=0.0, op0=ALU.mult, op1=ALU.add),
                     reads=[b_cst, b_P2[hl]], writes=[bB])
            else:
                k.op(DVE, lambda e: e.tensor_tensor_scan(out=B[:], data0=ones512, data1=P2[hl][:], initial=0.0, op0=ALU.mult, op1=ALU.add),
                     reads=[b_onesf, b_P2[hl]], writes=[bB])
            k.op(POOL, lambda e: e.tensor_tensor(out=P1[hl][:], in0=P1[hl][:], in1=B[:], op=ALU.subtract),
                 reads=[b_P1[hl], bB], writes=[b_P1[hl]])
            if main:
                B3 = B.rearrange("p (c l) -> p c l", l=64)
                k.op(POOL, lambda e: e.tensor_tensor(out=P3[hl].rearrange("p (c l) -> p c l", l=64), in0=P1[hl].rearrange("p (c l) -> p c l", l=64),
                                                     in1=B3[:, :, 63:64].to_broadcast([128, 8, 64]), op=ALU.add),
                     reads=[b_P1[hl], bB], writes=[b_P3[hl]])
            else:
                k.op(POOL, lambda e: e.tensor_tensor(out=bia[:, hl:hl + 1], in0=B[:, 511:512], in1=LNOML(h), op=ALU.add),
                     reads=[bB, b_der], writes=[b_bia[hl]])
        for hl, h in enumerate(heads):
            B = Bt[hl]
            if main:
                B3 = B.rearrange("p (c l) -> p c l", l=64)
                k.op(ACT, lambda e: e.activation(out=KP[:, hl, :], in_=P1[hl][:], func=AF.Exp, bias=LNOML(h)), reads=[b_P1[hl], b_der], writes=[b_KP])
                k.op(ACT, lambda e: e.activation(out=kppT[hl][:], in_=P3[hl][:], func=AF.Exp, bias=LNOML(h)), reads=[b_P3[hl], b_der], writes=[b_kppT[hl]])
                k.op(ACT, lambda e: e.activation(out=decs[:, h, :], in_=B3[:, :, 63], func=AF.Exp), reads=[b_Bt[hl]], writes=[b_decs[h]])
            else:
                k.op(ACT, lambda e: e.activation(out=kppT[hl][:], in_=P1[hl][:], func=AF.Exp, bias=bia[:, hl:hl + 1]), reads=[b_P1[hl], b_bia[hl]], writes=[b_kppT[hl]])
                k.op(ACT, lambda e: e.activation(out=decs[:, h, 0:1], in_=B[:, 511:512], func=AF.Exp), reads=[b_Bt[hl]], writes=[b_decs[h]])

        if main:
            for hp in range(0, 4, 2):
                st = []
                for hl in (hp, hp + 1):
                    bi = fm_group(sl_q, hl, HT, [b_HT])
                    E, bE = ta()
                    st.append((hl, heads[hl], bi, E, bE))
                    k.op(ACT, lambda e: e.activation(out=E[:], in_=MMt[bi][:], func=AF.Exp, scale=-1.0), reads=[b_MM[bi]], writes=[bE])
                for hl, h, bi, E, bE in st:
                    k.op(ACT, lambda e: e.activation(out=E[:], in_=E[:], func=AF.Ln, bias=1.0), reads=[bE], writes=[bE])
                    k.op(POOL, lambda e: e.tensor_tensor(out=E[:], in0=Bt[hl][:], in1=E[:], op=ALU.subtract), reads=[bE, b_Bt[hl]], writes=[bE])
                for hl, h, bi, E, bE in st:
                    k.op(ACT, lambda e: e.activation(out=E[:], in_=E[:], func=AF.Exp), reads=[bE], writes=[bE])
                for hl, h, bi, E, bE in st:
                    k.op(DVE, lambda e: e.tensor_tensor(out=QP[:, hl, :], in0=MMt[bi][:], in1=E[:], op=ALU.mult), reads=[b_MM[bi], bE], writes=[b_QP])
        for hl, h in enumerate(heads):
            def ftp(e):
                ins = None
                for s in range(4):
                    ins = e.transpose(TPt[:, s, :], kppT[hl][:, s * 128:(s + 1) * 128], identb[:])
                return ins
            k.op(PE, ftp, reads=[b_kppT[hl], b_identb], writes=[b_TP])
            k.op(DVE, lambda e: e.tensor_copy(out=KPP[:, :, hl * 128:(hl + 1) * 128], in_=TPt[:, 0:4, :]), reads=[b_TP], writes=[b_KPP])
        if not main:
            def fst(e):
                ins = None
                for hl in range(4):
                    for s in range(4):
                        ins = e.matmul(STt2[0][:, hl, :], lhsT=KPP[:, s, hl * 128:(hl + 1) * 128],
                                       rhs=VI[:, s, hl * 128:(hl + 1) * 128], start=(s == 0), stop=(s == 3))
                return ins
            k.op(PE, fst, reads=[b_KPP, b_VI], writes=[b_ST2[0]])
            for hl, h in enumerate(heads):
                k.op(DVE, lambda e: e.scalar_tensor_tensor(out=S32[:, h, :], in0=S32[:, h, :], scalar=decs[:, h, 0:1],
                                                           in1=STt2[0][:, hl, :], op0=ALU.mult, op1=ALU.add),
                     reads=[b_S32[h], b_decs[h], b_ST2[0]], writes=[b_S32[h]])
                if last_pre:
                    k.op(POOL, lambda e: e.tensor_copy(out=SBF[:, h, 0, :], in_=S32[:, h, :]), reads=[b_S32[h]], writes=[b_SBF[h]])
            return

    def hgrn_pass1(gi, gt):
        heads = [4 * gi + i for i in range(4)]
        for j in range(8):
            s, pb = j // 2, (j % 2) * 64
            sb_i = j % 2

            def fst(e):
                ins = None
                for hl in range(4):
                    ins = e.matmul(STt2[sb_i][:, hl, :], lhsT=KPP[pb:pb + 64, s, hl * 128:(hl + 1) * 128],
                                   rhs=VI[pb:pb + 64, s, hl * 128:(hl + 1) * 128], start=True, stop=True)
                return ins
            k.op(PE, fst, reads=[b_KPP, b_VI], writes=[b_ST2[sb_i]])
            for hl, h in enumerate(heads):
                k.op(DVE, lambda e: e.scalar_tensor_tensor(out=S32[:, h, :], in0=S32[:, h, :], scalar=decs[:, h, j:j + 1],
                                                           in1=STt2[sb_i][:, hl, :], op0=ALU.mult, op1=ALU.add),
                     reads=[b_S32[h], b_decs[h], b_ST2[sb_i]], writes=[b_S32[h]])
                so = sbf_slot(gt, j + 1)
                if hl % 2 == 0:
                    k.op(ACT, lambda e: e.activation(out=SBF[:, h, so, :], in_=S32[:, h, :], func=AF.Copy), reads=[b_S32[h]], writes=[b_SBF[h]])
                else:
                    k.op(POOL, lambda e: e.tensor_copy(out=SBF[:, h, so, :], in_=S32[:, h, :]), reads=[b_S32[h]], writes=[b_SBF[h]])
            yield

    def hgrn_pass2(gi, gt):
        heads = [4 * gi + i for i in range(4)]

        def sc(hl):
            ci = hl % 2
            SCv = STt2[ci].rearrange("p q v -> p (q v)")

            def fsc(e):
                ins = None
                for j in range(8):
                    pb = (j % 2) * 64
                    ins = e.matmul(SCv[pb:pb + 64, (j // 2) * 64:(j // 2 + 1) * 64], lhsT=KP[:, hl, j * 64:(j + 1) * 64],
                                   rhs=QP[:, hl, j * 64:(j + 1) * 64], start=True, stop=True)
                return ins
            k.op(PE, fsc, reads=[b_KP, b_QP], writes=[b_ST2[ci]])
            k.op(DVE, lambda e: e.tensor_tensor(out=scm[ci][:], in0=SCv[:, 0:256], in1=mask_sc, op=ALU.mult),
                 reads=[b_ST2[ci], b_cst], writes=[b_scm[ci]])

        sc(0)
        sc(1)
        yield
        for hl, h in enumerate(heads):
            ci = hl % 2
            oi = hl % 2

            def fo(e):
                ins = None
                for j in range(8):
                    s, pb = j // 2, (j % 2) * 64
                    si = sbf_slot(gt, j)
                    e.matmul(OTt[oi][:, j * 64:(j + 1) * 64], lhsT=VI[pb:pb + 64, s, hl * 128:(hl + 1) * 128],
                             rhs=scm[ci][pb:pb + 64, s * 64:(s + 1) * 64], start=True, stop=False)
                    ins = e.matmul(OTt[oi][:, j * 64:(j + 1) * 64], lhsT=SBF[:, h, si, :], rhs=QP[:, hl, j * 64:(j + 1) * 64],
                                   start=False, stop=True)
                return ins
            k.op(PE, fo, reads=[b_VI, b_scm[ci], b_SBF[h], b_QP], writes=[b_OT[oi]])
            OC, bOC = P1[hl], b_P1[hl]
            k.op(DVE, lambda e: e.tensor_copy(out=OC[:], in_=OTt[oi][:]), reads=[b_OT[oi]], writes=[bOC])
            SQ, bSQ = P2[hl], b_P2[hl]
            k.op(ACT, lambda e: e.activation(out=SQ[:], in_=OC[:], func=AF.Square), reads=[bOC], writes=[bSQ])
            if hl + 2 < 4:
                sc(hl + 2)
            yield
        for hl, h in enumerate(heads):
            OC, bOC = P1[hl], b_P1[hl]
            SQ, bSQ = P2[hl], b_P2[hl]
            bi = mm_bank()
            k.op(PE, lambda e: e.matmul(MMt[bi][:], lhsT=onesf[:], rhs=SQ[:], start=True, stop=True), reads=[b_onesf, bSQ], writes=[b_MM[bi]])
            k.op(ACT, lambda e: e.activation(out=SQ[:], in_=MMt[bi][:], func=AF.Ln, scale=1.0 / 128, bias=EPS), reads=[b_MM[bi]], writes=[bSQ])
            k.op(ACT, lambda e: e.activation(out=SQ[:], in_=SQ[:], func=AF.Exp, scale=-0.5), reads=[bSQ], writes=[bSQ])
            k.op(DVE, lambda e: e.scalar_tensor_tensor(out=SQ[:], in0=OC[:], scalar=cols[:, 24 + h:25 + h], in1=SQ[:],
                                                       op0=ALU.mult, op1=ALU.mult), reads=[bOC, b_cols, bSQ], writes=[bSQ])
            k.op(POOL, lambda e: e.tensor_tensor(out=R5[:, h, :], in0=SQ[:], in1=R5[:, h, :], op=ALU.mult), reads=[bSQ, b_R5], writes=[b_R5])
            if hl < 3:
                yield

    def interleave(gen, fillers):
        fillers = list(fillers)
        for _ in gen:
            if fillers:
                fillers.pop(0)()
        for f in fillers:
            f()

    def out_proj(loader, nk, lhs, lbufs, gain_off, final, t):
        nkh = nk // 8 if nk == 8 else 2
        kper = 8 if nk == 8 else 11
        pend = []
        ssq_ = [stat_cols(2) for _ in range(4)]
        ssq = [a for a, _ in ssq_]
        bssq = [b for _, b in ssq_]
        for nh in range(2):
            slots = [loader(kh, nh) for kh in range(nkh)]
            banks = [0, 1, 2, 3]
            for s in range(4):
                bi = banks[s]
                if bi == 3:
                    ptile, pbuf = STt.rearrange("p q v -> p (q v)"), b_ST
                else:
                    ptile, pbuf = MMt[bi], [b_MM[bi]]

                def f(e):
                    ins = None
                    n = 0
                    for kh in range(nkh):
                        v = slot_view(slots[kh], kper)
                        for kc in range(kper):
                            ins = e.matmul(ptile[:], lhsT=lhs[:, kh * kper + kc, s * 128:(s + 1) * 128], rhs=v[:, kc, :],
                                           start=(n == 0), stop=(n == nkh * kper - 1))
                            n += 1
                    return ins
                k.op(PE, f, reads=[b_ring[sl] for sl in slots] + lbufs, writes=pbuf)
                if pend:
                    norm_b(pend.pop(0))
                k.op(ACT, lambda e: e.activation(out=junk[:, 0:512], in_=ptile[:], func=AF.Square, accum_out=ssq[s][:, nh:nh + 1]),
                     reads=pbuf, writes=[b_junk, bssq[s]])
                park = BT[s // 2][:, (s % 2) * 512:(s % 2 + 1) * 512]
                if nh == 0:
                    k.op(ACT, lambda e: e.activation(out=park, in_=ptile[:], func=AF.Copy), reads=pbuf, writes=[b_BT[s // 2]])
                else:
                    tot, btot = stat_cols(1)
                    k.op(POOL, lambda e: e.tensor_tensor(out=tot, in0=ssq[s][:, 0:1], in1=ssq[s][:, 1:2], op=ALU.add),
                         reads=[bssq[s]], writes=[btot])
                    r, br = rstd_small(tot, btot, 1, 1.0 / D)
                    tA, btA = ta()
                    k.op(DVE, lambda e: e.scalar_tensor_tensor(out=tA[:], in0=ptile[:], scalar=r, in1=rows[:, gain_off + 512:gain_off + 1024],
                                                               op0=ALU.mult, op1=ALU.mult), reads=pbuf + [br, b_rows], writes=[btA])
                    k.op(POOL, lambda e: e.tensor_tensor(out=X[s][:, 512:1024], in0=X[s][:, 512:1024], in1=tA[:], op=ALU.add),
                         reads=[btA, b_X[s]], writes=[b_X[s]])
                    tB, btB = ta()
                    k.op(DVE, lambda e: e.scalar_tensor_tensor(out=tB[:], in0=park, scalar=r, in1=rows[:, gain_off:gain_off + 512],
                                                               op0=ALU.mult, op1=ALU.mult), reads=[b_BT[s // 2], br, b_rows], writes=[btB])
                    k.op(POOL, lambda e: e.tensor_tensor(out=X[s][:, 0:512], in0=X[s][:, 0:512], in1=tB[:], op=ALU.add),
                         reads=[btB, b_X[s]], writes=[b_X[s]])
                    if final:
                        k.dma(SP, d_xs[s], y_out[t * T + s * 128:t * T + (s + 1) * 128, :], X[s][:], reads=[b_X[s]])
                        if t + 1 < ntiles_main:
                            load_x(x_main, t + 1, s)
                            pend.append(norm_a(s))
                    else:
                        pend.append(norm_a(s))
        while pend:
            norm_b(pend.pop(0))

    gt = 0
    for t in range(ntiles_pre):
        for s in range(4):
            load_x(x_prev, t, s)
        for s in range(4):
            norm_to_hT(s)
        for gi in range(2):
            hgrn_group(gi, 0, False, last_pre=(t == ntiles_pre - 1))
        conv_run(n_phase0 + (t + 1) * (-(-(nconv - n_phase0) // max(ntiles_pre, 1))))
    conv_run(nconv, flush=True)

    for s in range(4):
        load_x(x_main, 0, s)
    for t in range(ntiles_main):
        if t == 0:
            for s in range(4):
                norm_to_hT(s)
        if stop == 2:
            return finish()
        sl_u = [load_k8(wb_in, "in", c0) for c0 in (0, 512)]
        sl_v = [load_k8(wb_in, "in", c0) for c0 in (1024, 1536)]
        for s in range(4):
            gi_ = nxt("bt", 2)
            gv, bgv = BT[gi_], b_BT[gi_]
            sm, bsm = stat_cols(2)
            for hf in range(2):
                bi = tm_group(sl_v[hf], HT, s, [b_HT])
                k.op(ACT, lambda e: e.activation(out=gv[:, hf * 512:(hf + 1) * 512], in_=MMt[bi][:], func=AF.Gelu, accum_out=sm[:, hf:hf + 1]),
                     reads=[b_MM[bi]], writes=[bgv, bsm])
            nm0, bnm0 = stat_cols(1)
            nm_, bnm = stat_cols(1)
            k.op(POOL, lambda e: e.tensor_tensor(out=nm0, in0=sm[:, 0:1], in1=sm[:, 1:2], op=ALU.add), reads=[bsm], writes=[bnm0])
            k.op(POOL, lambda e: e.tensor_scalar(out=nm_, in0=nm0, scalar1=-1.0 / D, scalar2=1.0, op0=ALU.mult, op1=ALU.mult),
                 reads=[bnm0], writes=[bnm])
            vs, bvs = stat_cols(1)
            k.op(ACT, lambda e: e.activation(out=junk[:], in_=gv[:], func=AF.Square, bias=nm_, accum_out=vs),
                 reads=[bgv, bnm], writes=[b_junk, bvs])
            r, br = rstd_small(vs, bvs, 1, 1.0 / D)
            k.op(DVE, lambda e: e.tensor_scalar(out=VLN[:, s, :], in0=gv[:], scalar1=nm_, scalar2=r, op0=ALU.add, op1=ALU.mult),
                 reads=[bgv, bnm, br], writes=[b_VLN])
        for c in range(8):
            bi = fm_group(sl_u[c // 4], c % 4, HT, [b_HT])
            k.op(ACT, lambda e: e.activation(out=U[:, c, :], in_=MMt[bi][:], func=AF.Gelu), reads=[b_MM[bi]], writes=[b_U])
        for g in range(8):
            bi = mm_bank()

            def fs(e):
                ins = None
                for n in range(4):
                    ins = e.matmul(MMt[bi][:, n * 128:(n + 1) * 128], lhsT=VLN[:, n, g * 128:(g + 1) * 128], rhs=wspb[:, g, :], start=True, stop=True)
                return ins
            k.op(PE, fs, reads=[b_VLN, b_wspb], writes=[b_MM[bi]])
            tA, btA = ta()
            k.op(DVE, lambda e: e.scalar_tensor_tensor(out=tA.rearrange("p (n t) -> p n t", t=128), in0=MMt[bi].rearrange("p (n t) -> p n t", t=128),
                                                       scalar=cols[:, 16 + g:17 + g], in1=cg[:, g:g + 1, :].to_broadcast([128, 4, 128]),
                                                       op0=ALU.mult, op1=ALU.add), reads=[b_MM[bi], b_cols, b_cg], writes=[btA])
            k.op(POOL, lambda e: e.tensor_tensor(out=U[:, g, :], in0=tA[:], in1=U[:, g, :], op=ALU.mult), reads=[btA, b_U], writes=[b_U])
        if stop == 3:
            return finish()
        hgrn_group(0, gt, True)
        sl_ga = [load_k8(wb_in, "in", c0) for c0 in (6144, 6656)]

        def ga_item(c):
            bi = fm_group(sl_ga[c // 4], c % 4, HT, [b_HT])
            k.op(ACT, lambda e: e.activation(out=R4[:, c, :], in_=MMt[bi][:], func=AF.Sigmoid), reads=[b_MM[bi]], writes=[b_R4])
        interleave(hgrn_pass1(0, gt), [(lambda c=c: ga_item(c)) for c in range(8)])
        sl_pa = [load_k8(wb_pa, "pa", c0) for c0 in (0, 512)]

        def pa_item(c):
            bi = fm_group(sl_pa[c // 4], c % 4, U, [b_U])
            k.op(DVE, lambda e: e.tensor_tensor(out=R4[:, c, :], in0=MMt[bi][:], in1=R4[:, c, :], op=ALU.mult), reads=[b_MM[bi], b_R4], writes=[b_R4])
        interleave(hgrn_pass2(0, gt), [(lambda c=c: pa_item(c)) for c in range(8)])
        hgrn_group(1, gt, True)
        sl_gb = [load_k8(wb_in, "in", c0) for c0 in (7168, 7680)]

        def gb_item(c):
            bi = fm_group(sl_gb[c // 4], c % 4, HT, [b_HT])
            k.op(ACT, lambda e: e.activation(out=GB[:, c, :], in_=MMt[bi][:], func=AF.Sigmoid), reads=[b_MM[bi]], writes=[b_U])
        interleave(hgrn_pass1(1, gt), [(lambda c=c: gb_item(c)) for c in range(8)])
        interleave(hgrn_pass2(1, gt), [])
        gt += 1
        sl_pb = [load_k8(wb_pb, "pb", c0) for c0 in (0, 512)]
        for c in range(8):
            bi = fm_group(sl_pb[c // 4], c % 4, R5, [b_R5])
            tA, btA = ta()
            k.op(DVE, lambda e: e.tensor_tensor(out=tA[:], in0=MMt[bi][:], in1=GB[:, c, :], op=ALU.mult), reads=[b_MM[bi], b_U], writes=[btA])
            k.op(POOL, lambda e: e.tensor_tensor(out=R4[:, c, :], in0=tA[:], in1=R4[:, c, :], op=ALU.add), reads=[btA, b_R4], writes=[b_R4])
        if stop == 6:
            return finish()
        out_proj(lambda kh, nh: load_k8(wb_out, "out", nh * 512), 8, R4, [b_R4], 0, False, t)
        if stop == 7:
            return finish()
        for blk in range(11):
            sl = load_up(blk)
            for jj in range(2):
                bg = fm_group(sl, jj, HT, [b_HT])
                bu = fm_group(sl, 2 + jj, HT, [b_HT])
                tA, btA = ta()
                k.op(ACT, lambda e: e.activation(out=tA[:], in_=MMt[bg][:], func=AF.Silu), reads=[b_MM[bg]], writes=[btA])
                k.op(DVE, lambda e: e.tensor_tensor(out=FACT[:, 2 * blk + jj, :], in0=MMt[bu][:], in1=tA[:], op=ALU.mult),
                     reads=[b_MM[bu], btA], writes=[b_FACT])
        out_proj(load_dn, 22, FACT, [b_FACT], D, True, t)

    return finish()


def _unused():
    for s in range(4):
        if d_xs[s].cnt:
            nc.sync.wait_ge(d_xs[s].sem, d_xs[s].cnt)
    nc.all_engine_barrier()
    for sm in k.sems:
        nc.gpsimd.sem_clear(sm)
    return nc


def _host_inputs(inp):
    f = lambda a: np.ascontiguousarray(np.asarray(a, dtype=np.float32))
    x = f(inp["x"])
    col = lambda v: f(v).reshape(8, 128).T
    cols = np.concatenate([col(inp["pre_mix_gain"][0]), col(inp["pre_ffn_gain"][0]), col(inp["sgu_norm_gain"][0]),
                           col(inp["hgrn_norm_gain"][0]), col(inp["lb_logits"][0]), col(inp["lb_logits"][1])], axis=1)
    rows = np.concatenate([np.broadcast_to(f(inp["post_mix_gain"][0])[None, :], (128, D)),
                           np.broadcast_to(f(inp["post_ffn_gain"][0])[None, :], (128, D))], axis=1)
    wspT = f(inp["w_spatial"][0]).transpose(2, 0, 1).reshape(128, 1024)
    ident = np.eye(128, dtype=np.float32)
    p = np.arange(128)
    tt = np.arange(64)
    msc = (p[:, None] % 64 <= tt[None, :]).astype(np.float32)
    mask_sc = np.tile(msc, (1, 4))
    mask_scan = np.ones((128, 512), np.float32)
    mask_scan[:, 0::64] = 0.0
    mask_sguT = (p[:, None] // 64 <= p[None, :] // 64).astype(np.float32)
    consts = np.concatenate([ident, mask_sc, mask_scan, mask_sguT], axis=1)
    sguA = np.stack([f(inp["sgu_norm_bias"][0]), np.ones(D, np.float32)], axis=0)
    bsp = f(inp["b_spatial"][0]).reshape(1, D)
    shared = {
        "w_in": f(inp["w_in"][0]), "w_pa": f(inp["w_proj_sgu"][0]), "w_pb": f(inp["w_proj_hgrn"][0]),
        "w_out": f(inp["w_out"][0]), "w_up": f(inp["w_ffn_up"][0]), "w_dn": f(inp["w_ffn_down"][0]),
        "cols": f(cols), "rows_bc": f(rows), "wspT": f(wspT), "consts": f(consts), "sguA": f(sguA), "bsp": bsp,
    }
    maps = []
    zeros = np.zeros((NTOK, D), np.float32)
    for c in range(8):
        b, half = c // 2, c % 2
        m = dict(shared)
        m["x_main"] = np.ascontiguousarray(x[b, half * NTOK:(half + 1) * NTOK])
        m["x_prev"] = np.ascontiguousarray(x[b, 0:NTOK]) if half == 1 else zeros
        maps.append(m)
    return maps


def kernel(**inputs):
    maps = _host_inputs(inputs)
    nc = build_nc()
    res = run_bass_kernel_spmd(nc, maps, core_ids=list(range(8)))
    out = np.empty((4, 8192, D), np.float32)
    for c in range(8):
        b, half = c // 2, c % 2
        out[b, half * NTOK:(half + 1) * NTOK] = np.asarray(res.results[c]["y_out"], dtype=np.float32)
    return out
```

```python
import numpy as np
import concourse.bass as bass
import concourse.mybir as mybir
from concourse.bass_utils import run_bass_kernel_spmd

F32 = mybir.dt.float32
BF16 = mybir.dt.bfloat16
AF = mybir.ActivationFunctionType
ALU = mybir.AluOpType

D = 1024
NTOK = 4096
T = 512
NT = NTOK // T
FH = 2816
EPS = 1e-6
NCOL = 48


class Eng:
    def __init__(self, nc, name, h, kind):
        self.h = h
        self.kind = kind
        self.sem = nc.alloc_semaphore("se_" + name)
        self.cnt = 0
        self.waited = {}


class DSem:
    def __init__(self, nc, name):
        self.sem = nc.alloc_semaphore("sd_" + name)
        self.cnt = 0


class Buf:
    def __init__(self, name):
        self.name = name
        self.wr = {}
        self.rd = {}
        self.ov = []


def overlap(*bufs):
    for a in bufs:
        for b in bufs:
            if a is not b and b not in a.ov:
                a.ov.append(b)


class K:
    def __init__(self, nc):
        self.nc = nc
        self.pe = Eng(nc, "pe", nc.tensor, "pe")
        self.act = Eng(nc, "act", nc.scalar, "c")
        self.dve = Eng(nc, "dve", nc.vector, "c")
        self.pool = Eng(nc, "pool", nc.gpsimd, "c")
        self.sp = Eng(nc, "sp", nc.sync, "q")
        self.sems = [e.sem for e in (self.pe, self.act, self.dve, self.pool, self.sp)]
        self.semobj = {}

    def dsem(self, name):
        d = DSem(self.nc, name)
        self.sems.append(d.sem)
        return d

    def _deps(self, eng, reads, writes):
        need = {}

        def add(dct, same_ok):
            for key, (sem, val, src) in dct.items():
                if src is eng:
                    if eng.kind == "pe":
                        continue
                if need.get(key, (None, 0))[1] < val:
                    need[key] = (sem, val)

        for b in reads:
            for o in [b] + b.ov:
                add(o.wr, True)
        for b in writes:
            for o in [b] + b.ov:
                add(o.wr, False)
                add(o.rd, False)
        for key, (sem, val) in need.items():
            if eng.waited.get(key, 0) < val:
                eng.h.wait_ge(sem, val)
                eng.waited[key] = val

    def _mark(self, tok, reads, writes):
        key = id(tok[0])
        for b in writes:
            b.wr[key] = tok
        for b in reads:
            b.rd[key] = tok

    def op(self, eng, fn, reads=(), writes=()):
        self._deps(eng, reads, writes)
        ins = fn(eng.h)
        eng.cnt += 1
        ins.then_inc(eng.sem, 1)
        self._mark((eng.sem, eng.cnt, eng), reads, writes)

    def dma(self, q, ds, out, in_, reads=(), writes=()):
        self._deps(q, reads, writes)
        ins = q.h.dma_start(out=out, in_=in_)
        ds.cnt += 16
        ins.then_inc(ds.sem, 16)
        self._mark((ds.sem, ds.cnt, None), reads, writes)


def build_nc(ntiles_main=NT, ntiles_pre=NT, stop=99):
    nc = bass.Bass("TRN2", target_bir_lowering=False)

    def din(name, shape, dt=F32):
        return nc.dram_tensor(name, shape, dt, kind="ExternalInput").ap()

    x_main = din("x_main", [NTOK, D])
    x_prev = din("x_prev", [NTOK, D])
    w_in = din("w_in", [D, 8192])
    w_pa = din("w_pa", [D, D])
    w_pb = din("w_pb", [D, D])
    w_out = din("w_out", [D, D])
    w_up = din("w_up", [D, 2 * FH])
    w_dn = din("w_dn", [FH, D])
    cols_d = din("cols", [128, NCOL])
    rows_d = din("rows_bc", [128, 2 * D])
    wspT_d = din("wspT", [128, 8 * 128])
    cst_d = din("consts", [128, 128 + 256 + 512 + 128])
    sguA_d = din("sguA", [2, D])
    bsp_d = din("bsp", [1, D])
    y_out = nc.dram_tensor("y_out", [NTOK, D], F32, kind="ExternalOutput").ap()

    def dscr(name, shape):
        return nc.dram_tensor(name, shape, BF16, kind="Internal").ap()

    wb_in = dscr("wb_in", [D, 8192])
    wb_pa = dscr("wb_pa", [D, D])
    wb_pb = dscr("wb_pb", [D, D])
    wb_out = dscr("wb_out", [D, D])
    wb_up = dscr("wb_up", [D, 2 * FH])
    wb_dn = dscr("wb_dn", [FH, D])

    k = K(nc)
    PE, ACT, DVE, POOL, SP = k.pe, k.act, k.dve, k.pool, k.sp

    def sb(name, shape, dt):
        return nc.alloc_sbuf_tensor("s_" + name, shape, dt)

    cols = sb("cols", [128, NCOL], F32)
    der = sb("der", [128, 64], F32)
    rows = sb("rows", [128, 2 * D], F32)
    cst = sb("cst", [128, 1024], F32)
    identb = sb("identb", [128, 128], BF16)
    onesf = sb("onesf", [128, 128], F32)
    wspb = sb("wspb", [128, 8, 128], BF16)
    cg = sb("cg", [128, 8, 128], F32)
    X = [sb(f"x{s}", [128, D], F32) for s in range(4)]
    wspf = X[1].rearrange("p (g t) -> p g t", t=128)
    sguA = X[2][0:2, :]
    sguB = X[3][0:2, :]
    HT = sb("hT", [128, 8, T], BF16)
    RR = sb("RR", [128, 12288], BF16)
    R4 = sb("R4", [128, 8, T], BF16)
    R5 = sb("R5", [128, 8, T], BF16)
    S32 = sb("S32", [128, 8, 128], F32)
    SBF = sb("SBF", [128, 8, 9, 128], BF16)
    decs = sb("decs", [128, 8, 8], F32)
    scm = [sb(f"scm{i}", [128, 256], BF16) for i in range(2)]
    Bt = [sb(f"Bt{i}", [128, T], F32) for i in range(4)]
    TA = [sb(f"ta{i}", [128, T], F32) for i in range(3)]
    P1 = [sb(f"P1_{i}", [128, T], F32) for i in range(4)]
    P2 = [sb(f"P2_{i}", [128, T], F32) for i in range(4)]
    P3 = [sb(f"P3_{i}", [128, T], F32) for i in range(4)]
    bia = sb("bia", [128, 4], F32)
    kppT = [sb(f"kppT{i}", [128, T], BF16) for i in range(4)]
    junk = sb("junk", [128, D], BF16)
    hb = [sb(f"hb{i}", [128, D], BF16) for i in range(2)]
    BT = [sb(f"BT{i}", [128, D], F32) for i in range(2)]
    stat = sb("stat", [128, 64], F32)
    RING = sb("RING", [128, 4 * 5632], BF16)

    U = RR[:, 0:4096].rearrange("p (k n) -> p k n", n=T)
    VLN = RR[:, 4096:8192].rearrange("p (s n) -> p s n", n=D)
    QP = RR[:, 4096:6144].rearrange("p (k n) -> p k n", n=T)
    KP = RR[:, 6144:8192].rearrange("p (k n) -> p k n", n=T)
    KPP = RR[:, 8192:10240].rearrange("p (s n) -> p s n", n=T)
    VI = RR[:, 10240:12288].rearrange("p (s n) -> p s n", n=T)
    GB = U
    FACT = RR[:, 0:11264].rearrange("p (k n) -> p k n", n=T)

    MMt = [nc.alloc_psum_tensor(f"mm{i}", [128, T], F32) for i in range(3)]
    STt2 = [nc.alloc_psum_tensor(f"st{i}", [128, 4, 128], F32) for i in range(2)]
    STt = STt2[0]
    TPt = nc.alloc_psum_tensor("tp", [128, 8, 128], BF16)
    OTt = [nc.alloc_psum_tensor(f"ot{i}", [128, T], F32) for i in range(2)]

    b_cols, b_der, b_rows, b_cst, b_identb, b_onesf = (Buf(n) for n in ("cols", "der", "rows", "cst", "identb", "onesf"))
    b_wspf, b_wspb, b_cg, b_sguA, b_sguB = (Buf(n) for n in ("wspf", "wspb", "cg", "sguA", "sguB"))
    b_X = [Buf(f"x{s}") for s in range(4)]
    overlap(b_X[1], b_wspf)
    overlap(b_X[2], b_sguA)
    overlap(b_X[3], b_sguB)
    b_HT = Buf("hT")
    b_U, b_VLN, b_QP, b_KP, b_KPP, b_VI, b_FACT = (Buf(n) for n in ("U", "VLN", "QP", "KP", "KPP", "VI", "FACT"))
    overlap(b_U, b_FACT)
    overlap(b_VLN, b_QP, b_FACT)
    overlap(b_VLN, b_KP, b_FACT)
    overlap(b_KPP, b_FACT)
    overlap(b_VI, b_FACT)
    b_R4, b_R5 = Buf("R4"), Buf("R5")
    b_S32 = [Buf(f"S32_{h}") for h in range(8)]
    b_SBF = [Buf(f"SBF_{h}") for h in range(8)]
    b_decs = [Buf(f"decs{h}") for h in range(8)]
    b_scm = [Buf("scm0"), Buf("scm1")]
    b_Bt = [Buf(f"Bt{i}") for i in range(4)]
    b_TA = [Buf(f"ta{i}") for i in range(3)]
    b_P1 = [Buf(f"P1_{i}") for i in range(4)]
    b_P2 = [Buf(f"P2_{i}") for i in range(4)]
    b_P3 = [Buf(f"P3_{i}") for i in range(4)]
    b_bia = [Buf(f"bia{i}") for i in range(4)]
    b_kppT = [Buf(f"kppT{i}") for i in range(4)]
    b_junk = Buf("junk")
    b_hb = [Buf("hb0"), Buf("hb1")]
    b_BT = [Buf("BT0"), Buf("BT1")]
    b_stat = Buf("stat")
    b_ring = [Buf(f"ring{i}") for i in range(4)]
    b_stb = [Buf("stb0"), Buf("stb1")]
    b_MM = [Buf(f"mm{i}") for i in range(3)]
    b_ST2 = [Buf(f"st{i}") for i in range(2)]
    b_ST = [b_ST2[0]]
    b_TP = Buf("tp")
    b_OT = [Buf("ot0"), Buf("ot1")]
    b_wb = {n: Buf("wb_" + n) for n in ("in", "pa", "pb", "out", "up", "dn")}

    d_const = k.dsem("const")
    d_ring = [k.dsem(f"ring{i}") for i in range(4)]
    d_stl = [k.dsem(f"stl{i}") for i in range(3)]
    d_conv = [k.dsem("conv0"), k.dsem("conv1")]
    d_xl = [k.dsem(f"xl{s}") for s in range(4)]
    d_xs = [k.dsem(f"xs{s}") for s in range(4)]

    def finish():
        for s in range(4):
            if d_xs[s].cnt:
                nc.sync.wait_ge(d_xs[s].sem, d_xs[s].cnt)
        nc.all_engine_barrier()
        for sm in k.sems:
            nc.gpsimd.sem_clear(sm)
        return nc

    for sm in k.sems:
        nc.gpsimd.sem_clear(sm)
    nc.all_engine_barrier()

    k.dma(SP, d_const, cols[:], cols_d[:, :], writes=[b_cols])
    k.dma(SP, d_const, rows[:], rows_d[:, :], writes=[b_rows])
    k.dma(SP, d_const, cst[:], cst_d[:, :], writes=[b_cst])
    k.dma(SP, d_const, X[1][:], wspT_d[:, :], writes=[b_wspf])
    k.dma(SP, d_const, sguA, sguA_d[:, :], writes=[b_sguA])
    k.dma(SP, d_const, sguB[1:2, :], bsp_d[:, :], writes=[b_sguB])
    for b_ in (b_cols, b_rows, b_cst, b_wspf, b_sguA, b_sguB):
        b_.wr[id(d_const.sem)] = (d_const.sem, d_const.cnt, None)
    ident_f = cst[:, 0:128]
    mask_sc = cst[:, 128:384]
    mask_scan = cst[:, 384:896]
    mask_sgu = cst[:, 896:1024]

    k.op(DVE, lambda e: e.tensor_copy(out=identb[:], in_=ident_f), reads=[b_cst], writes=[b_identb])
    k.op(POOL, lambda e: e.memset(onesf[:], 1.0), writes=[b_onesf])
    k.op(POOL, lambda e: e.memset(der[:, 40:48], -0.5), writes=[b_der])
    k.op(POOL, lambda e: e.memset(S32.rearrange("p h v -> p (h v)"), 0.0), writes=b_S32)
    k.op(POOL, lambda e: e.memset(SBF.rearrange("p h j v -> p (h j v)"), 0.0), writes=b_SBF)
    k.op(DVE, lambda e: e.tensor_tensor(out=der[:, 24:32], in0=cols[:, 40:48], in1=cols[:, 32:40], op=ALU.subtract),
         reads=[b_cols], writes=[b_der])
    k.op(ACT, lambda e: e.activation(out=der[:, 24:32], in_=der[:, 24:32], func=AF.Exp), reads=[b_der], writes=[b_der])
    k.op(DVE, lambda e: e.tensor_scalar(out=der[:, 32:40], in0=der[:, 24:32], scalar1=1.0, scalar2=None, op0=ALU.add),
         reads=[b_der], writes=[b_der])
    k.op(DVE, lambda e: e.reciprocal(out=der[:, 0:8], in_=der[:, 32:40]), reads=[b_der], writes=[b_der])
    k.op(DVE, lambda e: e.tensor_tensor(out=der[:, 8:16], in0=der[:, 24:32], in1=der[:, 0:8], op=ALU.mult),
         reads=[b_der], writes=[b_der])
    k.op(ACT, lambda e: e.activation(out=der[:, 16:24], in_=der[:, 8:16], func=AF.Ln), reads=[b_der], writes=[b_der])
    LB = lambda h: der[:, h:h + 1]
    LNOML = lambda h: der[:, 16 + h:17 + h]
    NEGH = der[:, 40:41]

    w3 = wspf
    ones512 = onesf[:, 0:1].to_broadcast([128, T])
    k.op(DVE, lambda e: e.tensor_tensor(out=w3, in0=w3, in1=mask_sgu.unsqueeze(1).to_broadcast([128, 8, 128]), op=ALU.mult),
         reads=[b_wspf, b_cst], writes=[b_wspf])
    k.op(POOL, lambda e: e.tensor_copy(out=wspb[:], in_=wspf), reads=[b_wspf], writes=[b_wspb])

    def f_wsum2(e):
        ins = None
        for g in range(8):
            dst = MMt[g // 4][0:1, (g % 4) * 128:(g % 4 + 1) * 128]
            ins = e.matmul(dst, lhsT=onesf[:, 0:1], rhs=wspf[:, g, :], start=True, stop=True)
        return ins
    k.op(PE, f_wsum2, reads=[b_onesf, b_wspf], writes=[b_MM[0], b_MM[1]])
    k.op(DVE, lambda e: e.tensor_copy(out=sguB[0:1, 0:512], in_=MMt[0][0:1, :]), reads=[b_MM[0]], writes=[b_sguB])
    k.op(DVE, lambda e: e.tensor_copy(out=sguB[0:1, 512:1024], in_=MMt[1][0:1, :]), reads=[b_MM[1]], writes=[b_sguB])

    def f_cg(e):
        ins = None
        for g in range(8):
            dst = MMt[g // 4][:, (g % 4) * 128:(g % 4 + 1) * 128]
            ins = e.matmul(dst, lhsT=sguA[:, g * 128:(g + 1) * 128], rhs=sguB[:, g * 128:(g + 1) * 128], start=True, stop=True)
        return ins
    k.op(PE, f_cg, reads=[b_sguA, b_sguB], writes=[b_MM[0], b_MM[1]])
    k.op(DVE, lambda e: e.tensor_copy(out=cg[:, 0:4, :].rearrange("p g t -> p (g t)"), in_=MMt[0][:]), reads=[b_MM[0]], writes=[b_cg])
    k.op(DVE, lambda e: e.tensor_copy(out=cg[:, 4:8, :].rearrange("p g t -> p (g t)"), in_=MMt[1][:]), reads=[b_MM[1]], writes=[b_cg])

    if stop == 0:
        return finish()
    stf = [R4.rearrange("p k n -> p (k n)").bitcast(F32), R5.rearrange("p k n -> p (k n)").bitcast(F32), RR[:, 0:4096].bitcast(F32)]
    stb = [RR[:, 4096:6144], RR[:, 6144:8192]]
    b_stf = [Buf(f"stf{i}") for i in range(3)]
    overlap(b_stf[0], b_R4)
    overlap(b_stf[1], b_R5)
    overlap(b_stf[2], b_U, b_FACT)
    overlap(b_stb[0], b_VLN, b_QP, b_FACT)
    overlap(b_stb[1], b_VLN, b_KP, b_FACT)
    conv = []

    def conv_win(c0):
        for kc in range(8):
            conv.append((w_in[kc * 128:(kc + 1) * 128, c0:c0 + 2048], wb_in[kc * 128:(kc + 1) * 128, c0:c0 + 2048], cols[:, kc:kc + 1], "in"))
    conv_win(2048)
    conv_win(4096)
    n_phase0 = len(conv)
    conv_win(0)
    conv_win(6144)
    for nm, src, dst in (("pa", w_pa, wb_pa), ("pb", w_pb, wb_pb), ("out", w_out, wb_out)):
        for kc in range(8):
            conv.append((src[kc * 128:(kc + 1) * 128, :], dst[kc * 128:(kc + 1) * 128, :], None, nm))
    for kc in range(8):
        for c0, cw in ((0, 2048), (2048, 2048), (4096, 1536)):
            conv.append((w_up[kc * 128:(kc + 1) * 128, c0:c0 + cw], wb_up[kc * 128:(kc + 1) * 128, c0:c0 + cw], cols[:, 8 + kc:9 + kc], "up"))
    for kc in range(22):
        conv.append((w_dn[kc * 128:(kc + 1) * 128, :], wb_dn[kc * 128:(kc + 1) * 128, :], None, "dn"))
    nconv = len(conv)
    cv = {"next": 0, "loaded": 0, "stored": 0}

    def conv_load(i):
        src = conv[i][0]
        w = src.shape[1]
        k.dma(SP, d_stl[i % 3], stf[i % 3][:, 0:w], src, writes=[b_stf[i % 3]])

    def conv_cast(i):
        src, dst, gc, nm = conv[i]
        w = src.shape[1]
        j = i % 2
        if i % 2 == 0:
            if gc is None:
                k.op(DVE, lambda e: e.tensor_copy(out=stb[j][:, 0:w], in_=stf[i % 3][:, 0:w]), reads=[b_stf[i % 3]], writes=[b_stb[j]])
            else:
                k.op(DVE, lambda e: e.tensor_scalar(out=stb[j][:, 0:w], in0=stf[i % 3][:, 0:w], scalar1=gc, scalar2=None, op0=ALU.mult),
                     reads=[b_stf[i % 3], b_cols], writes=[b_stb[j]])
        else:
            if gc is None:
                k.op(ACT, lambda e: e.activation(out=stb[j][:, 0:w], in_=stf[i % 3][:, 0:w], func=AF.Copy), reads=[b_stf[i % 3]], writes=[b_stb[j]])
            else:
                k.op(ACT, lambda e: e.activation(out=stb[j][:, 0:w], in_=stf[i % 3][:, 0:w], func=AF.Copy, scale=gc),
                     reads=[b_stf[i % 3], b_cols], writes=[b_stb[j]])

    def conv_store(i):
        src, dst, gc, nm = conv[i]
        w = src.shape[1]
        j = i % 2
        k.dma(ACT, d_conv[j], dst, stb[j][:, 0:w], reads=[b_stb[j]], writes=[b_wb[nm]])

    def conv_run(upto, flush=False):
        upto = min(upto, nconv)
        while cv["next"] < upto:
            i = cv["next"]
            while cv["loaded"] < min(i + 3, nconv):
                conv_load(cv["loaded"])
                cv["loaded"] += 1
            conv_cast(i)
            if i - 1 >= cv["stored"]:
                conv_store(cv["stored"])
                cv["stored"] += 1
            cv["next"] += 1
        if flush:
            while cv["stored"] < cv["next"]:
                conv_store(cv["stored"])
                cv["stored"] += 1

    conv_run(n_phase0, flush=True)

    if stop == 1:
        return finish()
    ring_ctr = [0]

    def ring_slot():
        i = ring_ctr[0] % 4
        ring_ctr[0] += 1
        return i

    def slot_view(i, kk):
        return RING[:, i * 5632:i * 5632 + kk * 512].rearrange("p (k n) -> p k n", n=512)

    def load_k8(wb, nm, c0):
        i = ring_slot()
        k.dma(SP, d_ring[i], slot_view(i, 8), wb[:, c0:c0 + 512].rearrange("(k p) n -> p k n", p=128),
              reads=[b_wb[nm]], writes=[b_ring[i]])
        return i

    def load_up(blk):
        i = ring_slot()
        v = slot_view(i, 8)
        j = 2 * blk
        k.dma(SP, d_ring[i], v[:, :, 0:256], wb_up[:, j * 128:j * 128 + 256].rearrange("(k p) n -> p k n", p=128),
              reads=[b_wb["up"]], writes=[b_ring[i]])
        k.dma(SP, d_ring[i], v[:, :, 256:512], wb_up[:, FH + j * 128:FH + j * 128 + 256].rearrange("(k p) n -> p k n", p=128),
              reads=[b_wb["up"]], writes=[b_ring[i]])
        return i

    def load_dn(kh, nh):
        i = ring_slot()
        k.dma(SP, d_ring[i], slot_view(i, 11), wb_dn[kh * 1408:(kh + 1) * 1408, nh * 512:(nh + 1) * 512].rearrange("(k p) n -> p k n", p=128),
              reads=[b_wb["dn"]], writes=[b_ring[i]])
        return i

    mm_ctr = [0]

    def mm_bank():
        i = mm_ctr[0] % 3
        mm_ctr[0] += 1
        return i

    ta_ctr = [0]

    def ta():
        i = ta_ctr[0] % 3
        ta_ctr[0] += 1
        return TA[i], b_TA[i]

    st_ctr = [0]
    misc = {"stat": 0, "hb": 0, "bt": 0, "kppT": 0, "sc": 0, "ot": 0}

    def nxt(name, n):
        v = misc[name] % n
        misc[name] += 1
        return v

    b_stats = [Buf(f"stat{i}") for i in range(32)]

    def stat_cols(n):
        i = misc["stat"] % 32
        misc["stat"] += 1
        return stat[:, 2 * i:2 * i + n], b_stats[i]

    def fm_group(slot, chunk, rhs, rbufs):
        bi = mm_bank()
        v = slot_view(slot, 8)

        def f(e):
            ins = None
            for kc in range(8):
                ins = e.matmul(MMt[bi][:], lhsT=v[:, kc, chunk * 128:(chunk + 1) * 128], rhs=rhs[:, kc, :], start=(kc == 0), stop=(kc == 7))
            return ins
        k.op(PE, f, reads=[b_ring[slot]] + rbufs, writes=[b_MM[bi]])
        return bi

    def tm_group(slot, lhs, s, rbufs):
        bi = mm_bank()
        v = slot_view(slot, 8)

        def f(e):
            ins = None
            for kc in range(8):
                ins = e.matmul(MMt[bi][:], lhsT=lhs[:, kc, s * 128:(s + 1) * 128], rhs=v[:, kc, :], start=(kc == 0), stop=(kc == 7))
            return ins
        k.op(PE, f, reads=[b_ring[slot]] + rbufs, writes=[b_MM[bi]])
        return bi

    def rstd_small(ss_ap, ss_buf, n, scale):
        m, bm = stat_cols(n)
        r, br = stat_cols(n)
        k.op(POOL, lambda e: e.tensor_scalar(out=m, in0=ss_ap, scalar1=scale, scalar2=EPS, op0=ALU.mult, op1=ALU.add),
             reads=[ss_buf], writes=[bm])
        k.op(POOL, lambda e: e.tensor_tensor(out=r, in0=m, in1=NEGH, op=ALU.pow),
             reads=[bm, b_der], writes=[br])
        return r, br

    def norm_to_hT(s):
        norm_b(norm_a(s))

    def norm_a(s):
        ss, bss = stat_cols(1)
        k.op(ACT, lambda e: e.activation(out=junk[:], in_=X[s][:], func=AF.Square, accum_out=ss),
             reads=[b_X[s]], writes=[b_junk, bss])
        r, br = rstd_small(ss, bss, 1, 1.0 / D)
        hi = nxt("hb", 2)
        k.op(DVE, lambda e: e.tensor_scalar(out=hb[hi][:], in0=X[s][:], scalar1=r, scalar2=None, op0=ALU.mult),
             reads=[b_X[s], br], writes=[b_hb[hi]])
        return (s, hi)

    def norm_b(sh):
        s, hi = sh

        def f(e):
            ins = None
            for kc in range(8):
                ins = e.transpose(TPt[:, kc, :], hb[hi][:, kc * 128:(kc + 1) * 128], identb[:])
            return ins
        k.op(PE, f, reads=[b_hb[hi], b_identb], writes=[b_TP])
        k.op(DVE, lambda e: e.tensor_copy(out=HT[:, :, s * 128:(s + 1) * 128], in_=TPt[:]), reads=[b_TP], writes=[b_HT])

    def load_x(src, t, s):
        k.dma(SP, d_xl[s], X[s][:], src[t * T + s * 128:t * T + (s + 1) * 128, :], writes=[b_X[s]])

    def sbf_slot(gt, j):
        return (8 * gt + j) % 9

    def hgrn_group(gi, gt, main, last_pre=False):
        heads = [4 * gi + i for i in range(4)]
        sl_f = load_k8(wb_in, "in", 3072 + gi * 512)
        sl_i = load_k8(wb_in, "in", 4096 + gi * 512)
        if main:
            sl_g = load_k8(wb_in, "in", 5120 + gi * 512)
            sl_q = load_k8(wb_in, "in", 2048 + gi * 512)
        for hl, h in enumerate(heads):
            bi = fm_group(sl_f, hl, HT, [b_HT])
            z = MMt[bi]
            E, bE = ta()
            k.op(ACT, lambda e: e.activation(out=E[:], in_=z[:], func=AF.Exp, scale=-1.0), reads=[b_MM[bi]], writes=[bE])
            k.op(ACT, lambda e: e.activation(out=P2[hl][:], in_=E[:], func=AF.Ln, scale=LB(h), bias=1.0), reads=[bE, b_der], writes=[b_P2[hl]])
            k.op(ACT, lambda e: e.activation(out=P3[hl][:], in_=E[:], func=AF.Ln, scale=1.0, bias=1.0), reads=[bE], writes=[b_P3[hl]])
            k.op(DVE, lambda e: e.scalar_tensor_tensor(out=P1[hl][:], in0=z[:], scalar=-1.0, in1=P3[hl][:], op0=ALU.mult, op1=ALU.subtract),
                 reads=[b_MM[bi], b_P3[hl]], writes=[b_P1[hl]])
        for s in range(4):
            bi = tm_group(sl_i, HT, s, [b_HT])
            k.op(DVE, lambda e: e.tensor_copy(out=VI[:, s, :], in_=MMt[bi][:]), reads=[b_MM[bi]], writes=[b_VI])
        if main:
            for hp in range(0, 4, 2):
                st = []
                for hl in (hp, hp + 1):
                    bi = fm_group(sl_g, hl, HT, [b_HT])
                    E, bE = ta()
                    st.append((hl, heads[hl], bi, E, bE))
                    k.op(ACT, lambda e: e.activation(out=E[:], in_=MMt[bi][:], func=AF.Exp, scale=-1.0), reads=[b_MM[bi]], writes=[bE])
                for hl, h, bi, E, bE in st:
                    k.op(ACT, lambda e: e.activation(out=E[:], in_=E[:], func=AF.Ln, bias=1.0), reads=[bE], writes=[bE])
                for hl, h, bi, E, bE in st:
                    k.op(ACT, lambda e: e.activation(out=E[:], in_=E[:], func=AF.Exp, scale=-1.0), reads=[bE], writes=[bE])
                for hl, h, bi, E, bE in st:
                    k.op(DVE, lambda e: e.tensor_tensor(out=R5[:, h, :], in0=MMt[bi][:], in1=E[:], op=ALU.mult), reads=[b_MM[bi], bE], writes=[b_R5])
        for hl, h in enumerate(heads):
            k.op(POOL, lambda e: e.tensor_tensor(out=P2[hl][:], in0=P2[hl][:], in1=P3[hl][:], op=ALU.subtract),
                 reads=[b_P2[hl], b_P3[hl]], writes=[b_P2[hl]])
            B, bB = Bt[hl], b_Bt[hl]
            if main:
                k.op(DVE, lambda e: e.tensor_tensor_scan(out=B[:], data0=mask_scan, data1=P2[hl][:], initial=0.0, op0=ALU.mult, op1=ALU.add),
                     reads=[b_cst, b_P2[hl]], writes=[bB])
            else:
                k.op(DVE, lambda e: e.tensor_tensor_scan(out=B[:], data0=ones512, data1=P2[hl][:], initial=0.0, op0=ALU.mult, op1=ALU.add),
                     reads=[b_onesf, b_P2[hl]], writes=[bB])
            k.op(POOL, lambda e: e.tensor_tensor(out=P1[hl][:], in0=P1[hl][:], in1=B[:], op=ALU.subtract),
                 reads=[b_P1[hl], bB], writes=[b_P1[hl]])
            if main:
                B3 = B.rearrange("p (c l) -> p c l", l=64)
                k.op(POOL, lambda e: e.tensor_tensor(out=P3[hl].rearrange("p (c l) -> p c l", l=64), in0=P1[hl].rearrange("p (c l) -> p c l", l=64),
                                                     in1=B3[:, :, 63:64].to_broadcast([128, 8, 64]), op=ALU.add),
                     reads=[b_P1[hl], bB], writes=[b_P3[hl]])
            else:
                k.op(POOL, lambda e: e.tensor_tensor(out=bia[:, hl:hl + 1], in0=B[:, 511:512], in1=LNOML(h), op=ALU.add),
                     reads=[bB, b_der], writes=[b_bia[hl]])
        for hl, h in enumerate(heads):
            B = Bt[hl]
            if main:
                B3 = B.rearrange("p (c l) -> p c l", l=64)
                k.op(ACT, lambda e: e.activation(out=KP[:, hl, :], in_=P1[hl][:], func=AF.Exp, bias=LNOML(h)), reads=[b_P1[hl], b_der], writes=[b_KP])
                k.op(ACT, lambda e: e.activation(out=kppT[hl][:], in_=P3[hl][:], func=AF.Exp, bias=LNOML(h)), reads=[b_P3[hl], b_der], writes=[b_kppT[hl]])
                k.op(ACT, lambda e: e.activation(out=decs[:, h, :], in_=B3[:, :, 63], func=AF.Exp), reads=[b_Bt[hl]], writes=[b_decs[h]])
            else:
                k.op(ACT, lambda e: e.activation(out=kppT[hl][:], in_=P1[hl][:], func=AF.Exp, bias=bia[:, hl:hl + 1]), reads=[b_P1[hl], b_bia[hl]], writes=[b_kppT[hl]])
                k.op(ACT, lambda e: e.activation(out=decs[:, h, 0:1], in_=B[:, 511:512], func=AF.Exp), reads=[b_Bt[hl]], writes=[b_decs[h]])

        if main:
            for hp in range(0, 4, 2):
                st = []
                for hl in (hp, hp + 1):
                    bi = fm_group(sl_q, hl, HT, [b_HT])
                    E, bE = ta()
                    st.append((hl, heads[hl], bi, E, bE))
                    k.op(ACT, lambda e: e.activation(out=E[:], in_=MMt[bi][:], func=AF.Exp, scale=-1.0), reads=[b_MM[bi]], writes=[bE])
                for hl, h, bi, E, bE in st:
                    k.op(ACT, lambda e: e.activation(out=E[:], in_=E[:], func=AF.Ln, bias=1.0), reads=[bE], writes=[bE])
                    k.op(POOL, lambda e: e.tensor_tensor(out=E[:], in0=Bt[hl][:], in1=E[:], op=ALU.subtract), reads=[bE, b_Bt[hl]], writes=[bE])
                for hl, h, bi, E, bE in st:
                    k.op(ACT, lambda e: e.activation(out=E[:], in_=E[:], func=AF.Exp), reads=[bE], writes=[bE])
                for hl, h, bi, E, bE in st:
                    k.op(DVE, lambda e: e.tensor_tensor(out=QP[:, hl, :], in0=MMt[bi][:], in1=E[:], op=ALU.mult), reads=[b_MM[bi], bE], writes=[b_QP])
        for hl, h in enumerate(heads):
            def ftp(e):
                ins = None
                for s in range(4):
                    ins = e.transpose(TPt[:, s, :], kppT[hl][:, s * 128:(s + 1) * 128], identb[:])
                return ins
            k.op(PE, ftp, reads=[b_kppT[hl], b_identb], writes=[b_TP])
            k.op(DVE, lambda e: e.tensor_copy(out=KPP[:, :, hl * 128:(hl + 1) * 128], in_=TPt[:, 0:4, :]), reads=[b_TP], writes=[b_KPP])
        if not main:
            def fst(e):
                ins = None
                for hl in range(4):
                    for s in range(4):
                        ins = e.matmul(STt2[0][:, hl, :], lhsT=KPP[:, s, hl * 128:(hl + 1) * 128],
                                       rhs=VI[:, s, hl * 128:(hl + 1) * 128], start=(s == 0), stop=(s == 3))
                return ins
            k.op(PE, fst, reads=[b_KPP, b_VI], writes=[b_ST2[0]])
            for hl, h in enumerate(heads):
                k.op(DVE, lambda e: e.scalar_tensor_tensor(out=S32[:, h, :], in0=S32[:, h, :], scalar=decs[:, h, 0:1],
                                                           in1=STt2[0][:, hl, :], op0=ALU.mult, op1=ALU.add),
                     reads=[b_S32[h], b_decs[h], b_ST2[0]], writes=[b_S32[h]])
                if last_pre:
                    k.op(POOL, lambda e: e.tensor_copy(out=SBF[:, h, 0, :], in_=S32[:, h, :]), reads=[b_S32[h]], writes=[b_SBF[h]])
            return

    def hgrn_pass1(gi, gt):
        heads = [4 * gi + i for i in range(4)]
        for j in range(8):
            s, pb = j // 2, (j % 2) * 64
            sb_i = j % 2

            def fst(e):
                ins = None
                for hl in range(4):
                    ins = e.matmul(STt2[sb_i][:, hl, :], lhsT=KPP[pb:pb + 64, s, hl * 128:(hl + 1) * 128],
                                   rhs=VI[pb:pb + 64, s, hl * 128:(hl + 1) * 128], start=True, stop=True)
                return ins
            k.op(PE, fst, reads=[b_KPP, b_VI], writes=[b_ST2[sb_i]])
            for hl, h in enumerate(heads):
                k.op(DVE, lambda e: e.scalar_tensor_tensor(out=S32[:, h, :], in0=S32[:, h, :], scalar=decs[:, h, j:j + 1],
                                                           in1=STt2[sb_i][:, hl, :], op0=ALU.mult, op1=ALU.add),
                     reads=[b_S32[h], b_decs[h], b_ST2[sb_i]], writes=[b_S32[h]])
                so = sbf_slot(gt, j + 1)
                if hl % 2 == 0:
                    k.op(ACT, lambda e: e.activation(out=SBF[:, h, so, :], in_=S32[:, h, :], func=AF.Copy), reads=[b_S32[h]], writes=[b_SBF[h]])
                else:
                    k.op(POOL, lambda e: e.tensor_copy(out=SBF[:, h, so, :], in_=S32[:, h, :]), reads=[b_S32[h]], writes=[b_SBF[h]])
            yield

    def hgrn_pass2(gi, gt):
        heads = [4 * gi + i for i in range(4)]

        def sc(hl):
            ci = hl % 2
            SCv = STt2[ci].rearrange("p q v -> p (q v)")

            def fsc(e):
                ins = None
                for j in range(8):
                    pb = (j % 2) * 64
                    ins = e.matmul(SCv[pb:pb + 64, (j // 2) * 64:(j // 2 + 1) * 64], lhsT=KP[:, hl, j * 64:(j + 1) * 64],
                                   rhs=QP[:, hl, j * 64:(j + 1) * 64], start=True, stop=True)
                return ins
            k.op(PE, fsc, reads=[b_KP, b_QP], writes=[b_ST2[ci]])
            k.op(DVE, lambda e: e.tensor_tensor(out=scm[ci][:], in0=SCv[:, 0:256], in1=mask_sc, op=ALU.mult),
                 reads=[b_ST2[ci], b_cst], writes=[b_scm[ci]])

        sc(0)
        sc(1)
        yield
        for hl, h in enumerate(heads):
            ci = hl % 2
            oi = hl % 2

            def fo(e):
                ins = None
                for j in range(8):
                    s, pb = j // 2, (j % 2) * 64
                    si = sbf_slot(gt, j)
                    e.matmul(OTt[oi][:, j * 64:(j + 1) * 64], lhsT=VI[pb:pb + 64, s, hl * 128:(hl + 1) * 128],
                             rhs=scm[ci][pb:pb + 64, s * 64:(s + 1) * 64], start=True, stop=False)
                    ins = e.matmul(OTt[oi][:, j * 64:(j + 1) * 64], lhsT=SBF[:, h, si, :], rhs=QP[:, hl, j * 64:(j + 1) * 64],
                                   start=False, stop=True)
                return ins
            k.op(PE, fo, reads=[b_VI, b_scm[ci], b_SBF[h], b_QP], writes=[b_OT[oi]])
            OC, bOC = P1[hl], b_P1[hl]
            k.op(DVE, lambda e: e.tensor_copy(out=OC[:], in_=OTt[oi][:]), reads=[b_OT[oi]], writes=[bOC])
            SQ, bSQ = P2[hl], b_P2[hl]
            k.op(ACT, lambda e: e.activation(out=SQ[:], in_=OC[:], func=AF.Square), reads=[bOC], writes=[bSQ])
            if hl + 2 < 4:
                sc(hl + 2)
            yield
        for hl, h in enumerate(heads):
            OC, bOC = P1[hl], b_P1[hl]
            SQ, bSQ = P2[hl], b_P2[hl]
            bi = mm_bank()
            k.op(PE, lambda e: e.matmul(MMt[bi][:], lhsT=onesf[:], rhs=SQ[:], start=True, stop=True), reads=[b_onesf, bSQ], writes=[b_MM[bi]])
            k.op(ACT, lambda e: e.activation(out=SQ[:], in_=MMt[bi][:], func=AF.Ln, scale=1.0 / 128, bias=EPS), reads=[b_MM[bi]], writes=[bSQ])
            k.op(ACT, lambda e: e.activation(out=SQ[:], in_=SQ[:], func=AF.Exp, scale=-0.5), reads=[bSQ], writes=[bSQ])
            k.op(DVE, lambda e: e.scalar_tensor_tensor(out=SQ[:], in0=OC[:], scalar=cols[:, 24 + h:25 + h], in1=SQ[:],
                                                       op0=ALU.mult, op1=ALU.mult), reads=[bOC, b_cols, bSQ], writes=[bSQ])
            k.op(POOL, lambda e: e.tensor_tensor(out=R5[:, h, :], in0=SQ[:], in1=R5[:, h, :], op=ALU.mult), reads=[bSQ, b_R5], writes=[b_R5])
            if hl < 3:
                yield

    def interleave(gen, fillers):
        fillers = list(fillers)
        for _ in gen:
            if fillers:
                fillers.pop(0)()
        for f in fillers:
            f()

    def out_proj(loader, nk, lhs, lbufs, gain_off, final, t):
        nkh = nk // 8 if nk == 8 else 2
        kper = 8 if nk == 8 else 11
        pend = []
        ssq_ = [stat_cols(2) for _ in range(4)]
        ssq = [a for a, _ in ssq_]
        bssq = [b for _, b in ssq_]
        for nh in range(2):
            slots = [loader(kh, nh) for kh in range(nkh)]
            banks = [0, 1, 2, 3]
            for s in range(4):
                bi = banks[s]
                if bi == 3:
                    ptile, pbuf = STt.rearrange("p q v -> p (q v)"), b_ST
                else:
                    ptile, pbuf = MMt[bi], [b_MM[bi]]

                def f(e):
                    ins = None
                    n = 0
                    for kh in range(nkh):
                        v = slot_view(slots[kh], kper)
                        for kc in range(kper):
                            ins = e.matmul(ptile[:], lhsT=lhs[:, kh * kper + kc, s * 128:(s + 1) * 128], rhs=v[:, kc, :],
                                           start=(n == 0), stop=(n == nkh * kper - 1))
                            n += 1
                    return ins
                k.op(PE, f, reads=[b_ring[sl] for sl in slots] + lbufs, writes=pbuf)
                k.op(ACT, lambda e: e.activation(out=junk[:, 0:512], in_=ptile[:], func=AF.Square, accum_out=ssq[s][:, nh:nh + 1]),
                     reads=pbuf, writes=[b_junk, bssq[s]])
                park = BT[s // 2][:, (s % 2) * 512:(s % 2 + 1) * 512]
                if nh == 0:
                    k.op(ACT, lambda e: e.activation(out=park, in_=ptile[:], func=AF.Copy), reads=pbuf, writes=[b_BT[s // 2]])
                else:
                    tot, btot = stat_cols(1)
                    k.op(POOL, lambda e: e.tensor_tensor(out=tot, in0=ssq[s][:, 0:1], in1=ssq[s][:, 1:2], op=ALU.add),
                         reads=[bssq[s]], writes=[btot])
                    r, br = rstd_small(tot, btot, 1, 1.0 / D)
                    tA, btA = ta()
                    k.op(DVE, lambda e: e.scalar_tensor_tensor(out=tA[:], in0=ptile[:], scalar=r, in1=rows[:, gain_off + 512:gain_off + 1024],
                                                               op0=ALU.mult, op1=ALU.mult), reads=pbuf + [br, b_rows], writes=[btA])
                    k.op(POOL, lambda e: e.tensor_tensor(out=X[s][:, 512:1024], in0=X[s][:, 512:1024], in1=tA[:], op=ALU.add),
                         reads=[btA, b_X[s]], writes=[b_X[s]])
                    tB, btB = ta()
                    k.op(DVE, lambda e: e.scalar_tensor_tensor(out=tB[:], in0=park, scalar=r, in1=rows[:, gain_off:gain_off + 512],
                                                               op0=ALU.mult, op1=ALU.mult), reads=[b_BT[s // 2], br, b_rows], writes=[btB])
                    k.op(POOL, lambda e: e.tensor_tensor(out=X[s][:, 0:512], in0=X[s][:, 0:512], in1=tB[:], op=ALU.add),
                         reads=[btB, b_X[s]], writes=[b_X[s]])
                    if final:
                        k.dma(SP, d_xs[s], y_out[t * T + s * 128:t * T + (s + 1) * 128, :], X[s][:], reads=[b_X[s]])
                        if t + 1 < ntiles_main:
                            load_x(x_main, t + 1, s)
                            if len(pend) == 2:
                                norm_b(pend.pop(0))
                            pend.append(norm_a(s))
                    else:
                        if len(pend) == 2:
                            norm_b(pend.pop(0))
                        pend.append(norm_a(s))
        while pend:
            norm_b(pend.pop(0))

    gt = 0
    for t in range(ntiles_pre):
        for s in range(4):
            load_x(x_prev, t, s)
        for s in range(4):
            norm_to_hT(s)
        for gi in range(2):
            hgrn_group(gi, 0, False, last_pre=(t == ntiles_pre - 1))
        conv_run(n_phase0 + (t + 1) * (-(-(nconv - n_phase0) // max(ntiles_pre, 1))))
    conv_run(nconv, flush=True)

    for s in range(4):
        load_x(x_main, 0, s)
    for t in range(ntiles_main):
        if t == 0:
            for s in range(4):
                norm_to_hT(s)
        if stop == 2:
            return finish()
        sl_u = [load_k8(wb_in, "in", c0) for c0 in (0, 512)]
        sl_v = [load_k8(wb_in, "in", c0) for c0 in (1024, 1536)]
        for s in range(4):
            gi_ = nxt("bt", 2)
            gv, bgv = BT[gi_], b_BT[gi_]
            sm, bsm = stat_cols(2)
            for hf in range(2):
                bi = tm_group(sl_v[hf], HT, s, [b_HT])
                k.op(ACT, lambda e: e.activation(out=gv[:, hf * 512:(hf + 1) * 512], in_=MMt[bi][:], func=AF.Gelu, accum_out=sm[:, hf:hf + 1]),
                     reads=[b_MM[bi]], writes=[bgv, bsm])
            nm0, bnm0 = stat_cols(1)
            nm_, bnm = stat_cols(1)
            k.op(POOL, lambda e: e.tensor_tensor(out=nm0, in0=sm[:, 0:1], in1=sm[:, 1:2], op=ALU.add), reads=[bsm], writes=[bnm0])
            k.op(POOL, lambda e: e.tensor_scalar(out=nm_, in0=nm0, scalar1=-1.0 / D, scalar2=1.0, op0=ALU.mult, op1=ALU.mult),
                 reads=[bnm0], writes=[bnm])
            vs, bvs = stat_cols(1)
            k.op(ACT, lambda e: e.activation(out=junk[:], in_=gv[:], func=AF.Square, bias=nm_, accum_out=vs),
                 reads=[bgv, bnm], writes=[b_junk, bvs])
            r, br = rstd_small(vs, bvs, 1, 1.0 / D)
            k.op(DVE, lambda e: e.tensor_scalar(out=VLN[:, s, :], in0=gv[:], scalar1=nm_, scalar2=r, op0=ALU.add, op1=ALU.mult),
                 reads=[bgv, bnm, br], writes=[b_VLN])
        for c in range(8):
            bi = fm_group(sl_u[c // 4], c % 4, HT, [b_HT])
            k.op(ACT, lambda e: e.activation(out=U[:, c, :], in_=MMt[bi][:], func=AF.Gelu), reads=[b_MM[bi]], writes=[b_U])
        for g in range(8):
            bi = mm_bank()

            def fs(e):
                ins = None
                for n in range(4):
                    ins = e.matmul(MMt[bi][:, n * 128:(n + 1) * 128], lhsT=VLN[:, n, g * 128:(g + 1) * 128], rhs=wspb[:, g, :], start=True, stop=True)
                return ins
            k.op(PE, fs, reads=[b_VLN, b_wspb], writes=[b_MM[bi]])
            tA, btA = ta()
            k.op(DVE, lambda e: e.scalar_tensor_tensor(out=tA.rearrange("p (n t) -> p n t", t=128), in0=MMt[bi].rearrange("p (n t) -> p n t", t=128),
                                                       scalar=cols[:, 16 + g:17 + g], in1=cg[:, g:g + 1, :].to_broadcast([128, 4, 128]),
                                                       op0=ALU.mult, op1=ALU.add), reads=[b_MM[bi], b_cols, b_cg], writes=[btA])
            k.op(POOL, lambda e: e.tensor_tensor(out=U[:, g, :], in0=tA[:], in1=U[:, g, :], op=ALU.mult), reads=[btA, b_U], writes=[b_U])
        if stop == 3:
            return finish()
        hgrn_group(0, gt, True)
        sl_ga = [load_k8(wb_in, "in", c0) for c0 in (6144, 6656)]

        def ga_item(c):
            bi = fm_group(sl_ga[c // 4], c % 4, HT, [b_HT])
            k.op(ACT, lambda e: e.activation(out=R4[:, c, :], in_=MMt[bi][:], func=AF.Sigmoid), reads=[b_MM[bi]], writes=[b_R4])
        interleave(hgrn_pass1(0, gt), [(lambda c=c: ga_item(c)) for c in range(8)])
        sl_pa = [load_k8(wb_pa, "pa", c0) for c0 in (0, 512)]

        def pa_item(c):
            bi = fm_group(sl_pa[c // 4], c % 4, U, [b_U])
            k.op(DVE, lambda e: e.tensor_tensor(out=R4[:, c, :], in0=MMt[bi][:], in1=R4[:, c, :], op=ALU.mult), reads=[b_MM[bi], b_R4], writes=[b_R4])
        interleave(hgrn_pass2(0, gt), [(lambda c=c: pa_item(c)) for c in range(8)])
        hgrn_group(1, gt, True)
        sl_gb = [load_k8(wb_in, "in", c0) for c0 in (7168, 7680)]

        def gb_item(c):
            bi = fm_group(sl_gb[c // 4], c % 4, HT, [b_HT])
            k.op(ACT, lambda e: e.activation(out=GB[:, c, :], in_=MMt[bi][:], func=AF.Sigmoid), reads=[b_MM[bi]], writes=[b_U])
        interleave(hgrn_pass1(1, gt), [(lambda c=c: gb_item(c)) for c in range(8)])
        interleave(hgrn_pass2(1, gt), [])
        gt += 1
        sl_pb = [load_k8(wb_pb, "pb", c0) for c0 in (0, 512)]
        for c in range(8):
            bi = fm_group(sl_pb[c // 4], c % 4, R5, [b_R5])
            tA, btA = ta()
            k.op(DVE, lambda e: e.tensor_tensor(out=tA[:], in0=MMt[bi][:], in1=GB[:, c, :], op=ALU.mult), reads=[b_MM[bi], b_U], writes=[btA])
            k.op(POOL, lambda e: e.tensor_tensor(out=R4[:, c, :], in0=tA[:], in1=R4[:, c, :], op=ALU.add), reads=[btA, b_R4], writes=[b_R4])
        if stop == 6:
            return finish()
        out_proj(lambda kh, nh: load_k8(wb_out, "out", nh * 512), 8, R4, [b_R4], 0, False, t)
        if stop == 7:
            return finish()
        for blk in range(11):
            sl = load_up(blk)
            for jj in range(2):
                bg = fm_group(sl, jj, HT, [b_HT])
                bu = fm_group(sl, 2 + jj, HT, [b_HT])
                tA, btA = ta()
                k.op(ACT, lambda e: e.activation(out=tA[:], in_=MMt[bg][:], func=AF.Silu), reads=[b_MM[bg]], writes=[btA])
                k.op(DVE, lambda e: e.tensor_tensor(out=FACT[:, 2 * blk + jj, :], in0=MMt[bu][:], in1=tA[:], op=ALU.mult),
                     reads=[b_MM[bu], btA], writes=[b_FACT])
        out_proj(load_dn, 22, FACT, [b_FACT], D, True, t)

    return finish()


def _unused():
    for s in range(4):
        if d_xs[s].cnt:
            nc.sync.wait_ge(d_xs[s].sem, d_xs[s].cnt)
    nc.all_engine_barrier()
    for sm in k.sems:
        nc.gpsimd.sem_clear(sm)
    return nc


def _host_inputs(inp):
    f = lambda a: np.ascontiguousarray(np.asarray(a, dtype=np.float32))
    x = f(inp["x"])
    col = lambda v: f(v).reshape(8, 128).T
    cols = np.concatenate([col(inp["pre_mix_gain"][0]), col(inp["pre_ffn_gain"][0]), col(inp["sgu_norm_gain"][0]),
                           col(inp["hgrn_norm_gain"][0]), col(inp["lb_logits"][0]), col(inp["lb_logits"][1])], axis=1)
    rows = np.concatenate([np.broadcast_to(f(inp["post_mix_gain"][0])[None, :], (128, D)),
                           np.broadcast_to(f(inp["post_ffn_gain"][0])[None, :], (128, D))], axis=1)
    wspT = f(inp["w_spatial"][0]).transpose(2, 0, 1).reshape(128, 1024)
    ident = np.eye(128, dtype=np.float32)
    p = np.arange(128)
    tt = np.arange(64)
    msc = (p[:, None] % 64 <= tt[None, :]).astype(np.float32)
    mask_sc = np.tile(msc, (1, 4))
    mask_scan = np.ones((128, 512), np.float32)
    mask_scan[:, 0::64] = 0.0
    mask_sguT = (p[:, None] // 64 <= p[None, :] // 64).astype(np.float32)
    consts = np.concatenate([ident, mask_sc, mask_scan, mask_sguT], axis=1)
    sguA = np.stack([f(inp["sgu_norm_bias"][0]), np.ones(D, np.float32)], axis=0)
    bsp = f(inp["b_spatial"][0]).reshape(1, D)
    shared = {
        "w_in": f(inp["w_in"][0]), "w_pa": f(inp["w_proj_sgu"][0]), "w_pb": f(inp["w_proj_hgrn"][0]),
        "w_out": f(inp["w_out"][0]), "w_up": f(inp["w_ffn_up"][0]), "w_dn": f(inp["w_ffn_down"][0]),
        "cols": f(cols), "rows_bc": f(rows), "wspT": f(wspT), "consts": f(consts), "sguA": f(sguA), "bsp": bsp,
    }
    maps = []
    zeros = np.zeros((NTOK, D), np.float32)
    for c in range(8):
        b, half = c // 2, c % 2
        m = dict(shared)
        m["x_main"] = np.ascontiguousarray(x[b, half * NTOK:(half + 1) * NTOK])
        m["x_prev"] = np.ascontiguousarray(x[b, 0:NTOK]) if half == 1 else zeros
        maps.append(m)
    return maps


def kernel(**inputs):
    maps = _host_inputs(inputs)
    nc = build_nc()
    res = run_bass_kernel_spmd(nc, maps, core_ids=list(range(8)))
    out = np.empty((4, 8192, D), np.float32)
    for c in range(8):
        b, half = c // 2, c % 2
        out[b, half * NTOK:(half + 1) * NTOK] = np.asarray(res.results[c]["y_out"], dtype=np.float32)
    return out
```

```python
import numpy as np
import concourse.bass as bass
import concourse.mybir as mybir
from concourse.bass_utils import run_bass_kernel_spmd

F32 = mybir.dt.float32
BF16 = mybir.dt.bfloat16
AF = mybir.ActivationFunctionType
ALU = mybir.AluOpType

D = 1024
NTOK = 4096
T = 512
NT = NTOK // T
FH = 2816
EPS = 1e-6
NCOL = 48


class Eng:
    def __init__(self, nc, name, h, kind):
        self.h = h
        self.kind = kind
        self.sem = nc.alloc_semaphore("se_" + name)
        self.cnt = 0
        self.waited = {}


class DSem:
    def __init__(self, nc, name):
        self.sem = nc.alloc_semaphore("sd_" + name)
        self.cnt = 0


class Buf:
    def __init__(self, name):
        self.name = name
        self.wr = {}
        self.rd = {}
        self.ov = []


def overlap(*bufs):
    for a in bufs:
        for b in bufs:
            if a is not b and b not in a.ov:
                a.ov.append(b)


class K:
    def __init__(self, nc):
        self.nc = nc
        self.pe = Eng(nc, "pe", nc.tensor, "pe")
        self.act = Eng(nc, "act", nc.scalar, "c")
        self.dve = Eng(nc, "dve", nc.vector, "c")
        self.pool = Eng(nc, "pool", nc.gpsimd, "c")
        self.sp = Eng(nc, "sp", nc.sync, "q")
        self.sems = [e.sem for e in (self.pe, self.act, self.dve, self.pool, self.sp)]
        self.semobj = {}

    def dsem(self, name):
        d = DSem(self.nc, name)
        self.sems.append(d.sem)
        return d

    def _deps(self, eng, reads, writes):
        need = {}

        def add(dct, same_ok):
            for key, (sem, val, src) in dct.items():
                if src is eng:
                    if eng.kind == "pe":
                        continue
                if need.get(key, (None, 0))[1] < val:
                    need[key] = (sem, val)

        for b in reads:
            for o in [b] + b.ov:
                add(o.wr, True)
        for b in writes:
            for o in [b] + b.ov:
                add(o.wr, False)
                add(o.rd, False)
        for key, (sem, val) in need.items():
            if eng.waited.get(key, 0) < val:
                eng.h.wait_ge(sem, val)
                eng.waited[key] = val

    def _mark(self, tok, reads, writes):
        key = id(tok[0])
        for b in writes:
            b.wr[key] = tok
        for b in reads:
            b.rd[key] = tok

    def op(self, eng, fn, reads=(), writes=()):
        self._deps(eng, reads, writes)
        ins = fn(eng.h)
        eng.cnt += 1
        ins.then_inc(eng.sem, 1)
        self._mark((eng.sem, eng.cnt, eng), reads, writes)

    def dma(self, q, ds, out, in_, reads=(), writes=()):
        self._deps(q, reads, writes)
        ins = q.h.dma_start(out=out, in_=in_)
        ds.cnt += 16
        ins.then_inc(ds.sem, 16)
        self._mark((ds.sem, ds.cnt, None), reads, writes)


def build_nc(ntiles_main=NT, ntiles_pre=NT, stop=99):
    nc = bass.Bass("TRN2", target_bir_lowering=False)

    def din(name, shape, dt=F32):
        return nc.dram_tensor(name, shape, dt, kind="ExternalInput").ap()

    x_main = din("x_main", [NTOK, D])
    x_prev = din("x_prev", [NTOK, D])
    w_in = din("w_in", [D, 8192])
    w_pa = din("w_pa", [D, D])
    w_pb = din("w_pb", [D, D])
    w_out = din("w_out", [D, D])
    w_up = din("w_up", [D, 2 * FH])
    w_dn = din("w_dn", [FH, D])
    cols_d = din("cols", [128, NCOL])
    rows_d = din("rows_bc", [128, 2 * D])
    wspT_d = din("wspT", [128, 8 * 128])
    cst_d = din("consts", [128, 128 + 256 + 512 + 128])
    sguA_d = din("sguA", [2, D])
    bsp_d = din("bsp", [1, D])
    y_out = nc.dram_tensor("y_out", [NTOK, D], F32, kind="ExternalOutput").ap()

    def dscr(name, shape):
        return nc.dram_tensor(name, shape, BF16, kind="Internal").ap()

    wb_in = dscr("wb_in", [D, 8192])
    wb_pa = dscr("wb_pa", [D, D])
    wb_pb = dscr("wb_pb", [D, D])
    wb_out = dscr("wb_out", [D, D])
    wb_up = dscr("wb_up", [D, 2 * FH])
    wb_dn = dscr("wb_dn", [FH, D])

    k = K(nc)
    PE, ACT, DVE, POOL, SP = k.pe, k.act, k.dve, k.pool, k.sp

    def sb(name, shape, dt):
        return nc.alloc_sbuf_tensor("s_" + name, shape, dt)

    cols = sb("cols", [128, NCOL], F32)
    der = sb("der", [128, 64], F32)
    rows = sb("rows", [128, 2 * D], F32)
    cst = sb("cst", [128, 1024], F32)
    identb = sb("identb", [128, 128], BF16)
    onesf = sb("onesf", [128, 128], F32)
    wspb = sb("wspb", [128, 8, 128], BF16)
    cg = sb("cg", [128, 8, 128], F32)
    X = [sb(f"x{s}", [128, D], F32) for s in range(4)]
    wspf = X[1].rearrange("p (g t) -> p g t", t=128)
    sguA = X[2][0:2, :]
    sguB = X[3][0:2, :]
    HT = sb("hT", [128, 8, T], BF16)
    RR = sb("RR", [128, 12288], BF16)
    R4 = sb("R4", [128, 8, T], BF16)
    R5 = sb("R5", [128, 8, T], BF16)
    S32 = sb("S32", [128, 8, 128], F32)
    SBF = sb("SBF", [128, 8, 9, 128], BF16)
    decs = sb("decs", [128, 8, 8], F32)
    scm = [sb(f"scm{i}", [128, 256], BF16) for i in range(2)]
    Bt = [sb(f"Bt{i}", [128, T], F32) for i in range(4)]
    TA = [sb(f"ta{i}", [128, T], F32) for i in range(3)]
    P1all = sb("P1all", [128, 4, T], F32)
    P1 = [P1all[:, i, :] for i in range(4)]
    Y = P1all[:, 0:2, :].rearrange("p a n -> p (a n)")
    P2 = [sb(f"P2_{i}", [128, T], F32) for i in range(4)]
    P3 = [sb(f"P3_{i}", [128, T], F32) for i in range(4)]
    bia = sb("bia", [128, 4], F32)
    kppT = [sb(f"kppT{i}", [128, T], BF16) for i in range(4)]
    junk = sb("junk", [128, D], BF16)
    hb = [sb(f"hb{i}", [128, D], BF16) for i in range(2)]
    BT = [sb(f"BT{i}", [128, D], F32) for i in range(2)]
    stat = sb("stat", [128, 64], F32)
    RING = sb("RING", [128, 4 * 5632], BF16)

    U = RR[:, 0:4096].rearrange("p (k n) -> p k n", n=T)
    VLN = RR[:, 4096:8192].rearrange("p (s n) -> p s n", n=D)
    QP = RR[:, 4096:6144].rearrange("p (k n) -> p k n", n=T)
    KP = RR[:, 6144:8192].rearrange("p (k n) -> p k n", n=T)
    KPP = RR[:, 8192:10240].rearrange("p (s n) -> p s n", n=T)
    VI = RR[:, 10240:12288].rearrange("p (s n) -> p s n", n=T)
    GB = U
    FACT = RR[:, 0:11264].rearrange("p (k n) -> p k n", n=T)

    MMt = [nc.alloc_psum_tensor(f"mm{i}", [128, T], F32) for i in range(3)]
    STt2 = [nc.alloc_psum_tensor(f"st{i}", [128, 4, 128], F32) for i in range(2)]
    STt = STt2[0]
    TPt = nc.alloc_psum_tensor("tp", [128, 8, 128], BF16)
    OTt = [nc.alloc_psum_tensor(f"ot{i}", [128, T], F32) for i in range(2)]

    b_cols, b_der, b_rows, b_cst, b_identb, b_onesf = (Buf(n) for n in ("cols", "der", "rows", "cst", "identb", "onesf"))
    b_wspf, b_wspb, b_cg, b_sguA, b_sguB = (Buf(n) for n in ("wspf", "wspb", "cg", "sguA", "sguB"))
    b_X = [Buf(f"x{s}") for s in range(4)]
    overlap(b_X[1], b_wspf)
    overlap(b_X[2], b_sguA)
    overlap(b_X[3], b_sguB)
    b_HT = Buf("hT")
    b_U, b_VLN, b_QP, b_KP, b_KPP, b_VI, b_FACT = (Buf(n) for n in ("U", "VLN", "QP", "KP", "KPP", "VI", "FACT"))
    overlap(b_U, b_FACT)
    overlap(b_VLN, b_QP, b_FACT)
    overlap(b_VLN, b_KP, b_FACT)
    overlap(b_KPP, b_FACT)
    overlap(b_VI, b_FACT)
    b_R4, b_R5 = Buf("R4"), Buf("R5")
    b_S32 = [Buf(f"S32_{h}") for h in range(8)]
    b_SBF = [Buf(f"SBF_{h}") for h in range(8)]
    b_decs = [Buf(f"decs{h}") for h in range(8)]
    b_scm = [Buf("scm0"), Buf("scm1")]
    b_Bt = [Buf(f"Bt{i}") for i in range(4)]
    b_TA = [Buf(f"ta{i}") for i in range(3)]
    b_P1 = [Buf(f"P1_{i}") for i in range(4)]
    b_P2 = [Buf(f"P2_{i}") for i in range(4)]
    b_P3 = [Buf(f"P3_{i}") for i in range(4)]
    b_bia = [Buf(f"bia{i}") for i in range(4)]
    b_Y = Buf("Y")
    overlap(b_Y, b_P1[0])
    overlap(b_Y, b_P1[1])
    b_kppT = [Buf(f"kppT{i}") for i in range(4)]
    b_junk = Buf("junk")
    b_hb = [Buf("hb0"), Buf("hb1")]
    b_BT = [Buf("BT0"), Buf("BT1")]
    b_stat = Buf("stat")
    b_ring = [Buf(f"ring{i}") for i in range(4)]
    b_stb = [Buf("stb0"), Buf("stb1")]
    b_MM = [Buf(f"mm{i}") for i in range(3)]
    b_ST2 = [Buf(f"st{i}") for i in range(2)]
    b_ST = [b_ST2[0]]
    b_TP = Buf("tp")
    b_OT = [Buf("ot0"), Buf("ot1")]
    b_wb = {n: Buf("wb_" + n) for n in ("in", "pa", "pb", "out", "up", "dn")}

    d_const = k.dsem("const")
    d_ring = [k.dsem(f"ring{i}") for i in range(4)]
    d_stl = [k.dsem(f"stl{i}") for i in range(3)]
    d_conv = [k.dsem("conv0"), k.dsem("conv1")]
    d_xl = [k.dsem(f"xl{s}") for s in range(4)]
    d_xs = [k.dsem(f"xs{s}") for s in range(4)]
    d_y = k.dsem("y")

    def finish():
        for s in range(4):
            if d_xs[s].cnt:
                nc.sync.wait_ge(d_xs[s].sem, d_xs[s].cnt)
        if d_y.cnt:
            nc.sync.wait_ge(d_y.sem, d_y.cnt)
        nc.all_engine_barrier()
        for sm in k.sems:
            nc.gpsimd.sem_clear(sm)
        return nc

    for sm in k.sems:
        nc.gpsimd.sem_clear(sm)
    nc.all_engine_barrier()

    k.dma(SP, d_const, cols[:], cols_d[:, :], writes=[b_cols])
    k.dma(SP, d_const, rows[:], rows_d[:, :], writes=[b_rows])
    k.dma(SP, d_const, cst[:], cst_d[:, :], writes=[b_cst])
    k.dma(SP, d_const, X[1][:], wspT_d[:, :], writes=[b_wspf])
    k.dma(SP, d_const, sguA, sguA_d[:, :], writes=[b_sguA])
    k.dma(SP, d_const, sguB[1:2, :], bsp_d[:, :], writes=[b_sguB])
    for b_ in (b_cols, b_rows, b_cst, b_wspf, b_sguA, b_sguB):
        b_.wr[id(d_const.sem)] = (d_const.sem, d_const.cnt, None)
    ident_f = cst[:, 0:128]
    mask_sc = cst[:, 128:384]
    mask_scan = cst[:, 384:896]
    mask_sgu = cst[:, 896:1024]

    k.op(DVE, lambda e: e.tensor_copy(out=identb[:], in_=ident_f), reads=[b_cst], writes=[b_identb])
    k.op(POOL, lambda e: e.memset(onesf[:], 1.0), writes=[b_onesf])
    k.op(POOL, lambda e: e.memset(der[:, 40:48], -0.5), writes=[b_der])
    k.op(POOL, lambda e: e.memset(S32.rearrange("p h v -> p (h v)"), 0.0), writes=b_S32)
    k.op(POOL, lambda e: e.memset(SBF.rearrange("p h j v -> p (h j v)"), 0.0), writes=b_SBF)
    k.op(DVE, lambda e: e.tensor_tensor(out=der[:, 24:32], in0=cols[:, 40:48], in1=cols[:, 32:40], op=ALU.subtract),
         reads=[b_cols], writes=[b_der])
    k.op(ACT, lambda e: e.activation(out=der[:, 24:32], in_=der[:, 24:32], func=AF.Exp), reads=[b_der], writes=[b_der])
    k.op(DVE, lambda e: e.tensor_scalar(out=der[:, 32:40], in0=der[:, 24:32], scalar1=1.0, scalar2=None, op0=ALU.add),
         reads=[b_der], writes=[b_der])
    k.op(DVE, lambda e: e.reciprocal(out=der[:, 0:8], in_=der[:, 32:40]), reads=[b_der], writes=[b_der])
    k.op(DVE, lambda e: e.tensor_tensor(out=der[:, 8:16], in0=der[:, 24:32], in1=der[:, 0:8], op=ALU.mult),
         reads=[b_der], writes=[b_der])
    k.op(ACT, lambda e: e.activation(out=der[:, 16:24], in_=der[:, 8:16], func=AF.Ln), reads=[b_der], writes=[b_der])
    LB = lambda h: der[:, h:h + 1]
    LNOML = lambda h: der[:, 16 + h:17 + h]
    NEGH = der[:, 40:41]

    w3 = wspf
    ones512 = onesf[:, 0:1].to_broadcast([128, T])
    k.op(DVE, lambda e: e.tensor_tensor(out=w3, in0=w3, in1=mask_sgu.unsqueeze(1).to_broadcast([128, 8, 128]), op=ALU.mult),
         reads=[b_wspf, b_cst], writes=[b_wspf])
    k.op(POOL, lambda e: e.tensor_copy(out=wspb[:], in_=wspf), reads=[b_wspf], writes=[b_wspb])

    def f_wsum2(e):
        ins = None
        for g in range(8):
            dst = MMt[g // 4][0:1, (g % 4) * 128:(g % 4 + 1) * 128]
            ins = e.matmul(dst, lhsT=onesf[:, 0:1], rhs=wspf[:, g, :], start=True, stop=True)
        return ins
    k.op(PE, f_wsum2, reads=[b_onesf, b_wspf], writes=[b_MM[0], b_MM[1]])
    k.op(DVE, lambda e: e.tensor_copy(out=sguB[0:1, 0:512], in_=MMt[0][0:1, :]), reads=[b_MM[0]], writes=[b_sguB])
    k.op(DVE, lambda e: e.tensor_copy(out=sguB[0:1, 512:1024], in_=MMt[1][0:1, :]), reads=[b_MM[1]], writes=[b_sguB])

    def f_cg(e):
        ins = None
        for g in range(8):
            dst = MMt[g // 4][:, (g % 4) * 128:(g % 4 + 1) * 128]
            ins = e.matmul(dst, lhsT=sguA[:, g * 128:(g + 1) * 128], rhs=sguB[:, g * 128:(g + 1) * 128], start=True, stop=True)
        return ins
    k.op(PE, f_cg, reads=[b_sguA, b_sguB], writes=[b_MM[0], b_MM[1]])
    k.op(DVE, lambda e: e.tensor_copy(out=cg[:, 0:4, :].rearrange("p g t -> p (g t)"), in_=MMt[0][:]), reads=[b_MM[0]], writes=[b_cg])
    k.op(DVE, lambda e: e.tensor_copy(out=cg[:, 4:8, :].rearrange("p g t -> p (g t)"), in_=MMt[1][:]), reads=[b_MM[1]], writes=[b_cg])

    if stop == 0:
        return finish()
    stf = [R4.rearrange("p k n -> p (k n)").bitcast(F32), R5.rearrange("p k n -> p (k n)").bitcast(F32), RR[:, 0:4096].bitcast(F32)]
    stb = [RR[:, 4096:6144], RR[:, 6144:8192]]
    b_stf = [Buf(f"stf{i}") for i in range(3)]
    overlap(b_stf[0], b_R4)
    overlap(b_stf[1], b_R5)
    overlap(b_stf[2], b_U, b_FACT)
    overlap(b_stb[0], b_VLN, b_QP, b_FACT)
    overlap(b_stb[1], b_VLN, b_KP, b_FACT)
    conv = []

    def conv_win(c0):
        for kc in range(8):
            conv.append((w_in[kc * 128:(kc + 1) * 128, c0:c0 + 2048], wb_in[kc * 128:(kc + 1) * 128, c0:c0 + 2048], cols[:, kc:kc + 1], "in"))
    conv_win(2048)
    conv_win(4096)
    n_phase0 = len(conv)
    conv_win(0)
    conv_win(6144)
    for nm, src, dst in (("pa", w_pa, wb_pa), ("pb", w_pb, wb_pb), ("out", w_out, wb_out)):
        for kc in range(8):
            conv.append((src[kc * 128:(kc + 1) * 128, :], dst[kc * 128:(kc + 1) * 128, :], None, nm))
    for kc in range(8):
        for c0, cw in ((0, 2048), (2048, 2048), (4096, 1536)):
            conv.append((w_up[kc * 128:(kc + 1) * 128, c0:c0 + cw], wb_up[kc * 128:(kc + 1) * 128, c0:c0 + cw], cols[:, 8 + kc:9 + kc], "up"))
    for kc in range(22):
        conv.append((w_dn[kc * 128:(kc + 1) * 128, :], wb_dn[kc * 128:(kc + 1) * 128, :], None, "dn"))
    nconv = len(conv)
    cv = {"next": 0, "loaded": 0, "stored": 0}

    def conv_load(i):
        src = conv[i][0]
        w = src.shape[1]
        k.dma(SP, d_stl[i % 3], stf[i % 3][:, 0:w], src, writes=[b_stf[i % 3]])

    def conv_cast(i):
        src, dst, gc, nm = conv[i]
        w = src.shape[1]
        j = i % 2
        if i % 2 == 0:
            if gc is None:
                k.op(DVE, lambda e: e.tensor_copy(out=stb[j][:, 0:w], in_=stf[i % 3][:, 0:w]), reads=[b_stf[i % 3]], writes=[b_stb[j]])
            else:
                k.op(DVE, lambda e: e.tensor_scalar(out=stb[j][:, 0:w], in0=stf[i % 3][:, 0:w], scalar1=gc, scalar2=None, op0=ALU.mult),
                     reads=[b_stf[i % 3], b_cols], writes=[b_stb[j]])
        else:
            if gc is None:
                k.op(ACT, lambda e: e.activation(out=stb[j][:, 0:w], in_=stf[i % 3][:, 0:w], func=AF.Copy), reads=[b_stf[i % 3]], writes=[b_stb[j]])
            else:
                k.op(ACT, lambda e: e.activation(out=stb[j][:, 0:w], in_=stf[i % 3][:, 0:w], func=AF.Copy, scale=gc),
                     reads=[b_stf[i % 3], b_cols], writes=[b_stb[j]])

    def conv_store(i):
        src, dst, gc, nm = conv[i]
        w = src.shape[1]
        j = i % 2
        k.dma(ACT, d_conv[j], dst, stb[j][:, 0:w], reads=[b_stb[j]], writes=[b_wb[nm]])

    def conv_run(upto, flush=False):
        upto = min(upto, nconv)
        while cv["next"] < upto:
            i = cv["next"]
            while cv["loaded"] < min(i + 3, nconv):
                conv_load(cv["loaded"])
                cv["loaded"] += 1
            conv_cast(i)
            if i - 1 >= cv["stored"]:
                conv_store(cv["stored"])
                cv["stored"] += 1
            cv["next"] += 1
        if flush:
            while cv["stored"] < cv["next"]:
                conv_store(cv["stored"])
                cv["stored"] += 1

    conv_run(n_phase0, flush=True)

    if stop == 1:
        return finish()
    ring_ctr = [0]

    def ring_slot():
        i = ring_ctr[0] % 4
        ring_ctr[0] += 1
        return i

    def slot_view(i, kk):
        return RING[:, i * 5632:i * 5632 + kk * 512].rearrange("p (k n) -> p k n", n=512)

    def load_k8(wb, nm, c0):
        i = ring_slot()
        k.dma(SP, d_ring[i], slot_view(i, 8), wb[:, c0:c0 + 512].rearrange("(k p) n -> p k n", p=128),
              reads=[b_wb[nm]], writes=[b_ring[i]])
        return i

    def load_up(blk):
        i = ring_slot()
        v = slot_view(i, 8)
        j = 2 * blk
        k.dma(SP, d_ring[i], v[:, :, 0:256], wb_up[:, j * 128:j * 128 + 256].rearrange("(k p) n -> p k n", p=128),
              reads=[b_wb["up"]], writes=[b_ring[i]])
        k.dma(SP, d_ring[i], v[:, :, 256:512], wb_up[:, FH + j * 128:FH + j * 128 + 256].rearrange("(k p) n -> p k n", p=128),
              reads=[b_wb["up"]], writes=[b_ring[i]])
        return i

    def load_dn(kh, nh):
        i = ring_slot()
        k.dma(SP, d_ring[i], slot_view(i, 11), wb_dn[kh * 1408:(kh + 1) * 1408, nh * 512:(nh + 1) * 512].rearrange("(k p) n -> p k n", p=128),
              reads=[b_wb["dn"]], writes=[b_ring[i]])
        return i

    mm_ctr = [0, 0]
    MMt.extend([OTt[0], OTt[1], STt2[0].rearrange("p q v -> p (q v)"), STt2[1].rearrange("p q v -> p (q v)")])
    b_MM.extend([b_OT[0], b_OT[1], b_ST2[0], b_ST2[1]])

    def mm_bank(wide=False):
        if wide:
            i = mm_ctr[1] % 7
            mm_ctr[1] += 1
        else:
            i = mm_ctr[0] % 3
            mm_ctr[0] += 1
        return i

    ta_ctr = [0]

    def ta():
        i = ta_ctr[0] % 3
        ta_ctr[0] += 1
        return TA[i], b_TA[i]

    st_ctr = [0]
    misc = {"stat": 0, "hb": 0, "bt": 0, "kppT": 0, "sc": 0, "ot": 0}

    def nxt(name, n):
        v = misc[name] % n
        misc[name] += 1
        return v

    b_stats = [Buf(f"stat{i}") for i in range(32)]

    def stat_cols(n):
        i = misc["stat"] % 32
        misc["stat"] += 1
        return stat[:, 2 * i:2 * i + n], b_stats[i]

    def fm_group(slot, chunk, rhs, rbufs, wide=True):
        bi = mm_bank(wide)
        v = slot_view(slot, 8)

        def f(e):
            ins = None
            for kc in range(8):
                ins = e.matmul(MMt[bi][:], lhsT=v[:, kc, chunk * 128:(chunk + 1) * 128], rhs=rhs[:, kc, :], start=(kc == 0), stop=(kc == 7))
            return ins
        k.op(PE, f, reads=[b_ring[slot]] + rbufs, writes=[b_MM[bi]])
        return bi

    def tm_group(slot, lhs, s, rbufs, wide=True):
        bi = mm_bank(wide)
        v = slot_view(slot, 8)

        def f(e):
            ins = None
            for kc in range(8):
                ins = e.matmul(MMt[bi][:], lhsT=lhs[:, kc, s * 128:(s + 1) * 128], rhs=v[:, kc, :], start=(kc == 0), stop=(kc == 7))
            return ins
        k.op(PE, f, reads=[b_ring[slot]] + rbufs, writes=[b_MM[bi]])
        return bi

    def rstd_small(ss_ap, ss_buf, n, scale):
        m, bm = stat_cols(n)
        r, br = stat_cols(n)
        k.op(POOL, lambda e: e.tensor_scalar(out=m, in0=ss_ap, scalar1=scale, scalar2=EPS, op0=ALU.mult, op1=ALU.add),
             reads=[ss_buf], writes=[bm])
        k.op(POOL, lambda e: e.tensor_tensor(out=r, in0=m, in1=NEGH, op=ALU.pow),
             reads=[bm, b_der], writes=[br])
        return r, br

    def norm_to_hT(s):
        norm_b(norm_a(s))

    def norm_a(s):
        ss, bss = stat_cols(1)
        k.op(ACT, lambda e: e.activation(out=junk[:], in_=X[s][:], func=AF.Square, accum_out=ss),
             reads=[b_X[s]], writes=[b_junk, bss])
        r, br = rstd_small(ss, bss, 1, 1.0 / D)
        hi = nxt("hb", 2)
        k.op(DVE, lambda e: e.tensor_scalar(out=hb[hi][:], in0=X[s][:], scalar1=r, scalar2=None, op0=ALU.mult),
             reads=[b_X[s], br], writes=[b_hb[hi]])
        return (s, hi)

    def norm_b(sh):
        s, hi = sh

        def f(e):
            ins = None
            for kc in range(8):
                ins = e.transpose(TPt[:, kc, :], hb[hi][:, kc * 128:(kc + 1) * 128], identb[:])
            return ins
        k.op(PE, f, reads=[b_hb[hi], b_identb], writes=[b_TP])
        k.op(DVE, lambda e: e.tensor_copy(out=HT[:, :, s * 128:(s + 1) * 128], in_=TPt[:]), reads=[b_TP], writes=[b_HT])

    def load_x(src, t, s):
        k.dma(SP, d_xl[s], X[s][:], src[t * T + s * 128:t * T + (s + 1) * 128, :], writes=[b_X[s]])

    def sbf_slot(gt, j):
        return (8 * gt + j) % 9

    def hgrn_group(gi, gt, main, last_pre=False):
        heads = [4 * gi + i for i in range(4)]
        sl_f = load_k8(wb_in, "in", 3072 + gi * 512)
        sl_i = load_k8(wb_in, "in", 4096 + gi * 512)
        if main:
            sl_g = load_k8(wb_in, "in", 5120 + gi * 512)
            sl_q = load_k8(wb_in, "in", 2048 + gi * 512)
        for hl, h in enumerate(heads):
            bi = fm_group(sl_f, hl, HT, [b_HT])
            z = MMt[bi]
            E, bE = ta()
            k.op(ACT, lambda e: e.activation(out=E[:], in_=z[:], func=AF.Exp, scale=-1.0), reads=[b_MM[bi]], writes=[bE])
            k.op(ACT, lambda e: e.activation(out=P2[hl][:], in_=E[:], func=AF.Ln, scale=LB(h), bias=1.0), reads=[bE, b_der], writes=[b_P2[hl]])
            k.op(ACT, lambda e: e.activation(out=P3[hl][:], in_=E[:], func=AF.Ln, scale=1.0, bias=1.0), reads=[bE], writes=[b_P3[hl]])
            k.op(DVE, lambda e: e.scalar_tensor_tensor(out=P1[hl][:], in0=z[:], scalar=-1.0, in1=P3[hl][:], op0=ALU.mult, op1=ALU.subtract),
                 reads=[b_MM[bi], b_P3[hl]], writes=[b_P1[hl]])
        for s in range(4):
            bi = tm_group(sl_i, HT, s, [b_HT])
            k.op(DVE, lambda e: e.tensor_copy(out=VI[:, s, :], in_=MMt[bi][:]), reads=[b_MM[bi]], writes=[b_VI])
        if main:
            for hp in range(0, 4, 2):
                st = []
                for hl in (hp, hp + 1):
                    bi = fm_group(sl_g, hl, HT, [b_HT])
                    E, bE = ta()
                    st.append((hl, heads[hl], bi, E, bE))
                    k.op(ACT, lambda e: e.activation(out=E[:], in_=MMt[bi][:], func=AF.Exp, scale=-1.0), reads=[b_MM[bi]], writes=[bE])
                for hl, h, bi, E, bE in st:
                    k.op(ACT, lambda e: e.activation(out=E[:], in_=E[:], func=AF.Ln, bias=1.0), reads=[bE], writes=[bE])
                for hl, h, bi, E, bE in st:
                    k.op(ACT, lambda e: e.activation(out=E[:], in_=E[:], func=AF.Exp, scale=-1.0), reads=[bE], writes=[bE])
                for hl, h, bi, E, bE in st:
                    k.op(DVE, lambda e: e.tensor_tensor(out=R5[:, h, :], in0=MMt[bi][:], in1=E[:], op=ALU.mult), reads=[b_MM[bi], bE], writes=[b_R5])
        for hl, h in enumerate(heads):
            k.op(POOL, lambda e: e.tensor_tensor(out=P2[hl][:], in0=P2[hl][:], in1=P3[hl][:], op=ALU.subtract),
                 reads=[b_P2[hl], b_P3[hl]], writes=[b_P2[hl]])
            B, bB = Bt[hl], b_Bt[hl]
            if main:
                k.op(DVE, lambda e: e.tensor_tensor_scan(out=B[:], data0=mask_scan, data1=P2[hl][:], initial=0.0, op0=ALU.mult, op1=ALU.add),
                     reads=[b_cst, b_P2[hl]], writes=[bB])
            else:
                k.op(DVE, lambda e: e.tensor_tensor_scan(out=B[:], data0=ones512, data1=P2[hl][:], initial=0.0, op0=ALU.mult, op1=ALU.add),
                     reads=[b_onesf, b_P2[hl]], writes=[bB])
            k.op(POOL, lambda e: e.tensor_tensor(out=P1[hl][:], in0=P1[hl][:], in1=B[:], op=ALU.subtract),
                 reads=[b_P1[hl], bB], writes=[b_P1[hl]])
            if main:
                B3 = B.rearrange("p (c l) -> p c l", l=64)
                k.op(POOL, lambda e: e.tensor_tensor(out=P3[hl].rearrange("p (c l) -> p c l", l=64), in0=P1[hl].rearrange("p (c l) -> p c l", l=64),
                                                     in1=B3[:, :, 63:64].to_broadcast([128, 8, 64]), op=ALU.add),
                     reads=[b_P1[hl], bB], writes=[b_P3[hl]])
            else:
                k.op(POOL, lambda e: e.tensor_tensor(out=bia[:, hl:hl + 1], in0=B[:, 511:512], in1=LNOML(h), op=ALU.add),
                     reads=[bB, b_der], writes=[b_bia[hl]])
        for hl, h in enumerate(heads):
            B = Bt[hl]
            if main:
                B3 = B.rearrange("p (c l) -> p c l", l=64)
                k.op(ACT, lambda e: e.activation(out=KP[:, hl, :], in_=P1[hl][:], func=AF.Exp, bias=LNOML(h)), reads=[b_P1[hl], b_der], writes=[b_KP])
                k.op(ACT, lambda e: e.activation(out=kppT[hl][:], in_=P3[hl][:], func=AF.Exp, bias=LNOML(h)), reads=[b_P3[hl], b_der], writes=[b_kppT[hl]])
                k.op(ACT, lambda e: e.activation(out=decs[:, h, :], in_=B3[:, :, 63], func=AF.Exp), reads=[b_Bt[hl]], writes=[b_decs[h]])
            else:
                k.op(ACT, lambda e: e.activation(out=kppT[hl][:], in_=P1[hl][:], func=AF.Exp, bias=bia[:, hl:hl + 1]), reads=[b_P1[hl], b_bia[hl]], writes=[b_kppT[hl]])
                k.op(ACT, lambda e: e.activation(out=decs[:, h, 0:1], in_=B[:, 511:512], func=AF.Exp), reads=[b_Bt[hl]], writes=[b_decs[h]])

        if main:
            for hp in range(0, 4, 2):
                st = []
                for hl in (hp, hp + 1):
                    bi = fm_group(sl_q, hl, HT, [b_HT])
                    E, bE = ta()
                    st.append((hl, heads[hl], bi, E, bE))
                    k.op(ACT, lambda e: e.activation(out=E[:], in_=MMt[bi][:], func=AF.Exp, scale=-1.0), reads=[b_MM[bi]], writes=[bE])
                for hl, h, bi, E, bE in st:
                    k.op(ACT, lambda e: e.activation(out=E[:], in_=E[:], func=AF.Ln, bias=1.0), reads=[bE], writes=[bE])
                    k.op(POOL, lambda e: e.tensor_tensor(out=E[:], in0=Bt[hl][:], in1=E[:], op=ALU.subtract), reads=[bE, b_Bt[hl]], writes=[bE])
                for hl, h, bi, E, bE in st:
                    k.op(ACT, lambda e: e.activation(out=E[:], in_=E[:], func=AF.Exp), reads=[bE], writes=[bE])
                for hl, h, bi, E, bE in st:
                    k.op(DVE, lambda e: e.tensor_tensor(out=QP[:, hl, :], in0=MMt[bi][:], in1=E[:], op=ALU.mult), reads=[b_MM[bi], bE], writes=[b_QP])
        for hl, h in enumerate(heads):
            def ftp(e):
                ins = None
                for s in range(4):
                    ins = e.transpose(TPt[:, s, :], kppT[hl][:, s * 128:(s + 1) * 128], identb[:])
                return ins
            k.op(PE, ftp, reads=[b_kppT[hl], b_identb], writes=[b_TP])
            k.op(DVE, lambda e: e.tensor_copy(out=KPP[:, :, hl * 128:(hl + 1) * 128], in_=TPt[:, 0:4, :]), reads=[b_TP], writes=[b_KPP])
        if not main:
            def fst(e):
                ins = None
                for hl in range(4):
                    for s in range(4):
                        ins = e.matmul(STt2[0][:, hl, :], lhsT=KPP[:, s, hl * 128:(hl + 1) * 128],
                                       rhs=VI[:, s, hl * 128:(hl + 1) * 128], start=(s == 0), stop=(s == 3))
                return ins
            k.op(PE, fst, reads=[b_KPP, b_VI], writes=[b_ST2[0]])
            for hl, h in enumerate(heads):
                k.op(DVE, lambda e: e.scalar_tensor_tensor(out=S32[:, h, :], in0=S32[:, h, :], scalar=decs[:, h, 0:1],
                                                           in1=STt2[0][:, hl, :], op0=ALU.mult, op1=ALU.add),
                     reads=[b_S32[h], b_decs[h], b_ST2[0]], writes=[b_S32[h]])
                if last_pre:
                    k.op(POOL, lambda e: e.tensor_copy(out=SBF[:, h, 0, :], in_=S32[:, h, :]), reads=[b_S32[h]], writes=[b_SBF[h]])
            return

    def hgrn_pass1(gi, gt):
        heads = [4 * gi + i for i in range(4)]
        for j in range(8):
            s, pb = j // 2, (j % 2) * 64
            sb_i = j % 2

            def fst(e):
                ins = None
                for hl in range(4):
                    ins = e.matmul(STt2[sb_i][:, hl, :], lhsT=KPP[pb:pb + 64, s, hl * 128:(hl + 1) * 128],
                                   rhs=VI[pb:pb + 64, s, hl * 128:(hl + 1) * 128], start=True, stop=True)
                return ins
            k.op(PE, fst, reads=[b_KPP, b_VI], writes=[b_ST2[sb_i]])
            for hl, h in enumerate(heads):
                k.op(DVE, lambda e: e.scalar_tensor_tensor(out=S32[:, h, :], in0=S32[:, h, :], scalar=decs[:, h, j:j + 1],
                                                           in1=STt2[sb_i][:, hl, :], op0=ALU.mult, op1=ALU.add),
                     reads=[b_S32[h], b_decs[h], b_ST2[sb_i]], writes=[b_S32[h]])
                so = sbf_slot(gt, j + 1)
                if hl % 2 == 0:
                    k.op(ACT, lambda e: e.activation(out=SBF[:, h, so, :], in_=S32[:, h, :], func=AF.Copy), reads=[b_S32[h]], writes=[b_SBF[h]])
                else:
                    k.op(POOL, lambda e: e.tensor_copy(out=SBF[:, h, so, :], in_=S32[:, h, :]), reads=[b_S32[h]], writes=[b_SBF[h]])
            yield

    def hgrn_pass2(gi, gt):
        heads = [4 * gi + i for i in range(4)]

        def sc(hl):
            ci = hl % 2
            SCv = STt2[ci].rearrange("p q v -> p (q v)")

            def fsc(e):
                ins = None
                for j in range(8):
                    pb = (j % 2) * 64
                    ins = e.matmul(SCv[pb:pb + 64, (j // 2) * 64:(j // 2 + 1) * 64], lhsT=KP[:, hl, j * 64:(j + 1) * 64],
                                   rhs=QP[:, hl, j * 64:(j + 1) * 64], start=True, stop=True)
                return ins
            k.op(PE, fsc, reads=[b_KP, b_QP], writes=[b_ST2[ci]])
            k.op(DVE, lambda e: e.tensor_tensor(out=scm[ci][:], in0=SCv[:, 0:256], in1=mask_sc, op=ALU.mult),
                 reads=[b_ST2[ci], b_cst], writes=[b_scm[ci]])

        sc(0)
        sc(1)
        yield
        for hl, h in enumerate(heads):
            ci = hl % 2
            oi = hl % 2

            def fo(e):
                ins = None
                for j in range(8):
                    s, pb = j // 2, (j % 2) * 64
                    si = sbf_slot(gt, j)
                    e.matmul(OTt[oi][:, j * 64:(j + 1) * 64], lhsT=VI[pb:pb + 64, s, hl * 128:(hl + 1) * 128],
                             rhs=scm[ci][pb:pb + 64, s * 64:(s + 1) * 64], start=True, stop=False)
                    ins = e.matmul(OTt[oi][:, j * 64:(j + 1) * 64], lhsT=SBF[:, h, si, :], rhs=QP[:, hl, j * 64:(j + 1) * 64],
                                   start=False, stop=True)
                return ins
            k.op(PE, fo, reads=[b_VI, b_scm[ci], b_SBF[h], b_QP], writes=[b_OT[oi]])
            OC, bOC = P1[hl], b_P1[hl]
            k.op(DVE, lambda e: e.tensor_copy(out=OC[:], in_=OTt[oi][:]), reads=[b_OT[oi]], writes=[bOC])
            SQ, bSQ = P2[hl], b_P2[hl]
            k.op(ACT, lambda e: e.activation(out=SQ[:], in_=OC[:], func=AF.Square), reads=[bOC], writes=[bSQ])
            if hl + 2 < 4:
                sc(hl + 2)
            yield
        for hl, h in enumerate(heads):
            OC, bOC = P1[hl], b_P1[hl]
            SQ, bSQ = P2[hl], b_P2[hl]
            bi = mm_bank()
            k.op(PE, lambda e: e.matmul(MMt[bi][:], lhsT=onesf[:], rhs=SQ[:], start=True, stop=True), reads=[b_onesf, bSQ], writes=[b_MM[bi]])
            k.op(ACT, lambda e: e.activation(out=SQ[:], in_=MMt[bi][:], func=AF.Ln, scale=1.0 / 128, bias=EPS), reads=[b_MM[bi]], writes=[bSQ])
            k.op(ACT, lambda e: e.activation(out=SQ[:], in_=SQ[:], func=AF.Exp, scale=-0.5), reads=[bSQ], writes=[bSQ])
            k.op(DVE, lambda e: e.scalar_tensor_tensor(out=SQ[:], in0=OC[:], scalar=cols[:, 24 + h:25 + h], in1=SQ[:],
                                                       op0=ALU.mult, op1=ALU.mult), reads=[bOC, b_cols, bSQ], writes=[bSQ])
            k.op(POOL, lambda e: e.tensor_tensor(out=R5[:, h, :], in0=SQ[:], in1=R5[:, h, :], op=ALU.mult), reads=[bSQ, b_R5], writes=[b_R5])
            if hl < 3:
                yield

    def interleave(gen, fillers):
        fillers = list(fillers)
        for _ in gen:
            if fillers:
                fillers.pop(0)()
        for f in fillers:
            f()

    def out_proj(loader, nk, lhs, lbufs, gain_off, final, t):
        nkh = nk // 8 if nk == 8 else 2
        kper = 8 if nk == 8 else 11
        pend = []
        ssq_ = [stat_cols(2) for _ in range(4)]
        ssq = [a for a, _ in ssq_]
        bssq = [b for _, b in ssq_]
        s_outer = (nk == 8)
        slots_all = {}
        if s_outer:
            for nh in range(2):
                slots_all[nh] = [loader(kh, nh) for kh in range(nkh)]
            order = [(s, nh) for s in range(4) for nh in range(2)]
        else:
            order = [(s, nh) for nh in range(2) for s in range(4)]
        for (s, nh) in order:
            if nh not in slots_all:
                slots_all[nh] = [loader(kh, nh) for kh in range(nkh)]
            slots = slots_all[nh]
            bi = (2 * s + nh) % 4 if s_outer else s
            if bi == 3:
                ptile, pbuf = STt.rearrange("p q v -> p (q v)"), b_ST
            else:
                ptile, pbuf = MMt[bi], [b_MM[bi]]

            def f(e):
                ins = None
                n = 0
                for kh in range(nkh):
                    v = slot_view(slots[kh], kper)
                    for kc in range(kper):
                        ins = e.matmul(ptile[:], lhsT=lhs[:, kh * kper + kc, s * 128:(s + 1) * 128], rhs=v[:, kc, :],
                                       start=(n == 0), stop=(n == nkh * kper - 1))
                        n += 1
                return ins
            k.op(PE, f, reads=[b_ring[sl] for sl in slots] + lbufs, writes=pbuf)
            k.op(ACT, lambda e: e.activation(out=junk[:, 0:512], in_=ptile[:], func=AF.Square, accum_out=ssq[s][:, nh:nh + 1]),
                 reads=pbuf, writes=[b_junk, bssq[s]])
            park = BT[s // 2][:, (s % 2) * 512:(s % 2 + 1) * 512]
            if nh == 0:
                k.op(ACT, lambda e: e.activation(out=park, in_=ptile[:], func=AF.Copy), reads=pbuf, writes=[b_BT[s // 2]])
                continue
            tot, btot = stat_cols(1)
            k.op(POOL, lambda e: e.tensor_tensor(out=tot, in0=ssq[s][:, 0:1], in1=ssq[s][:, 1:2], op=ALU.add),
                 reads=[bssq[s]], writes=[btot])
            r, br = rstd_small(tot, btot, 1, 1.0 / D)
            tA, btA = ta()
            k.op(DVE, lambda e: e.scalar_tensor_tensor(out=tA[:], in0=ptile[:], scalar=r, in1=rows[:, gain_off + 512:gain_off + 1024],
                                                       op0=ALU.mult, op1=ALU.mult), reads=pbuf + [br, b_rows], writes=[btA])
            tB, btB = ta()
            k.op(DVE, lambda e: e.scalar_tensor_tensor(out=tB[:], in0=park, scalar=r, in1=rows[:, gain_off:gain_off + 512],
                                                       op0=ALU.mult, op1=ALU.mult), reads=[b_BT[s // 2], br, b_rows], writes=[btB])
            if final:
                k.op(POOL, lambda e: e.tensor_tensor(out=Y[:, 512:1024], in0=X[s][:, 512:1024], in1=tA[:], op=ALU.add),
                     reads=[btA, b_X[s]], writes=[b_Y])
                k.op(POOL, lambda e: e.tensor_tensor(out=Y[:, 0:512], in0=X[s][:, 0:512], in1=tB[:], op=ALU.add),
                     reads=[btB, b_X[s]], writes=[b_Y])
                k.dma(SP, d_y, y_out[t * T + s * 128:t * T + (s + 1) * 128, :], Y, reads=[b_Y])
                if t + 1 < ntiles_main:
                    load_x(x_main, t + 1, s)
                    if len(pend) == 2:
                        norm_b(pend.pop(0))
                    pend.append(norm_a(s))
            else:
                k.op(POOL, lambda e: e.tensor_tensor(out=X[s][:, 512:1024], in0=X[s][:, 512:1024], in1=tA[:], op=ALU.add),
                     reads=[btA, b_X[s]], writes=[b_X[s]])
                k.op(POOL, lambda e: e.tensor_tensor(out=X[s][:, 0:512], in0=X[s][:, 0:512], in1=tB[:], op=ALU.add),
                     reads=[btB, b_X[s]], writes=[b_X[s]])
                if len(pend) == 2:
                    norm_b(pend.pop(0))
                pend.append(norm_a(s))
        while pend:
            norm_b(pend.pop(0))

    gt = 0
    for t in range(ntiles_pre):
        for s in range(4):
            load_x(x_prev, t, s)
        for s in range(4):
            norm_to_hT(s)
        for gi in range(2):
            hgrn_group(gi, 0, False, last_pre=(t == ntiles_pre - 1))
        conv_run(n_phase0 + (t + 1) * (-(-(nconv - n_phase0) // max(ntiles_pre, 1))))
    conv_run(nconv, flush=True)

    for s in range(4):
        load_x(x_main, 0, s)
    for t in range(ntiles_main):
        if t == 0:
            for s in range(4):
                norm_to_hT(s)
        if stop == 2:
            return finish()
        sl_u = [load_k8(wb_in, "in", c0) for c0 in (0, 512)]
        sl_v = [load_k8(wb_in, "in", c0) for c0 in (1024, 1536)]
        for s in range(4):
            gi_ = nxt("bt", 2)
            gv, bgv = BT[gi_], b_BT[gi_]
            sm, bsm = stat_cols(2)
            for hf in range(2):
                bi = tm_group(sl_v[hf], HT, s, [b_HT])
                k.op(ACT, lambda e: e.activation(out=gv[:, hf * 512:(hf + 1) * 512], in_=MMt[bi][:], func=AF.Gelu, accum_out=sm[:, hf:hf + 1]),
                     reads=[b_MM[bi]], writes=[bgv, bsm])
            nm0, bnm0 = stat_cols(1)
            nm_, bnm = stat_cols(1)
            k.op(POOL, lambda e: e.tensor_tensor(out=nm0, in0=sm[:, 0:1], in1=sm[:, 1:2], op=ALU.add), reads=[bsm], writes=[bnm0])
            k.op(POOL, lambda e: e.tensor_scalar(out=nm_, in0=nm0, scalar1=-1.0 / D, scalar2=1.0, op0=ALU.mult, op1=ALU.mult),
                 reads=[bnm0], writes=[bnm])
            vs, bvs = stat_cols(1)
            k.op(ACT, lambda e: e.activation(out=junk[:], in_=gv[:], func=AF.Square, bias=nm_, accum_out=vs),
                 reads=[bgv, bnm], writes=[b_junk, bvs])
            r, br = rstd_small(vs, bvs, 1, 1.0 / D)
            k.op(DVE, lambda e: e.tensor_scalar(out=VLN[:, s, :], in0=gv[:], scalar1=nm_, scalar2=r, op0=ALU.add, op1=ALU.mult),
                 reads=[bgv, bnm, br], writes=[b_VLN])
        for c in range(8):
            bi = fm_group(sl_u[c // 4], c % 4, HT, [b_HT])
            k.op(ACT, lambda e: e.activation(out=U[:, c, :], in_=MMt[bi][:], func=AF.Gelu), reads=[b_MM[bi]], writes=[b_U])
        for g in range(8):
            bi = mm_bank()

            def fs(e):
                ins = None
                for n in range(4):
                    ins = e.matmul(MMt[bi][:, n * 128:(n + 1) * 128], lhsT=VLN[:, n, g * 128:(g + 1) * 128], rhs=wspb[:, g, :], start=True, stop=True)
                return ins
            k.op(PE, fs, reads=[b_VLN, b_wspb], writes=[b_MM[bi]])
            tA, btA = ta()
            k.op(DVE, lambda e: e.scalar_tensor_tensor(out=tA.rearrange("p (n t) -> p n t", t=128), in0=MMt[bi].rearrange("p (n t) -> p n t", t=128),
                                                       scalar=cols[:, 16 + g:17 + g], in1=cg[:, g:g + 1, :].to_broadcast([128, 4, 128]),
                                                       op0=ALU.mult, op1=ALU.add), reads=[b_MM[bi], b_cols, b_cg], writes=[btA])
            k.op(POOL, lambda e: e.tensor_tensor(out=U[:, g, :], in0=tA[:], in1=U[:, g, :], op=ALU.mult), reads=[btA, b_U], writes=[b_U])
        if stop == 3:
            return finish()
        hgrn_group(0, gt, True)
        sl_ga = [load_k8(wb_in, "in", c0) for c0 in (6144, 6656)]

        def ga_item(c):
            bi = fm_group(sl_ga[c // 4], c % 4, HT, [b_HT], wide=False)
            k.op(ACT, lambda e: e.activation(out=R4[:, c, :], in_=MMt[bi][:], func=AF.Sigmoid), reads=[b_MM[bi]], writes=[b_R4])
        interleave(hgrn_pass1(0, gt), [(lambda c=c: ga_item(c)) for c in range(8)])
        sl_pa = [load_k8(wb_pa, "pa", c0) for c0 in (0, 512)]

        def pa_item(c):
            bi = fm_group(sl_pa[c // 4], c % 4, U, [b_U], wide=False)
            k.op(DVE, lambda e: e.tensor_tensor(out=R4[:, c, :], in0=MMt[bi][:], in1=R4[:, c, :], op=ALU.mult), reads=[b_MM[bi], b_R4], writes=[b_R4])
        interleave(hgrn_pass2(0, gt), [(lambda c=c: pa_item(c)) for c in range(8)])
        hgrn_group(1, gt, True)
        sl_gb = [load_k8(wb_in, "in", c0) for c0 in (7168, 7680)]

        def gb_item(c):
            bi = fm_group(sl_gb[c // 4], c % 4, HT, [b_HT], wide=False)
            k.op(ACT, lambda e: e.activation(out=GB[:, c, :], in_=MMt[bi][:], func=AF.Sigmoid), reads=[b_MM[bi]], writes=[b_U])
        interleave(hgrn_pass1(1, gt), [(lambda c=c: gb_item(c)) for c in range(8)])
        interleave(hgrn_pass2(1, gt), [])
        gt += 1
        sl_pb = [load_k8(wb_pb, "pb", c0) for c0 in (0, 512)]
        for c in range(8):
            bi = fm_group(sl_pb[c // 4], c % 4, R5, [b_R5])
            tA, btA = ta()
            k.op(DVE, lambda e: e.tensor_tensor(out=tA[:], in0=MMt[bi][:], in1=GB[:, c, :], op=ALU.mult), reads=[b_MM[bi], b_U], writes=[btA])
            k.op(POOL, lambda e: e.tensor_tensor(out=R4[:, c, :], in0=tA[:], in1=R4[:, c, :], op=ALU.add), reads=[btA, b_R4], writes=[b_R4])
        if stop == 6:
            return finish()
        out_proj(lambda kh, nh: load_k8(wb_out, "out", nh * 512), 8, R4, [b_R4], 0, False, t)
        if stop == 7:
            return finish()
        for blk in range(11):
            sl = load_up(blk)
            for jj in range(2):
                bg = fm_group(sl, jj, HT, [b_HT])
                bu = fm_group(sl, 2 + jj, HT, [b_HT])
                tA, btA = ta()
                k.op(ACT, lambda e: e.activation(out=tA[:], in_=MMt[bg][:], func=AF.Silu), reads=[b_MM[bg]], writes=[btA])
                k.op(DVE, lambda e: e.tensor_tensor(out=FACT[:, 2 * blk + jj, :], in0=MMt[bu][:], in1=tA[:], op=ALU.mult),
                     reads=[b_MM[bu], btA], writes=[b_FACT])
        out_proj(load_dn, 22, FACT, [b_FACT], D, True, t)

    return finish()


def _unused():
    for s in range(4):
        if d_xs[s].cnt:
            nc.sync.wait_ge(d_xs[s].sem, d_xs[s].cnt)
    nc.all_engine_barrier()
    for sm in k.sems:
        nc.gpsimd.sem_clear(sm)
    return nc


def _host_inputs(inp):
    f = lambda a: np.ascontiguousarray(np.asarray(a, dtype=np.float32))
    x = f(inp["x"])
    col = lambda v: f(v).reshape(8, 128).T
    cols = np.concatenate([col(inp["pre_mix_gain"][0]), col(inp["pre_ffn_gain"][0]), col(inp["sgu_norm_gain"][0]),
                           col(inp["hgrn_norm_gain"][0]), col(inp["lb_logits"][0]), col(inp["lb_logits"][1])], axis=1)
    rows = np.concatenate([np.broadcast_to(f(inp["post_mix_gain"][0])[None, :], (128, D)),
                           np.broadcast_to(f(inp["post_ffn_gain"][0])[None, :], (128, D))], axis=1)
    wspT = f(inp["w_spatial"][0]).transpose(2, 0, 1).reshape(128, 1024)
    ident = np.eye(128, dtype=np.float32)
    p = np.arange(128)
    tt = np.arange(64)
    msc = (p[:, None] % 64 <= tt[None, :]).astype(np.float32)
    mask_sc = np.tile(msc, (1, 4))
    mask_scan = np.ones((128, 512), np.float32)
    mask_scan[:, 0::64] = 0.0
    mask_sguT = (p[:, None] // 64 <= p[None, :] // 64).astype(np.float32)
    consts = np.concatenate([ident, mask_sc, mask_scan, mask_sguT], axis=1)
    sguA = np.stack([f(inp["sgu_norm_bias"][0]), np.ones(D, np.float32)], axis=0)
    bsp = f(inp["b_spatial"][0]).reshape(1, D)
    shared = {
        "w_in": f(inp["w_in"][0]), "w_pa": f(inp["w_proj_sgu"][0]), "w_pb": f(inp["w_proj_hgrn"][0]),
        "w_out": f(inp["w_out"][0]), "w_up": f(inp["w_ffn_up"][0]), "w_dn": f(inp["w_ffn_down"][0]),
        "cols": f(cols), "rows_bc": f(rows), "wspT": f(wspT), "consts": f(consts), "sguA": f(sguA), "bsp": bsp,
    }
    maps = []
    zeros = np.zeros((NTOK, D), np.float32)
    for c in range(8):
        b, half = c // 2, c % 2
        m = dict(shared)
        m["x_main"] = np.ascontiguousarray(x[b, half * NTOK:(half + 1) * NTOK])
        m["x_prev"] = np.ascontiguousarray(x[b, 0:NTOK]) if half == 1 else zeros
        maps.append(m)
    return maps


def kernel(**inputs):
    maps = _host_inputs(inputs)
    nc = build_nc()
    res = run_bass_kernel_spmd(nc, maps, core_ids=list(range(8)))
    out = np.empty((4, 8192, D), np.float32)
    for c in range(8):
        b, half = c // 2, c % 2
        out[b, half * NTOK:(half + 1) * NTOK] = np.asarray(res.results[c]["y_out"], dtype=np.float32)
    return out
```

```python
import numpy as np
import concourse.bass as bass
import concourse.mybir as mybir
from concourse.bass_utils import run_bass_kernel_spmd

F32 = mybir.dt.float32
BF16 = mybir.dt.bfloat16
AF = mybir.ActivationFunctionType
ALU = mybir.AluOpType

D = 1024
NTOK = 4096
T = 512
NT = NTOK // T
FH = 2816
EPS = 1e-6
NCOL = 48


class Eng:
    def __init__(self, nc, name, h, kind):
        self.h = h
        self.kind = kind
        self.sem = nc.alloc_semaphore("se_" + name)
        self.cnt = 0
        self.waited = {}


class DSem:
    def __init__(self, nc, name):
        self.sem = nc.alloc_semaphore("sd_" + name)
        self.cnt = 0


class Buf:
    def __init__(self, name):
        self.name = name
        self.wr = {}
        self.rd = {}
        self.ov = []


def overlap(*bufs):
    for a in bufs:
        for b in bufs:
            if a is not b and b not in a.ov:
                a.ov.append(b)


class K:
    def __init__(self, nc):
        self.nc = nc
        self.pe = Eng(nc, "pe", nc.tensor, "pe")
        self.act = Eng(nc, "act", nc.scalar, "c")
        self.dve = Eng(nc, "dve", nc.vector, "c")
        self.pool = Eng(nc, "pool", nc.gpsimd, "c")
        self.sp = Eng(nc, "sp", nc.sync, "q")
        self.sems = [e.sem for e in (self.pe, self.act, self.dve, self.pool, self.sp)]
        self.semobj = {}

    def dsem(self, name):
        d = DSem(self.nc, name)
        self.sems.append(d.sem)
        return d

    def _deps(self, eng, reads, writes):
        need = {}

        def add(dct, same_ok):
            for key, (sem, val, src) in dct.items():
                if src is eng:
                    if eng.kind == "pe":
                        continue
                if need.get(key, (None, 0))[1] < val:
                    need[key] = (sem, val)

        for b in reads:
            for o in [b] + b.ov:
                add(o.wr, True)
        for b in writes:
            for o in [b] + b.ov:
                add(o.wr, False)
                add(o.rd, False)
        for key, (sem, val) in need.items():
            if eng.waited.get(key, 0) < val:
                eng.h.wait_ge(sem, val)
                eng.waited[key] = val

    def _mark(self, tok, reads, writes):
        key = id(tok[0])
        for b in writes:
            b.wr[key] = tok
        for b in reads:
            b.rd[key] = tok

    def op(self, eng, fn, reads=(), writes=()):
        self._deps(eng, reads, writes)
        ins = fn(eng.h)
        eng.cnt += 1
        ins.then_inc(eng.sem, 1)
        self._mark((eng.sem, eng.cnt, eng), reads, writes)

    def dma(self, q, ds, out, in_, reads=(), writes=()):
        self._deps(q, reads, writes)
        ins = q.h.dma_start(out=out, in_=in_)
        ds.cnt += 16
        ins.then_inc(ds.sem, 16)
        self._mark((ds.sem, ds.cnt, None), reads, writes)


def build_nc(ntiles_main=NT, ntiles_pre=NT, stop=99):
    nc = bass.Bass("TRN2", target_bir_lowering=False)

    def din(name, shape, dt=F32):
        return nc.dram_tensor(name, shape, dt, kind="ExternalInput").ap()

    x_main = din("x_main", [NTOK, D])
    x_prev = din("x_prev", [NTOK, D])
    w_in = din("w_in", [D, 8192])
    w_pa = din("w_pa", [D, D])
    w_pb = din("w_pb", [D, D])
    w_out = din("w_out", [D, D])
    w_up = din("w_up", [D, 2 * FH])
    w_dn = din("w_dn", [FH, D])
    cols_d = din("cols", [128, NCOL])
    rows_d = din("rows_bc", [128, 2 * D])
    wspT_d = din("wspT", [128, 8 * 128])
    cst_d = din("consts", [128, 128 + 256 + 512 + 128])
    sguA_d = din("sguA", [2, D])
    bsp_d = din("bsp", [1, D])
    y_out = nc.dram_tensor("y_out", [NTOK, D], F32, kind="ExternalOutput").ap()

    def dscr(name, shape):
        return nc.dram_tensor(name, shape, BF16, kind="Internal").ap()

    wb_in = dscr("wb_in", [D, 8192])
    wb_pa = dscr("wb_pa", [D, D])
    wb_pb = dscr("wb_pb", [D, D])
    wb_out = dscr("wb_out", [D, D])
    wb_up = dscr("wb_up", [D, 2 * FH])
    wb_dn = dscr("wb_dn", [FH, D])

    k = K(nc)
    PE, ACT, DVE, POOL, SP = k.pe, k.act, k.dve, k.pool, k.sp

    def sb(name, shape, dt):
        return nc.alloc_sbuf_tensor("s_" + name, shape, dt)

    cols = sb("cols", [128, NCOL], F32)
    der = sb("der", [128, 64], F32)
    rows = sb("rows", [128, 2 * D], F32)
    cst = sb("cst", [128, 1024], F32)
    identb = sb("identb", [128, 128], BF16)
    onesf = sb("onesf", [128, 128], F32)
    wspb = sb("wspb", [128, 8, 128], BF16)
    cg = sb("cg", [128, 8, 128], F32)
    X = [sb(f"x{s}", [128, D], F32) for s in range(4)]
    wspf = X[1].rearrange("p (g t) -> p g t", t=128)
    sguA = X[2][0:2, :]
    sguB = X[3][0:2, :]
    HT = sb("hT", [128, 8, T], BF16)
    RR = sb("RR", [128, 12288], BF16)
    R4 = sb("R4", [128, 8, T], BF16)
    R5 = sb("R5", [128, 8, T], BF16)
    S32 = sb("S32", [128, 8, 128], F32)
    SBF = sb("SBF", [128, 8, 9, 128], BF16)
    decs = sb("decs", [128, 8, 8], F32)
    scm = [sb(f"scm{i}", [128, 256], BF16) for i in range(2)]
    Bt = [sb(f"Bt{i}", [128, T], F32) for i in range(4)]
    TA = [sb(f"ta{i}", [128, T], F32) for i in range(3)]
    P1all = sb("P1all", [128, 4, T], F32)
    P1 = [P1all[:, i, :] for i in range(4)]
    Y = P1all[:, 0:2, :].rearrange("p a n -> p (a n)")
    P2all = sb("P2all", [128, 4, T], F32)
    P3all = sb("P3all", [128, 4, T], F32)
    P2 = [P2all[:, i, :] for i in range(4)]
    P3 = [P3all[:, i, :] for i in range(4)]
    Xn = [P2all[:, 0:2, :].rearrange("p a n -> p (a n)"), P2all[:, 2:4, :].rearrange("p a n -> p (a n)"),
          P3all[:, 0:2, :].rearrange("p a n -> p (a n)"), P3all[:, 2:4, :].rearrange("p a n -> p (a n)")]
    bia = sb("bia", [128, 4], F32)
    kppT = [sb(f"kppT{i}", [128, T], BF16) for i in range(4)]
    junk = sb("junk", [128, D], BF16)
    hb = [sb(f"hb{i}", [128, D], BF16) for i in range(2)]
    BT = [sb(f"BT{i}", [128, D], F32) for i in range(2)]
    stat = sb("stat", [128, 64], F32)
    RING = sb("RING", [128, 4 * 5632], BF16)

    U = RR[:, 0:4096].rearrange("p (k n) -> p k n", n=T)
    VLN = RR[:, 4096:8192].rearrange("p (s n) -> p s n", n=D)
    QP = RR[:, 4096:6144].rearrange("p (k n) -> p k n", n=T)
    KP = RR[:, 6144:8192].rearrange("p (k n) -> p k n", n=T)
    KPP = RR[:, 8192:10240].rearrange("p (s n) -> p s n", n=T)
    VI = RR[:, 10240:12288].rearrange("p (s n) -> p s n", n=T)
    GB = U
    FACT = RR[:, 0:11264].rearrange("p (k n) -> p k n", n=T)

    MMt = [nc.alloc_psum_tensor(f"mm{i}", [128, T], F32) for i in range(3)]
    STt2 = [nc.alloc_psum_tensor(f"st{i}", [128, 4, 128], F32) for i in range(2)]
    STt = STt2[0]
    TPt = nc.alloc_psum_tensor("tp", [128, 8, 128], BF16)
    OTt = [nc.alloc_psum_tensor(f"ot{i}", [128, T], F32) for i in range(2)]

    b_cols, b_der, b_rows, b_cst, b_identb, b_onesf = (Buf(n) for n in ("cols", "der", "rows", "cst", "identb", "onesf"))
    b_wspf, b_wspb, b_cg, b_sguA, b_sguB = (Buf(n) for n in ("wspf", "wspb", "cg", "sguA", "sguB"))
    b_X = [Buf(f"x{s}") for s in range(4)]
    overlap(b_X[1], b_wspf)
    overlap(b_X[2], b_sguA)
    overlap(b_X[3], b_sguB)
    b_HT = Buf("hT")
    b_U, b_VLN, b_QP, b_KP, b_KPP, b_VI, b_FACT = (Buf(n) for n in ("U", "VLN", "QP", "KP", "KPP", "VI", "FACT"))
    overlap(b_U, b_FACT)
    overlap(b_VLN, b_QP, b_FACT)
    overlap(b_VLN, b_KP, b_FACT)
    overlap(b_KPP, b_FACT)
    overlap(b_VI, b_FACT)
    b_R4, b_R5 = Buf("R4"), Buf("R5")
    b_S32 = [Buf(f"S32_{h}") for h in range(8)]
    b_SBF = [Buf(f"SBF_{h}") for h in range(8)]
    b_decs = [Buf(f"decs{h}") for h in range(8)]
    b_scm = [Buf("scm0"), Buf("scm1")]
    b_Bt = [Buf(f"Bt{i}") for i in range(4)]
    b_TA = [Buf(f"ta{i}") for i in range(3)]
    b_P1 = [Buf(f"P1_{i}") for i in range(4)]
    b_P2 = [Buf(f"P2_{i}") for i in range(4)]
    b_P3 = [Buf(f"P3_{i}") for i in range(4)]
    b_Xn = [Buf(f"xn{i}") for i in range(4)]
    overlap(b_Xn[0], b_P2[0])
    overlap(b_Xn[0], b_P2[1])
    overlap(b_Xn[1], b_P2[2])
    overlap(b_Xn[1], b_P2[3])
    overlap(b_Xn[2], b_P3[0])
    overlap(b_Xn[2], b_P3[1])
    overlap(b_Xn[3], b_P3[2])
    overlap(b_Xn[3], b_P3[3])
    b_bia = [Buf(f"bia{i}") for i in range(4)]
    b_Y = Buf("Y")
    overlap(b_Y, b_P1[0])
    overlap(b_Y, b_P1[1])
    b_kppT = [Buf(f"kppT{i}") for i in range(4)]
    b_junk = Buf("junk")
    b_hb = [Buf("hb0"), Buf("hb1")]
    b_BT = [Buf("BT0"), Buf("BT1")]
    b_stat = Buf("stat")
    b_ring = [Buf(f"ring{i}") for i in range(4)]
    b_stb = [Buf("stb0"), Buf("stb1")]
    b_MM = [Buf(f"mm{i}") for i in range(3)]
    b_ST2 = [Buf(f"st{i}") for i in range(2)]
    b_ST = [b_ST2[0]]
    b_TP = Buf("tp")
    b_OT = [Buf("ot0"), Buf("ot1")]
    b_wb = {n: Buf("wb_" + n) for n in ("in", "pa", "pb", "out", "up", "dn")}

    d_const = k.dsem("const")
    d_ring = [k.dsem(f"ring{i}") for i in range(4)]
    d_stl = [k.dsem(f"stl{i}") for i in range(3)]
    d_conv = [k.dsem("conv0"), k.dsem("conv1")]
    d_xl = [k.dsem(f"xl{s}") for s in range(4)]
    d_xs = [k.dsem(f"xs{s}") for s in range(4)]
    d_y = k.dsem("y")

    def finish():
        for s in range(4):
            if d_xs[s].cnt:
                nc.sync.wait_ge(d_xs[s].sem, d_xs[s].cnt)
        if d_y.cnt:
            nc.sync.wait_ge(d_y.sem, d_y.cnt)
        nc.all_engine_barrier()
        for sm in k.sems:
            nc.gpsimd.sem_clear(sm)
        return nc

    for sm in k.sems:
        nc.gpsimd.sem_clear(sm)
    nc.all_engine_barrier()

    k.dma(SP, d_const, cols[:], cols_d[:, :], writes=[b_cols])
    k.dma(SP, d_const, rows[:], rows_d[:, :], writes=[b_rows])
    k.dma(SP, d_const, cst[:], cst_d[:, :], writes=[b_cst])
    k.dma(SP, d_const, X[1][:], wspT_d[:, :], writes=[b_wspf])
    k.dma(SP, d_const, sguA, sguA_d[:, :], writes=[b_sguA])
    k.dma(SP, d_const, sguB[1:2, :], bsp_d[:, :], writes=[b_sguB])
    for b_ in (b_cols, b_rows, b_cst, b_wspf, b_sguA, b_sguB):
        b_.wr[id(d_const.sem)] = (d_const.sem, d_const.cnt, None)
    ident_f = cst[:, 0:128]
    mask_sc = cst[:, 128:384]
    mask_scan = cst[:, 384:896]
    mask_sgu = cst[:, 896:1024]

    k.op(DVE, lambda e: e.tensor_copy(out=identb[:], in_=ident_f), reads=[b_cst], writes=[b_identb])
    k.op(POOL, lambda e: e.memset(onesf[:], 1.0), writes=[b_onesf])
    k.op(POOL, lambda e: e.memset(der[:, 40:48], -0.5), writes=[b_der])
    k.op(POOL, lambda e: e.memset(S32.rearrange("p h v -> p (h v)"), 0.0), writes=b_S32)
    k.op(POOL, lambda e: e.memset(SBF.rearrange("p h j v -> p (h j v)"), 0.0), writes=b_SBF)
    k.op(DVE, lambda e: e.tensor_tensor(out=der[:, 24:32], in0=cols[:, 40:48], in1=cols[:, 32:40], op=ALU.subtract),
         reads=[b_cols], writes=[b_der])
    k.op(ACT, lambda e: e.activation(out=der[:, 24:32], in_=der[:, 24:32], func=AF.Exp), reads=[b_der], writes=[b_der])
    k.op(DVE, lambda e: e.tensor_scalar(out=der[:, 32:40], in0=der[:, 24:32], scalar1=1.0, scalar2=None, op0=ALU.add),
         reads=[b_der], writes=[b_der])
    k.op(DVE, lambda e: e.reciprocal(out=der[:, 0:8], in_=der[:, 32:40]), reads=[b_der], writes=[b_der])
    k.op(DVE, lambda e: e.tensor_tensor(out=der[:, 8:16], in0=der[:, 24:32], in1=der[:, 0:8], op=ALU.mult),
         reads=[b_der], writes=[b_der])
    k.op(ACT, lambda e: e.activation(out=der[:, 16:24], in_=der[:, 8:16], func=AF.Ln), reads=[b_der], writes=[b_der])
    LB = lambda h: der[:, h:h + 1]
    LNOML = lambda h: der[:, 16 + h:17 + h]
    NEGH = der[:, 40:41]

    w3 = wspf
    ones512 = onesf[:, 0:1].to_broadcast([128, T])
    k.op(DVE, lambda e: e.tensor_tensor(out=w3, in0=w3, in1=mask_sgu.unsqueeze(1).to_broadcast([128, 8, 128]), op=ALU.mult),
         reads=[b_wspf, b_cst], writes=[b_wspf])
    k.op(POOL, lambda e: e.tensor_copy(out=wspb[:], in_=wspf), reads=[b_wspf], writes=[b_wspb])

    def f_wsum2(e):
        ins = None
        for g in range(8):
            dst = MMt[g // 4][0:1, (g % 4) * 128:(g % 4 + 1) * 128]
            ins = e.matmul(dst, lhsT=onesf[:, 0:1], rhs=wspf[:, g, :], start=True, stop=True)
        return ins
    k.op(PE, f_wsum2, reads=[b_onesf, b_wspf], writes=[b_MM[0], b_MM[1]])
    k.op(DVE, lambda e: e.tensor_copy(out=sguB[0:1, 0:512], in_=MMt[0][0:1, :]), reads=[b_MM[0]], writes=[b_sguB])
    k.op(DVE, lambda e: e.tensor_copy(out=sguB[0:1, 512:1024], in_=MMt[1][0:1, :]), reads=[b_MM[1]], writes=[b_sguB])

    def f_cg(e):
        ins = None
        for g in range(8):
            dst = MMt[g // 4][:, (g % 4) * 128:(g % 4 + 1) * 128]
            ins = e.matmul(dst, lhsT=sguA[:, g * 128:(g + 1) * 128], rhs=sguB[:, g * 128:(g + 1) * 128], start=True, stop=True)
        return ins
    k.op(PE, f_cg, reads=[b_sguA, b_sguB], writes=[b_MM[0], b_MM[1]])
    k.op(DVE, lambda e: e.tensor_copy(out=cg[:, 0:4, :].rearrange("p g t -> p (g t)"), in_=MMt[0][:]), reads=[b_MM[0]], writes=[b_cg])
    k.op(DVE, lambda e: e.tensor_copy(out=cg[:, 4:8, :].rearrange("p g t -> p (g t)"), in_=MMt[1][:]), reads=[b_MM[1]], writes=[b_cg])

    if stop == 0:
        return finish()
    stf = [R4.rearrange("p k n -> p (k n)").bitcast(F32), R5.rearrange("p k n -> p (k n)").bitcast(F32), RR[:, 0:4096].bitcast(F32)]
    stb = [RR[:, 4096:6144], RR[:, 6144:8192]]
    b_stf = [Buf(f"stf{i}") for i in range(3)]
    overlap(b_stf[0], b_R4)
    overlap(b_stf[1], b_R5)
    overlap(b_stf[2], b_U, b_FACT)
    overlap(b_stb[0], b_VLN, b_QP, b_FACT)
    overlap(b_stb[1], b_VLN, b_KP, b_FACT)
    conv = []

    def conv_win(c0):
        for kc in range(8):
            conv.append((w_in[kc * 128:(kc + 1) * 128, c0:c0 + 2048], wb_in[kc * 128:(kc + 1) * 128, c0:c0 + 2048], cols[:, kc:kc + 1], "in"))
    conv_win(2048)
    conv_win(4096)
    n_phase0 = len(conv)
    conv_win(0)
    conv_win(6144)
    for nm, src, dst in (("pa", w_pa, wb_pa), ("pb", w_pb, wb_pb), ("out", w_out, wb_out)):
        for kc in range(8):
            conv.append((src[kc * 128:(kc + 1) * 128, :], dst[kc * 128:(kc + 1) * 128, :], None, nm))
    for kc in range(8):
        for c0, cw in ((0, 2048), (2048, 2048), (4096, 1536)):
            conv.append((w_up[kc * 128:(kc + 1) * 128, c0:c0 + cw], wb_up[kc * 128:(kc + 1) * 128, c0:c0 + cw], cols[:, 8 + kc:9 + kc], "up"))
    for kc in range(22):
        conv.append((w_dn[kc * 128:(kc + 1) * 128, :], wb_dn[kc * 128:(kc + 1) * 128, :], None, "dn"))
    nconv = len(conv)
    cv = {"next": 0, "loaded": 0, "stored": 0}

    def conv_load(i):
        src = conv[i][0]
        w = src.shape[1]
        k.dma(SP, d_stl[i % 3], stf[i % 3][:, 0:w], src, writes=[b_stf[i % 3]])

    def conv_cast(i):
        src, dst, gc, nm = conv[i]
        w = src.shape[1]
        j = i % 2
        if i % 2 == 0:
            if gc is None:
                k.op(DVE, lambda e: e.tensor_copy(out=stb[j][:, 0:w], in_=stf[i % 3][:, 0:w]), reads=[b_stf[i % 3]], writes=[b_stb[j]])
            else:
                k.op(DVE, lambda e: e.tensor_scalar(out=stb[j][:, 0:w], in0=stf[i % 3][:, 0:w], scalar1=gc, scalar2=None, op0=ALU.mult),
                     reads=[b_stf[i % 3], b_cols], writes=[b_stb[j]])
        else:
            if gc is None:
                k.op(ACT, lambda e: e.activation(out=stb[j][:, 0:w], in_=stf[i % 3][:, 0:w], func=AF.Copy), reads=[b_stf[i % 3]], writes=[b_stb[j]])
            else:
                k.op(ACT, lambda e: e.activation(out=stb[j][:, 0:w], in_=stf[i % 3][:, 0:w], func=AF.Copy, scale=gc),
                     reads=[b_stf[i % 3], b_cols], writes=[b_stb[j]])

    def conv_store(i):
        src, dst, gc, nm = conv[i]
        w = src.shape[1]
        j = i % 2
        k.dma(ACT, d_conv[j], dst, stb[j][:, 0:w], reads=[b_stb[j]], writes=[b_wb[nm]])

    def conv_run(upto, flush=False):
        upto = min(upto, nconv)
        while cv["next"] < upto:
            i = cv["next"]
            while cv["loaded"] < min(i + 3, nconv):
                conv_load(cv["loaded"])
                cv["loaded"] += 1
            conv_cast(i)
            if i - 1 >= cv["stored"]:
                conv_store(cv["stored"])
                cv["stored"] += 1
            cv["next"] += 1
        if flush:
            while cv["stored"] < cv["next"]:
                conv_store(cv["stored"])
                cv["stored"] += 1

    conv_run(n_phase0, flush=True)

    if stop == 1:
        return finish()
    ring_ctr = [0]

    def ring_slot():
        i = ring_ctr[0] % 4
        ring_ctr[0] += 1
        return i

    def slot_view(i, kk):
        return RING[:, i * 5632:i * 5632 + kk * 512].rearrange("p (k n) -> p k n", n=512)

    def load_k8(wb, nm, c0):
        i = ring_slot()
        k.dma(SP, d_ring[i], slot_view(i, 8), wb[:, c0:c0 + 512].rearrange("(k p) n -> p k n", p=128),
              reads=[b_wb[nm]], writes=[b_ring[i]])
        return i

    def load_up(blk):
        i = ring_slot()
        v = slot_view(i, 8)
        j = 2 * blk
        k.dma(SP, d_ring[i], v[:, :, 0:256], wb_up[:, j * 128:j * 128 + 256].rearrange("(k p) n -> p k n", p=128),
              reads=[b_wb["up"]], writes=[b_ring[i]])
        k.dma(SP, d_ring[i], v[:, :, 256:512], wb_up[:, FH + j * 128:FH + j * 128 + 256].rearrange("(k p) n -> p k n", p=128),
              reads=[b_wb["up"]], writes=[b_ring[i]])
        return i

    def load_dn(kh, nh):
        i = ring_slot()
        k.dma(SP, d_ring[i], slot_view(i, 11), wb_dn[kh * 1408:(kh + 1) * 1408, nh * 512:(nh + 1) * 512].rearrange("(k p) n -> p k n", p=128),
              reads=[b_wb["dn"]], writes=[b_ring[i]])
        return i

    mm_ctr = [0, 0]
    MMt.extend([OTt[0], OTt[1], STt2[0].rearrange("p q v -> p (q v)"), STt2[1].rearrange("p q v -> p (q v)")])
    b_MM.extend([b_OT[0], b_OT[1], b_ST2[0], b_ST2[1]])

    def mm_bank(wide=False):
        if wide:
            i = mm_ctr[1] % 7
            mm_ctr[1] += 1
        else:
            i = mm_ctr[0] % 3
            mm_ctr[0] += 1
        return i

    ta_ctr = [0]

    def ta():
        i = ta_ctr[0] % 3
        ta_ctr[0] += 1
        return TA[i], b_TA[i]

    st_ctr = [0]
    misc = {"stat": 0, "hb": 0, "bt": 0, "kppT": 0, "sc": 0, "ot": 0}

    def nxt(name, n):
        v = misc[name] % n
        misc[name] += 1
        return v

    b_stats = [Buf(f"stat{i}") for i in range(32)]

    def stat_cols(n):
        i = misc["stat"] % 32
        misc["stat"] += 1
        return stat[:, 2 * i:2 * i + n], b_stats[i]

    def fm_group(slot, chunk, rhs, rbufs, wide=True):
        bi = mm_bank(wide)
        v = slot_view(slot, 8)

        def f(e):
            ins = None
            for kc in range(8):
                ins = e.matmul(MMt[bi][:], lhsT=v[:, kc, chunk * 128:(chunk + 1) * 128], rhs=rhs[:, kc, :], start=(kc == 0), stop=(kc == 7))
            return ins
        k.op(PE, f, reads=[b_ring[slot]] + rbufs, writes=[b_MM[bi]])
        return bi

    def tm_group(slot, lhs, s, rbufs, wide=True):
        bi = mm_bank(wide)
        v = slot_view(slot, 8)

        def f(e):
            ins = None
            for kc in range(8):
                ins = e.matmul(MMt[bi][:], lhsT=lhs[:, kc, s * 128:(s + 1) * 128], rhs=v[:, kc, :], start=(kc == 0), stop=(kc == 7))
            return ins
        k.op(PE, f, reads=[b_ring[slot]] + rbufs, writes=[b_MM[bi]])
        return bi

    def rstd_small(ss_ap, ss_buf, n, scale):
        m, bm = stat_cols(n)
        r, br = stat_cols(n)
        k.op(POOL, lambda e: e.tensor_scalar(out=m, in0=ss_ap, scalar1=scale, scalar2=EPS, op0=ALU.mult, op1=ALU.add),
             reads=[ss_buf], writes=[bm])
        k.op(POOL, lambda e: e.tensor_tensor(out=r, in0=m, in1=NEGH, op=ALU.pow),
             reads=[bm, b_der], writes=[br])
        return r, br

    def norm_to_hT(s):
        norm_b(norm_a(s))

    def norm_a(s, src=None, bsrc=None):
        if src is None:
            src, bsrc = X[s][:], b_X[s]
        ss, bss = stat_cols(1)
        k.op(ACT, lambda e: e.activation(out=junk[:], in_=src, func=AF.Square, accum_out=ss),
             reads=[bsrc], writes=[b_junk, bss])
        r, br = rstd_small(ss, bss, 1, 1.0 / D)
        hi = nxt("hb", 2)
        k.op(DVE, lambda e: e.tensor_scalar(out=hb[hi][:], in0=src, scalar1=r, scalar2=None, op0=ALU.mult),
             reads=[bsrc, br], writes=[b_hb[hi]])
        return (s, hi)

    def norm_b(sh):
        s, hi = sh

        def f(e):
            ins = None
            for kc in range(8):
                ins = e.transpose(TPt[:, kc, :], hb[hi][:, kc * 128:(kc + 1) * 128], identb[:])
            return ins
        k.op(PE, f, reads=[b_hb[hi], b_identb], writes=[b_TP])
        k.op(DVE, lambda e: e.tensor_copy(out=HT[:, :, s * 128:(s + 1) * 128], in_=TPt[:]), reads=[b_TP], writes=[b_HT])

    def load_x(src, t, s):
        k.dma(SP, d_xl[s], X[s][:], src[t * T + s * 128:t * T + (s + 1) * 128, :], writes=[b_X[s]])

    def sbf_slot(gt, j):
        return (8 * gt + j) % 9

    def hgrn_group(gi, gt, main, last_pre=False):
        heads = [4 * gi + i for i in range(4)]
        sl_f = load_k8(wb_in, "in", 3072 + gi * 512)
        sl_i = load_k8(wb_in, "in", 4096 + gi * 512)
        if main:
            sl_g = load_k8(wb_in, "in", 5120 + gi * 512)
            sl_q = load_k8(wb_in, "in", 2048 + gi * 512)
        for hl, h in enumerate(heads):
            bi = fm_group(sl_f, hl, HT, [b_HT])
            z = MMt[bi]
            E, bE = ta()
            k.op(ACT, lambda e: e.activation(out=E[:], in_=z[:], func=AF.Exp, scale=-1.0), reads=[b_MM[bi]], writes=[bE])
            k.op(ACT, lambda e: e.activation(out=P2[hl][:], in_=E[:], func=AF.Ln, scale=LB(h), bias=1.0), reads=[bE, b_der], writes=[b_P2[hl]])
            k.op(ACT, lambda e: e.activation(out=P3[hl][:], in_=E[:], func=AF.Ln, scale=1.0, bias=1.0), reads=[bE], writes=[b_P3[hl]])
            k.op(DVE, lambda e: e.scalar_tensor_tensor(out=P1[hl][:], in0=z[:], scalar=-1.0, in1=P3[hl][:], op0=ALU.mult, op1=ALU.subtract),
                 reads=[b_MM[bi], b_P3[hl]], writes=[b_P1[hl]])
        if main:
            for hp in range(0, 4, 2):
                st = []
                for hl in (hp, hp + 1):
                    bi = fm_group(sl_g, hl, HT, [b_HT])
                    E, bE = ta()
                    st.append((hl, heads[hl], bi, E, bE))
                    k.op(ACT, lambda e: e.activation(out=E[:], in_=MMt[bi][:], func=AF.Exp, scale=-1.0), reads=[b_MM[bi]], writes=[bE])
                for hl, h, bi, E, bE in st:
                    k.op(ACT, lambda e: e.activation(out=E[:], in_=E[:], func=AF.Ln, bias=1.0), reads=[bE], writes=[bE])
                for hl, h, bi, E, bE in st:
                    k.op(ACT, lambda e: e.activation(out=E[:], in_=E[:], func=AF.Exp, scale=-1.0), reads=[bE], writes=[bE])
                for hl, h, bi, E, bE in st:
                    k.op(DVE, lambda e: e.tensor_tensor(out=R5[:, h, :], in0=MMt[bi][:], in1=E[:], op=ALU.mult), reads=[b_MM[bi], bE], writes=[b_R5])
        for s in range(4):
            bi = tm_group(sl_i, HT, s, [b_HT])
            k.op(DVE, lambda e: e.tensor_copy(out=VI[:, s, :], in_=MMt[bi][:]), reads=[b_MM[bi]], writes=[b_VI])
        for hl, h in enumerate(heads):
            k.op(POOL, lambda e: e.tensor_tensor(out=P2[hl][:], in0=P2[hl][:], in1=P3[hl][:], op=ALU.subtract),
                 reads=[b_P2[hl], b_P3[hl]], writes=[b_P2[hl]])
            B, bB = Bt[hl], b_Bt[hl]
            if main:
                k.op(DVE, lambda e: e.tensor_tensor_scan(out=B[:], data0=mask_scan, data1=P2[hl][:], initial=0.0, op0=ALU.mult, op1=ALU.add),
                     reads=[b_cst, b_P2[hl]], writes=[bB])
            else:
                k.op(DVE, lambda e: e.tensor_tensor_scan(out=B[:], data0=ones512, data1=P2[hl][:], initial=0.0, op0=ALU.mult, op1=ALU.add),
                     reads=[b_onesf, b_P2[hl]], writes=[bB])
            k.op(POOL, lambda e: e.tensor_tensor(out=P1[hl][:], in0=P1[hl][:], in1=B[:], op=ALU.subtract),
                 reads=[b_P1[hl], bB], writes=[b_P1[hl]])
            if main:
                B3 = B.rearrange("p (c l) -> p c l", l=64)
                k.op(POOL, lambda e: e.tensor_tensor(out=P3[hl].rearrange("p (c l) -> p c l", l=64), in0=P1[hl].rearrange("p (c l) -> p c l", l=64),
                                                     in1=B3[:, :, 63:64].to_broadcast([128, 8, 64]), op=ALU.add),
                     reads=[b_P1[hl], bB], writes=[b_P3[hl]])
            else:
                k.op(POOL, lambda e: e.tensor_tensor(out=bia[:, hl:hl + 1], in0=B[:, 511:512], in1=LNOML(h), op=ALU.add),
                     reads=[bB, b_der], writes=[b_bia[hl]])
        for hl, h in enumerate(heads):
            B = Bt[hl]
            if main:
                B3 = B.rearrange("p (c l) -> p c l", l=64)
                k.op(ACT, lambda e: e.activation(out=KP[:, hl, :], in_=P1[hl][:], func=AF.Exp, bias=LNOML(h)), reads=[b_P1[hl], b_der], writes=[b_KP])
                k.op(ACT, lambda e: e.activation(out=kppT[hl][:], in_=P3[hl][:], func=AF.Exp, bias=LNOML(h)), reads=[b_P3[hl], b_der], writes=[b_kppT[hl]])
                k.op(ACT, lambda e: e.activation(out=decs[:, h, :], in_=B3[:, :, 63], func=AF.Exp), reads=[b_Bt[hl]], writes=[b_decs[h]])
            else:
                k.op(ACT, lambda e: e.activation(out=kppT[hl][:], in_=P1[hl][:], func=AF.Exp, bias=bia[:, hl:hl + 1]), reads=[b_P1[hl], b_bia[hl]], writes=[b_kppT[hl]])
                k.op(ACT, lambda e: e.activation(out=decs[:, h, 0:1], in_=B[:, 511:512], func=AF.Exp), reads=[b_Bt[hl]], writes=[b_decs[h]])

        if main:
            for hp in range(0, 4, 2):
                st = []
                for hl in (hp, hp + 1):
                    bi = fm_group(sl_q, hl, HT, [b_HT])
                    E, bE = ta()
                    st.append((hl, heads[hl], bi, E, bE))
                    k.op(ACT, lambda e: e.activation(out=E[:], in_=MMt[bi][:], func=AF.Exp, scale=-1.0), reads=[b_MM[bi]], writes=[bE])
                for hl, h, bi, E, bE in st:
                    k.op(ACT, lambda e: e.activation(out=E[:], in_=E[:], func=AF.Ln, bias=1.0), reads=[bE], writes=[bE])
                    k.op(POOL, lambda e: e.tensor_tensor(out=E[:], in0=Bt[hl][:], in1=E[:], op=ALU.subtract), reads=[bE, b_Bt[hl]], writes=[bE])
                for hl, h, bi, E, bE in st:
                    k.op(ACT, lambda e: e.activation(out=E[:], in_=E[:], func=AF.Exp), reads=[bE], writes=[bE])
                for hl, h, bi, E, bE in st:
                    k.op(DVE, lambda e: e.tensor_tensor(out=QP[:, hl, :], in0=MMt[bi][:], in1=E[:], op=ALU.mult), reads=[b_MM[bi], bE], writes=[b_QP])
        for hl, h in enumerate(heads):
            def ftp(e):
                ins = None
                for s in range(4):
                    ins = e.transpose(TPt[:, s, :], kppT[hl][:, s * 128:(s + 1) * 128], identb[:])
                return ins
            k.op(PE, ftp, reads=[b_kppT[hl], b_identb], writes=[b_TP])
            k.op(DVE, lambda e: e.tensor_copy(out=KPP[:, :, hl * 128:(hl + 1) * 128], in_=TPt[:, 0:4, :]), reads=[b_TP], writes=[b_KPP])
        if not main:
            def fst(e):
                ins = None
                for hl in range(4):
                    for s in range(4):
                        ins = e.matmul(STt2[0][:, hl, :], lhsT=KPP[:, s, hl * 128:(hl + 1) * 128],
                                       rhs=VI[:, s, hl * 128:(hl + 1) * 128], start=(s == 0), stop=(s == 3))
                return ins
            k.op(PE, fst, reads=[b_KPP, b_VI], writes=[b_ST2[0]])
            for hl, h in enumerate(heads):
                k.op(DVE, lambda e: e.scalar_tensor_tensor(out=S32[:, h, :], in0=S32[:, h, :], scalar=decs[:, h, 0:1],
                                                           in1=STt2[0][:, hl, :], op0=ALU.mult, op1=ALU.add),
                     reads=[b_S32[h], b_decs[h], b_ST2[0]], writes=[b_S32[h]])
                if last_pre:
                    k.op(POOL, lambda e: e.tensor_copy(out=SBF[:, h, 0, :], in_=S32[:, h, :]), reads=[b_S32[h]], writes=[b_SBF[h]])
            return

    def hgrn_pass1(gi, gt):
        heads = [4 * gi + i for i in range(4)]
        for j in range(8):
            s, pb = j // 2, (j % 2) * 64
            sb_i = j % 2

            def fst(e):
                ins = None
                for hl in range(4):
                    ins = e.matmul(STt2[sb_i][:, hl, :], lhsT=KPP[pb:pb + 64, s, hl * 128:(hl + 1) * 128],
                                   rhs=VI[pb:pb + 64, s, hl * 128:(hl + 1) * 128], start=True, stop=True)
                return ins
            k.op(PE, fst, reads=[b_KPP, b_VI], writes=[b_ST2[sb_i]])
            for hl, h in enumerate(heads):
                k.op(DVE, lambda e: e.scalar_tensor_tensor(out=S32[:, h, :], in0=S32[:, h, :], scalar=decs[:, h, j:j + 1],
                                                           in1=STt2[sb_i][:, hl, :], op0=ALU.mult, op1=ALU.add),
                     reads=[b_S32[h], b_decs[h], b_ST2[sb_i]], writes=[b_S32[h]])
                so = sbf_slot(gt, j + 1)
                if hl % 2 == 0:
                    k.op(ACT, lambda e: e.activation(out=SBF[:, h, so, :], in_=S32[:, h, :], func=AF.Copy), reads=[b_S32[h]], writes=[b_SBF[h]])
                else:
                    k.op(POOL, lambda e: e.tensor_copy(out=SBF[:, h, so, :], in_=S32[:, h, :]), reads=[b_S32[h]], writes=[b_SBF[h]])
            yield

    def hgrn_pass2(gi, gt):
        heads = [4 * gi + i for i in range(4)]

        def sc(hl):
            ci = hl % 2
            SCv = STt2[ci].rearrange("p q v -> p (q v)")

            def fsc(e):
                ins = None
                for j in range(8):
                    pb = (j % 2) * 64
                    ins = e.matmul(SCv[pb:pb + 64, (j // 2) * 64:(j // 2 + 1) * 64], lhsT=KP[:, hl, j * 64:(j + 1) * 64],
                                   rhs=QP[:, hl, j * 64:(j + 1) * 64], start=True, stop=True)
                return ins
            k.op(PE, fsc, reads=[b_KP, b_QP], writes=[b_ST2[ci]])
            k.op(DVE, lambda e: e.tensor_tensor(out=scm[ci][:], in0=SCv[:, 0:256], in1=mask_sc, op=ALU.mult),
                 reads=[b_ST2[ci], b_cst], writes=[b_scm[ci]])

        sc(0)
        sc(1)
        yield
        for hl, h in enumerate(heads):
            ci = hl % 2
            oi = hl % 2

            def fo(e):
                ins = None
                for j in range(8):
                    s, pb = j // 2, (j % 2) * 64
                    si = sbf_slot(gt, j)
                    e.matmul(OTt[oi][:, j * 64:(j + 1) * 64], lhsT=VI[pb:pb + 64, s, hl * 128:(hl + 1) * 128],
                             rhs=scm[ci][pb:pb + 64, s * 64:(s + 1) * 64], start=True, stop=False)
                    ins = e.matmul(OTt[oi][:, j * 64:(j + 1) * 64], lhsT=SBF[:, h, si, :], rhs=QP[:, hl, j * 64:(j + 1) * 64],
                                   start=False, stop=True)
                return ins
            k.op(PE, fo, reads=[b_VI, b_scm[ci], b_SBF[h], b_QP], writes=[b_OT[oi]])
            OC, bOC = P1[hl], b_P1[hl]
            k.op(DVE, lambda e: e.tensor_copy(out=OC[:], in_=OTt[oi][:]), reads=[b_OT[oi]], writes=[bOC])
            SQ, bSQ = P2[hl], b_P2[hl]
            k.op(ACT, lambda e: e.activation(out=SQ[:], in_=OC[:], func=AF.Square), reads=[bOC], writes=[bSQ])
            if hl + 2 < 4:
                sc(hl + 2)
            yield
        for hl, h in enumerate(heads):
            OC, bOC = P1[hl], b_P1[hl]
            SQ, bSQ = P2[hl], b_P2[hl]
            bi = mm_bank()
            k.op(PE, lambda e: e.matmul(MMt[bi][:], lhsT=onesf[:], rhs=SQ[:], start=True, stop=True), reads=[b_onesf, bSQ], writes=[b_MM[bi]])
            k.op(ACT, lambda e: e.activation(out=SQ[:], in_=MMt[bi][:], func=AF.Ln, scale=1.0 / 128, bias=EPS), reads=[b_MM[bi]], writes=[bSQ])
            k.op(ACT, lambda e: e.activation(out=SQ[:], in_=SQ[:], func=AF.Exp, scale=-0.5), reads=[bSQ], writes=[bSQ])
            k.op(DVE, lambda e: e.scalar_tensor_tensor(out=SQ[:], in0=OC[:], scalar=cols[:, 24 + h:25 + h], in1=SQ[:],
                                                       op0=ALU.mult, op1=ALU.mult), reads=[bOC, b_cols, bSQ], writes=[bSQ])
            k.op(POOL, lambda e: e.tensor_tensor(out=R5[:, h, :], in0=SQ[:], in1=R5[:, h, :], op=ALU.mult), reads=[bSQ, b_R5], writes=[b_R5])
            if hl < 3:
                yield

    def interleave(gen, fillers):
        fillers = list(fillers)
        for _ in gen:
            if fillers:
                fillers.pop(0)()
        for f in fillers:
            f()

    def out_proj(loader, nk, lhs, lbufs, gain_off, final, t):
        nkh = nk // 8 if nk == 8 else 2
        kper = 8 if nk == 8 else 11
        pend = []
        ssq_ = [stat_cols(2) for _ in range(4)]
        ssq = [a for a, _ in ssq_]
        bssq = [b for _, b in ssq_]
        s_outer = (nk == 8)
        slots_all = {}
        if s_outer:
            for nh in range(2):
                slots_all[nh] = [loader(kh, nh) for kh in range(nkh)]
            order = [(s, nh) for s in range(4) for nh in range(2)]
        else:
            order = [(s, nh) for nh in range(2) for s in range(4)]
        for (s, nh) in order:
            if nh not in slots_all:
                slots_all[nh] = [loader(kh, nh) for kh in range(nkh)]
            slots = slots_all[nh]
            bi = (2 * s + nh) % 4 if s_outer else s
            if bi == 3:
                ptile, pbuf = STt.rearrange("p q v -> p (q v)"), b_ST
            else:
                ptile, pbuf = MMt[bi], [b_MM[bi]]

            def f(e):
                ins = None
                n = 0
                for kh in range(nkh):
                    v = slot_view(slots[kh], kper)
                    for kc in range(kper):
                        ins = e.matmul(ptile[:], lhsT=lhs[:, kh * kper + kc, s * 128:(s + 1) * 128], rhs=v[:, kc, :],
                                       start=(n == 0), stop=(n == nkh * kper - 1))
                        n += 1
                return ins
            k.op(PE, f, reads=[b_ring[sl] for sl in slots] + lbufs, writes=pbuf)
            k.op(ACT, lambda e: e.activation(out=junk[:, 0:512], in_=ptile[:], func=AF.Square, accum_out=ssq[s][:, nh:nh + 1]),
                 reads=pbuf, writes=[b_junk, bssq[s]])
            park = BT[s // 2][:, (s % 2) * 512:(s % 2 + 1) * 512]
            if nh == 0:
                k.op(ACT, lambda e: e.activation(out=park, in_=ptile[:], func=AF.Copy), reads=pbuf, writes=[b_BT[s // 2]])
                if final and t + 1 < ntiles_main:
                    if len(pend) == 2:
                        norm_b(pend.pop(0))
                    pend.append(norm_a(s, Xn[s], b_Xn[s]))
                    if s == 3:
                        while pend:
                            norm_b(pend.pop(0))
                continue
            tot, btot = stat_cols(1)
            k.op(POOL, lambda e: e.tensor_tensor(out=tot, in0=ssq[s][:, 0:1], in1=ssq[s][:, 1:2], op=ALU.add),
                 reads=[bssq[s]], writes=[btot])
            r, br = rstd_small(tot, btot, 1, 1.0 / D)
            tA, btA = ta()
            k.op(DVE, lambda e: e.scalar_tensor_tensor(out=tA[:], in0=ptile[:], scalar=r, in1=rows[:, gain_off + 512:gain_off + 1024],
                                                       op0=ALU.mult, op1=ALU.mult), reads=pbuf + [br, b_rows], writes=[btA])
            tB, btB = ta()
            k.op(DVE, lambda e: e.scalar_tensor_tensor(out=tB[:], in0=park, scalar=r, in1=rows[:, gain_off:gain_off + 512],
                                                       op0=ALU.mult, op1=ALU.mult), reads=[b_BT[s // 2], br, b_rows], writes=[btB])
            if final:
                k.op(POOL, lambda e: e.tensor_tensor(out=Y[:, 512:1024], in0=X[s][:, 512:1024], in1=tA[:], op=ALU.add),
                     reads=[btA, b_X[s]], writes=[b_Y])
                k.op(POOL, lambda e: e.tensor_tensor(out=Y[:, 0:512], in0=X[s][:, 0:512], in1=tB[:], op=ALU.add),
                     reads=[btB, b_X[s]], writes=[b_Y])
                k.dma(SP, d_y, y_out[t * T + s * 128:t * T + (s + 1) * 128, :], Y, reads=[b_Y])
                if t + 1 < ntiles_main:
                    k.op(DVE, lambda e: e.tensor_copy(out=X[s][:], in_=Xn[s]), reads=[b_Xn[s]], writes=[b_X[s]])
            else:
                k.op(POOL, lambda e: e.tensor_tensor(out=X[s][:, 512:1024], in0=X[s][:, 512:1024], in1=tA[:], op=ALU.add),
                     reads=[btA, b_X[s]], writes=[b_X[s]])
                k.op(POOL, lambda e: e.tensor_tensor(out=X[s][:, 0:512], in0=X[s][:, 0:512], in1=tB[:], op=ALU.add),
                     reads=[btB, b_X[s]], writes=[b_X[s]])
                if len(pend) == 2:
                    norm_b(pend.pop(0))
                pend.append(norm_a(s))
        while pend:
            norm_b(pend.pop(0))

    gt = 0
    if ntiles_pre:
        for s in range(4):
            load_x(x_prev, 0, s)
    else:
        for s in range(4):
            load_x(x_main, 0, s)
    for t in range(ntiles_pre):
        for s in range(4):
            norm_to_hT(s)
        for s in range(4):
            if t + 1 < ntiles_pre:
                load_x(x_prev, t + 1, s)
            else:
                load_x(x_main, 0, s)
        for gi in range(2):
            hgrn_group(gi, 0, False, last_pre=(t == ntiles_pre - 1))
        conv_run(n_phase0 + (t + 1) * (-(-(nconv - n_phase0) // max(ntiles_pre, 1))))
    conv_run(nconv, flush=True)

    for t in range(ntiles_main):
        if t == 0:
            for s in range(4):
                norm_to_hT(s)
        if stop == 2:
            return finish()
        sl_u = [load_k8(wb_in, "in", c0) for c0 in (0, 512)]
        sl_v = [load_k8(wb_in, "in", c0) for c0 in (1024, 1536)]
        for s in range(4):
            gi_ = nxt("bt", 2)
            gv, bgv = BT[gi_], b_BT[gi_]
            sm, bsm = stat_cols(2)
            for hf in range(2):
                bi = tm_group(sl_v[hf], HT, s, [b_HT])
                k.op(ACT, lambda e: e.activation(out=gv[:, hf * 512:(hf + 1) * 512], in_=MMt[bi][:], func=AF.Gelu, accum_out=sm[:, hf:hf + 1]),
                     reads=[b_MM[bi]], writes=[bgv, bsm])
            nm0, bnm0 = stat_cols(1)
            nm_, bnm = stat_cols(1)
            k.op(POOL, lambda e: e.tensor_tensor(out=nm0, in0=sm[:, 0:1], in1=sm[:, 1:2], op=ALU.add), reads=[bsm], writes=[bnm0])
            k.op(POOL, lambda e: e.tensor_scalar(out=nm_, in0=nm0, scalar1=-1.0 / D, scalar2=1.0, op0=ALU.mult, op1=ALU.mult),
                 reads=[bnm0], writes=[bnm])
            vs, bvs = stat_cols(1)
            k.op(ACT, lambda e: e.activation(out=junk[:], in_=gv[:], func=AF.Square, bias=nm_, accum_out=vs),
                 reads=[bgv, bnm], writes=[b_junk, bvs])
            r, br = rstd_small(vs, bvs, 1, 1.0 / D)
            k.op(DVE, lambda e: e.tensor_scalar(out=VLN[:, s, :], in0=gv[:], scalar1=nm_, scalar2=r, op0=ALU.add, op1=ALU.mult),
                 reads=[bgv, bnm, br], writes=[b_VLN])
        for c in range(8):
            bi = fm_group(sl_u[c // 4], c % 4, HT, [b_HT])
            k.op(ACT, lambda e: e.activation(out=U[:, c, :], in_=MMt[bi][:], func=AF.Gelu), reads=[b_MM[bi]], writes=[b_U])
        for g in range(8):
            bi = mm_bank()

            def fs(e):
                ins = None
                for n in range(4):
                    ins = e.matmul(MMt[bi][:, n * 128:(n + 1) * 128], lhsT=VLN[:, n, g * 128:(g + 1) * 128], rhs=wspb[:, g, :], start=True, stop=True)
                return ins
            k.op(PE, fs, reads=[b_VLN, b_wspb], writes=[b_MM[bi]])
            tA, btA = ta()
            k.op(DVE, lambda e: e.scalar_tensor_tensor(out=tA.rearrange("p (n t) -> p n t", t=128), in0=MMt[bi].rearrange("p (n t) -> p n t", t=128),
                                                       scalar=cols[:, 16 + g:17 + g], in1=cg[:, g:g + 1, :].to_broadcast([128, 4, 128]),
                                                       op0=ALU.mult, op1=ALU.add), reads=[b_MM[bi], b_cols, b_cg], writes=[btA])
            k.op(POOL, lambda e: e.tensor_tensor(out=U[:, g, :], in0=tA[:], in1=U[:, g, :], op=ALU.mult), reads=[btA, b_U], writes=[b_U])
        if stop == 3:
            return finish()
        hgrn_group(0, gt, True)
        sl_ga = [load_k8(wb_in, "in", c0) for c0 in (6144, 6656)]

        def ga_item(c):
            bi = fm_group(sl_ga[c // 4], c % 4, HT, [b_HT], wide=False)
            k.op(ACT, lambda e: e.activation(out=R4[:, c, :], in_=MMt[bi][:], func=AF.Sigmoid), reads=[b_MM[bi]], writes=[b_R4])
        interleave(hgrn_pass1(0, gt), [(lambda c=c: ga_item(c)) for c in range(8)])
        sl_pa = [load_k8(wb_pa, "pa", c0) for c0 in (0, 512)]

        def pa_item(c):
            bi = fm_group(sl_pa[c // 4], c % 4, U, [b_U], wide=False)
            k.op(DVE, lambda e: e.tensor_tensor(out=R4[:, c, :], in0=MMt[bi][:], in1=R4[:, c, :], op=ALU.mult), reads=[b_MM[bi], b_R4], writes=[b_R4])
        interleave(hgrn_pass2(0, gt), [(lambda c=c: pa_item(c)) for c in range(8)])
        hgrn_group(1, gt, True)
        sl_gb = [load_k8(wb_in, "in", c0) for c0 in (7168, 7680)]

        def gb_item(c):
            bi = fm_group(sl_gb[c // 4], c % 4, HT, [b_HT], wide=False)
            k.op(ACT, lambda e: e.activation(out=GB[:, c, :], in_=MMt[bi][:], func=AF.Sigmoid), reads=[b_MM[bi]], writes=[b_U])
        interleave(hgrn_pass1(1, gt), [(lambda c=c: gb_item(c)) for c in range(8)])
        interleave(hgrn_pass2(1, gt), [])
        gt += 1
        sl_pb = [load_k8(wb_pb, "pb", c0) for c0 in (0, 512)]
        for c in range(8):
            bi = fm_group(sl_pb[c // 4], c % 4, R5, [b_R5])
            tA, btA = ta()
            k.op(DVE, lambda e: e.tensor_tensor(out=tA[:], in0=MMt[bi][:], in1=GB[:, c, :], op=ALU.mult), reads=[b_MM[bi], b_U], writes=[btA])
            k.op(POOL, lambda e: e.tensor_tensor(out=R4[:, c, :], in0=tA[:], in1=R4[:, c, :], op=ALU.add), reads=[btA, b_R4], writes=[b_R4])
        if stop == 6:
            return finish()
        out_proj(lambda kh, nh: load_k8(wb_out, "out", nh * 512), 8, R4, [b_R4], 0, False, t)
        if stop == 7:
            return finish()
        if t + 1 < ntiles_main:
            for s in range(4):
                k.dma(SP, d_xl[s], Xn[s], x_main[(t + 1) * T + s * 128:(t + 1) * T + (s + 1) * 128, :], writes=[b_Xn[s]])
        for blk in range(11):
            sl = load_up(blk)
            for jj in range(2):
                bg = fm_group(sl, jj, HT, [b_HT])
                bu = fm_group(sl, 2 + jj, HT, [b_HT])
                tA, btA = ta()
                k.op(ACT, lambda e: e.activation(out=tA[:], in_=MMt[bg][:], func=AF.Silu), reads=[b_MM[bg]], writes=[btA])
                k.op(DVE, lambda e: e.tensor_tensor(out=FACT[:, 2 * blk + jj, :], in0=MMt[bu][:], in1=tA[:], op=ALU.mult),
                     reads=[b_MM[bu], btA], writes=[b_FACT])
        out_proj(load_dn, 22, FACT, [b_FACT], D, True, t)

    return finish()


def _unused():
    for s in range(4):
        if d_xs[s].cnt:
            nc.sync.wait_ge(d_xs[s].sem, d_xs[s].cnt)
    nc.all_engine_barrier()
    for sm in k.sems:
        nc.gpsimd.sem_clear(sm)
    return nc


def _host_inputs(inp):
    f = lambda a: np.ascontiguousarray(np.asarray(a, dtype=np.float32))
    x = f(inp["x"])
    col = lambda v: f(v).reshape(8, 128).T
    cols = np.concatenate([col(inp["pre_mix_gain"][0]), col(inp["pre_ffn_gain"][0]), col(inp["sgu_norm_gain"][0]),
                           col(inp["hgrn_norm_gain"][0]), col(inp["lb_logits"][0]), col(inp["lb_logits"][1])], axis=1)
    rows = np.concatenate([np.broadcast_to(f(inp["post_mix_gain"][0])[None, :], (128, D)),
                           np.broadcast_to(f(inp["post_ffn_gain"][0])[None, :], (128, D))], axis=1)
    wspT = f(inp["w_spatial"][0]).transpose(2, 0, 1).reshape(128, 1024)
    ident = np.eye(128, dtype=np.float32)
    p = np.arange(128)
    tt = np.arange(64)
    msc = (p[:, None] % 64 <= tt[None, :]).astype(np.float32)
    mask_sc = np.tile(msc, (1, 4))
    mask_scan = np.ones((128, 512), np.float32)
    mask_scan[:, 0::64] = 0.0
    mask_sguT = (p[:, None] // 64 <= p[None, :] // 64).astype(np.float32)
    consts = np.concatenate([ident, mask_sc, mask_scan, mask_sguT], axis=1)
    sguA = np.stack([f(inp["sgu_norm_bias"][0]), np.ones(D, np.float32)], axis=0)
    bsp = f(inp["b_spatial"][0]).reshape(1, D)
    shared = {
        "w_in": f(inp["w_in"][0]), "w_pa": f(inp["w_proj_sgu"][0]), "w_pb": f(inp["w_proj_hgrn"][0]),
        "w_out": f(inp["w_out"][0]), "w_up": f(inp["w_ffn_up"][0]), "w_dn": f(inp["w_ffn_down"][0]),
        "cols": f(cols), "rows_bc": f(rows), "wspT": f(wspT), "consts": f(consts), "sguA": f(sguA), "bsp": bsp,
    }
    maps = []
    zeros = np.zeros((NTOK, D), np.float32)
    for c in range(8):
        b, half = c // 2, c % 2
        m = dict(shared)
        m["x_main"] = np.ascontiguousarray(x[b, half * NTOK:(half + 1) * NTOK])
        m["x_prev"] = np.ascontiguousarray(x[b, 0:NTOK]) if half == 1 else zeros
        maps.append(m)
    return maps


def kernel(**inputs):
    maps = _host_inputs(inputs)
    nc = build_nc()
    res = run_bass_kernel_spmd(nc, maps, core_ids=list(range(8)))
    out = np.empty((4, 8192, D), np.float32)
    for c in range(8):
        b, half = c // 2, c % 2
        out[b, half * NTOK:(half + 1) * NTOK] = np.asarray(res.results[c]["y_out"], dtype=np.float32)
    return out
```

```python
import numpy as np
import concourse.bass as bass
import concourse.mybir as mybir
from concourse.bass_utils import run_bass_kernel_spmd

F32 = mybir.dt.float32
BF16 = mybir.dt.bfloat16
AF = mybir.ActivationFunctionType
ALU = mybir.AluOpType

D = 1024
NTOK = 4096
T = 512
NT = NTOK // T
FH = 2816
EPS = 1e-6
NCOL = 48


class Eng:
    def __init__(self, nc, name, h, kind):
        self.h = h
        self.kind = kind
        self.sem = nc.alloc_semaphore("se_" + name)
        self.cnt = 0
        self.waited = {}


class DSem:
    def __init__(self, nc, name):
        self.sem = nc.alloc_semaphore("sd_" + name)
        self.cnt = 0


class Buf:
    def __init__(self, name):
        self.name = name
        self.wr = {}
        self.rd = {}
        self.ov = []


def overlap(*bufs):
    for a in bufs:
        for b in bufs:
            if a is not b and b not in a.ov:
                a.ov.append(b)


class K:
    def __init__(self, nc):
        self.nc = nc
        self.pe = Eng(nc, "pe", nc.tensor, "pe")
        self.act = Eng(nc, "act", nc.scalar, "c")
        self.dve = Eng(nc, "dve", nc.vector, "c")
        self.pool = Eng(nc, "pool", nc.gpsimd, "c")
        self.sp = Eng(nc, "sp", nc.sync, "q")
        self.sems = [e.sem for e in (self.pe, self.act, self.dve, self.pool, self.sp)]
        self.semobj = {}

    def dsem(self, name):
        d = DSem(self.nc, name)
        self.sems.append(d.sem)
        return d

    def _deps(self, eng, reads, writes):
        need = {}

        def add(dct, same_ok):
            for key, (sem, val, src) in dct.items():
                if src is eng:
                    if eng.kind == "pe":
                        continue
                if need.get(key, (None, 0))[1] < val:
                    need[key] = (sem, val)

        for b in reads:
            for o in [b] + b.ov:
                add(o.wr, True)
        for b in writes:
            for o in [b] + b.ov:
                add(o.wr, False)
                add(o.rd, False)
        for key, (sem, val) in need.items():
            if eng.waited.get(key, 0) < val:
                eng.h.wait_ge(sem, val)
                eng.waited[key] = val

    def _mark(self, tok, reads, writes):
        key = id(tok[0])
        for b in writes:
            b.wr[key] = tok
        for b in reads:
            b.rd[key] = tok

    def op(self, eng, fn, reads=(), writes=()):
        self._deps(eng, reads, writes)
        ins = fn(eng.h)
        eng.cnt += 1
        ins.then_inc(eng.sem, 1)
        self._mark((eng.sem, eng.cnt, eng), reads, writes)

    def dma(self, q, ds, out, in_, reads=(), writes=()):
        self._deps(q, reads, writes)
        ins = q.h.dma_start(out=out, in_=in_)
        ds.cnt += 16
        ins.then_inc(ds.sem, 16)
        self._mark((ds.sem, ds.cnt, None), reads, writes)


def build_nc(ntiles_main=NT, ntiles_pre=NT, stop=99):
    nc = bass.Bass("TRN2", target_bir_lowering=False)

    def din(name, shape, dt=F32):
        return nc.dram_tensor(name, shape, dt, kind="ExternalInput").ap()

    x_main = din("x_main", [NTOK, D])
    x_prev = din("x_prev", [NTOK, D])
    w_in = din("w_in", [D, 8192])
    w_pa = din("w_pa", [D, D])
    w_pb = din("w_pb", [D, D])
    w_out = din("w_out", [D, D])
    w_up = din("w_up", [D, 2 * FH])
    w_dn = din("w_dn", [FH, D])
    cols_d = din("cols", [128, NCOL])
    rows_d = din("rows_bc", [128, 2 * D])
    wspT_d = din("wspT", [128, 8 * 128])
    cst_d = din("consts", [128, 128 + 256 + 512 + 128])
    sguA_d = din("sguA", [2, D])
    bsp_d = din("bsp", [1, D])
    y_out = nc.dram_tensor("y_out", [NTOK, D], F32, kind="ExternalOutput").ap()

    def dscr(name, shape):
        return nc.dram_tensor(name, shape, BF16, kind="Internal").ap()

    wb_in = dscr("wb_in", [D, 8192])
    wb_pa = dscr("wb_pa", [D, D])
    wb_pb = dscr("wb_pb", [D, D])
    wb_out = dscr("wb_out", [D, D])
    wb_up = dscr("wb_up", [D, 2 * FH])
    wb_dn = dscr("wb_dn", [FH, D])

    k = K(nc)
    PE, ACT, DVE, POOL, SP = k.pe, k.act, k.dve, k.pool, k.sp

    def sb(name, shape, dt):
        return nc.alloc_sbuf_tensor("s_" + name, shape, dt)

    cols = sb("cols", [128, NCOL], F32)
    der = sb("der", [128, 64], F32)
    rows = sb("rows", [128, 2 * D], F32)
    cst = sb("cst", [128, 1024], F32)
    identb = sb("identb", [128, 128], BF16)
    onesf = sb("onesf", [128, 128], F32)
    wspb = sb("wspb", [128, 8, 128], BF16)
    cg = sb("cg", [128, 8, 128], F32)
    X = [sb(f"x{s}", [128, D], F32) for s in range(4)]
    wspf = X[1].rearrange("p (g t) -> p g t", t=128)
    sguA = X[2][0:2, :]
    sguB = X[3][0:2, :]
    HT = sb("hT", [128, 8, T], BF16)
    RR = sb("RR", [128, 12288], BF16)
    R4 = sb("R4", [128, 8, T], BF16)
    R5 = sb("R5", [128, 8, T], BF16)
    S32 = sb("S32", [128, 8, 128], F32)
    SBF = sb("SBF", [128, 8, 9, 128], BF16)
    decs = sb("decs", [128, 8, 8], F32)
    scm = [sb(f"scm{i}", [128, 256], BF16) for i in range(2)]
    Bt = [sb(f"Bt{i}", [128, T], F32) for i in range(4)]
    TA = [sb(f"ta{i}", [128, T], F32) for i in range(3)]
    P1all = sb("P1all", [128, 4, T], F32)
    P1 = [P1all[:, i, :] for i in range(4)]
    Y = P1all[:, 0:2, :].rearrange("p a n -> p (a n)")
    P2all = sb("P2all", [128, 4, T], F32)
    P3all = sb("P3all", [128, 4, T], F32)
    P2 = [P2all[:, i, :] for i in range(4)]
    P3 = [P3all[:, i, :] for i in range(4)]
    Xn = [P2all[:, 0:2, :].rearrange("p a n -> p (a n)"), P2all[:, 2:4, :].rearrange("p a n -> p (a n)"),
          P3all[:, 0:2, :].rearrange("p a n -> p (a n)"), P3all[:, 2:4, :].rearrange("p a n -> p (a n)")]
    bia = sb("bia", [128, 4], F32)
    kppT = [sb(f"kppT{i}", [128, T], BF16) for i in range(4)]
    junk = sb("junk", [128, D], BF16)
    hb = [sb(f"hb{i}", [128, D], BF16) for i in range(2)]
    BT = [sb(f"BT{i}", [128, D], F32) for i in range(2)]
    stat = sb("stat", [128, 64], F32)
    RING = sb("RING", [128, 4 * 5632], BF16)

    U = RR[:, 0:4096].rearrange("p (k n) -> p k n", n=T)
    VLN = RR[:, 4096:8192].rearrange("p (s n) -> p s n", n=D)
    QP = RR[:, 4096:6144].rearrange("p (k n) -> p k n", n=T)
    KP = RR[:, 6144:8192].rearrange("p (k n) -> p k n", n=T)
    KPP = RR[:, 8192:10240].rearrange("p (s n) -> p s n", n=T)
    VI = RR[:, 10240:12288].rearrange("p (s n) -> p s n", n=T)
    GB = U
    FACT = RR[:, 0:11264].rearrange("p (k n) -> p k n", n=T)

    MMt = [nc.alloc_psum_tensor(f"mm{i}", [128, T], F32) for i in range(3)]
    STt2 = [nc.alloc_psum_tensor(f"st{i}", [128, 4, 128], F32) for i in range(2)]
    STt = STt2[0]
    TPt = nc.alloc_psum_tensor("tp", [128, 8, 128], BF16)
    OTt = [nc.alloc_psum_tensor(f"ot{i}", [128, T], F32) for i in range(2)]

    b_cols, b_der, b_rows, b_cst, b_identb, b_onesf = (Buf(n) for n in ("cols", "der", "rows", "cst", "identb", "onesf"))
    b_wspf, b_wspb, b_cg, b_sguA, b_sguB = (Buf(n) for n in ("wspf", "wspb", "cg", "sguA", "sguB"))
    b_X = [Buf(f"x{s}") for s in range(4)]
    overlap(b_X[1], b_wspf)
    overlap(b_X[2], b_sguA)
    overlap(b_X[3], b_sguB)
    b_HT = Buf("hT")
    b_U, b_VLN, b_QP, b_KP, b_KPP, b_VI, b_FACT = (Buf(n) for n in ("U", "VLN", "QP", "KP", "KPP", "VI", "FACT"))
    overlap(b_U, b_FACT)
    overlap(b_VLN, b_QP, b_FACT)
    overlap(b_VLN, b_KP, b_FACT)
    overlap(b_KPP, b_FACT)
    overlap(b_VI, b_FACT)
    b_R4, b_R5 = Buf("R4"), Buf("R5")
    b_S32 = [Buf(f"S32_{h}") for h in range(8)]
    b_SBF = [Buf(f"SBF_{h}") for h in range(8)]
    b_decs = [Buf(f"decs{h}") for h in range(8)]
    b_scm = [Buf("scm0"), Buf("scm1")]
    b_Bt = [Buf(f"Bt{i}") for i in range(4)]
    b_TA = [Buf(f"ta{i}") for i in range(3)]
    b_P1 = [Buf(f"P1_{i}") for i in range(4)]
    b_P2 = [Buf(f"P2_{i}") for i in range(4)]
    b_P3 = [Buf(f"P3_{i}") for i in range(4)]
    b_Xn = [Buf(f"xn{i}") for i in range(4)]
    overlap(b_Xn[0], b_P2[0])
    overlap(b_Xn[0], b_P2[1])
    overlap(b_Xn[1], b_P2[2])
    overlap(b_Xn[1], b_P2[3])
    overlap(b_Xn[2], b_P3[0])
    overlap(b_Xn[2], b_P3[1])
    overlap(b_Xn[3], b_P3[2])
    overlap(b_Xn[3], b_P3[3])
    b_bia = [Buf(f"bia{i}") for i in range(4)]
    b_Y = Buf("Y")
    overlap(b_Y, b_P1[0])
    overlap(b_Y, b_P1[1])
    b_kppT = [Buf(f"kppT{i}") for i in range(4)]
    b_junk = Buf("junk")
    b_hb = [Buf("hb0"), Buf("hb1")]
    b_BT = [Buf("BT0"), Buf("BT1")]
    b_stat = Buf("stat")
    b_ring = [Buf(f"ring{i}") for i in range(4)]
    b_stb = [Buf("stb0"), Buf("stb1")]
    b_MM = [Buf(f"mm{i}") for i in range(3)]
    b_ST2 = [Buf(f"st{i}") for i in range(2)]
    b_ST = [b_ST2[0]]
    b_TP = Buf("tp")
    b_OT = [Buf("ot0"), Buf("ot1")]
    b_wb = {n: Buf("wb_" + n) for n in ("in", "pa", "pb", "out", "up", "dn")}

    d_const = k.dsem("const")
    d_ring = [k.dsem(f"ring{i}") for i in range(4)]
    d_stl = [k.dsem(f"stl{i}") for i in range(3)]
    d_conv = [k.dsem("conv0"), k.dsem("conv1")]
    d_xl = [k.dsem(f"xl{s}") for s in range(4)]
    d_xs = [k.dsem(f"xs{s}") for s in range(4)]
    d_y = k.dsem("y")

    def finish():
        for s in range(4):
            if d_xs[s].cnt:
                nc.sync.wait_ge(d_xs[s].sem, d_xs[s].cnt)
        if d_y.cnt:
            nc.sync.wait_ge(d_y.sem, d_y.cnt)
        nc.all_engine_barrier()
        for sm in k.sems:
            nc.gpsimd.sem_clear(sm)
        return nc

    for sm in k.sems:
        nc.gpsimd.sem_clear(sm)
    nc.all_engine_barrier()

    k.dma(SP, d_const, cols[:], cols_d[:, :], writes=[b_cols])
    k.dma(SP, d_const, rows[:], rows_d[:, :], writes=[b_rows])
    k.dma(SP, d_const, cst[:], cst_d[:, :], writes=[b_cst])
    k.dma(SP, d_const, X[1][:], wspT_d[:, :], writes=[b_wspf])
    k.dma(SP, d_const, sguA, sguA_d[:, :], writes=[b_sguA])
    k.dma(SP, d_const, sguB[1:2, :], bsp_d[:, :], writes=[b_sguB])
    for b_ in (b_cols, b_rows, b_cst, b_wspf, b_sguA, b_sguB):
        b_.wr[id(d_const.sem)] = (d_const.sem, d_const.cnt, None)
    ident_f = cst[:, 0:128]
    mask_sc = cst[:, 128:384]
    mask_scan = cst[:, 384:896]
    mask_sgu = cst[:, 896:1024]

    k.op(DVE, lambda e: e.tensor_copy(out=identb[:], in_=ident_f), reads=[b_cst], writes=[b_identb])
    k.op(POOL, lambda e: e.memset(onesf[:], 1.0), writes=[b_onesf])
    k.op(POOL, lambda e: e.memset(der[:, 40:48], -0.5), writes=[b_der])
    k.op(POOL, lambda e: e.memset(S32.rearrange("p h v -> p (h v)"), 0.0), writes=b_S32)
    k.op(POOL, lambda e: e.memset(SBF.rearrange("p h j v -> p (h j v)"), 0.0), writes=b_SBF)
    k.op(DVE, lambda e: e.tensor_tensor(out=der[:, 24:32], in0=cols[:, 40:48], in1=cols[:, 32:40], op=ALU.subtract),
         reads=[b_cols], writes=[b_der])
    k.op(ACT, lambda e: e.activation(out=der[:, 24:32], in_=der[:, 24:32], func=AF.Exp), reads=[b_der], writes=[b_der])
    k.op(DVE, lambda e: e.tensor_scalar(out=der[:, 32:40], in0=der[:, 24:32], scalar1=1.0, scalar2=None, op0=ALU.add),
         reads=[b_der], writes=[b_der])
    k.op(DVE, lambda e: e.reciprocal(out=der[:, 0:8], in_=der[:, 32:40]), reads=[b_der], writes=[b_der])
    k.op(DVE, lambda e: e.tensor_tensor(out=der[:, 8:16], in0=der[:, 24:32], in1=der[:, 0:8], op=ALU.mult),
         reads=[b_der], writes=[b_der])
    k.op(ACT, lambda e: e.activation(out=der[:, 16:24], in_=der[:, 8:16], func=AF.Ln), reads=[b_der], writes=[b_der])
    LB = lambda h: der[:, h:h + 1]
    LNOML = lambda h: der[:, 16 + h:17 + h]
    NEGH = der[:, 40:41]

    w3 = wspf
    ones512 = onesf[:, 0:1].to_broadcast([128, T])
    k.op(DVE, lambda e: e.tensor_tensor(out=w3, in0=w3, in1=mask_sgu.unsqueeze(1).to_broadcast([128, 8, 128]), op=ALU.mult),
         reads=[b_wspf, b_cst], writes=[b_wspf])
    k.op(POOL, lambda e: e.tensor_copy(out=wspb[:], in_=wspf), reads=[b_wspf], writes=[b_wspb])

    def f_wsum2(e):
        ins = None
        for g in range(8):
            dst = MMt[g // 4][0:1, (g % 4) * 128:(g % 4 + 1) * 128]
            ins = e.matmul(dst, lhsT=onesf[:, 0:1], rhs=wspf[:, g, :], start=True, stop=True)
        return ins
    k.op(PE, f_wsum2, reads=[b_onesf, b_wspf], writes=[b_MM[0], b_MM[1]])
    k.op(DVE, lambda e: e.tensor_copy(out=sguB[0:1, 0:512], in_=MMt[0][0:1, :]), reads=[b_MM[0]], writes=[b_sguB])
    k.op(DVE, lambda e: e.tensor_copy(out=sguB[0:1, 512:1024], in_=MMt[1][0:1, :]), reads=[b_MM[1]], writes=[b_sguB])

    def f_cg(e):
        ins = None
        for g in range(8):
            dst = MMt[g // 4][:, (g % 4) * 128:(g % 4 + 1) * 128]
            ins = e.matmul(dst, lhsT=sguA[:, g * 128:(g + 1) * 128], rhs=sguB[:, g * 128:(g + 1) * 128], start=True, stop=True)
        return ins
    k.op(PE, f_cg, reads=[b_sguA, b_sguB], writes=[b_MM[0], b_MM[1]])
    k.op(DVE, lambda e: e.tensor_copy(out=cg[:, 0:4, :].rearrange("p g t -> p (g t)"), in_=MMt[0][:]), reads=[b_MM[0]], writes=[b_cg])
    k.op(DVE, lambda e: e.tensor_copy(out=cg[:, 4:8, :].rearrange("p g t -> p (g t)"), in_=MMt[1][:]), reads=[b_MM[1]], writes=[b_cg])

    if stop == 0:
        return finish()
    stf = [R4.rearrange("p k n -> p (k n)").bitcast(F32), R5.rearrange("p k n -> p (k n)").bitcast(F32), RR[:, 0:4096].bitcast(F32)]
    stb = [RR[:, 4096:6144], RR[:, 6144:8192]]
    b_stf = [Buf(f"stf{i}") for i in range(3)]
    overlap(b_stf[0], b_R4)
    overlap(b_stf[1], b_R5)
    overlap(b_stf[2], b_U, b_FACT)
    overlap(b_stb[0], b_VLN, b_QP, b_FACT)
    overlap(b_stb[1], b_VLN, b_KP, b_FACT)
    conv = []

    def conv_win(c0):
        for kc in range(8):
            conv.append((w_in[kc * 128:(kc + 1) * 128, c0:c0 + 2048], wb_in[kc * 128:(kc + 1) * 128, c0:c0 + 2048], cols[:, kc:kc + 1], "in"))
    conv_win(2048)
    conv_win(4096)
    n_phase0 = len(conv)
    conv_win(0)
    conv_win(6144)
    for nm, src, dst in (("pa", w_pa, wb_pa), ("pb", w_pb, wb_pb), ("out", w_out, wb_out)):
        for kc in range(8):
            conv.append((src[kc * 128:(kc + 1) * 128, :], dst[kc * 128:(kc + 1) * 128, :], None, nm))
    for kc in range(8):
        for c0, cw in ((0, 2048), (2048, 2048), (4096, 1536)):
            conv.append((w_up[kc * 128:(kc + 1) * 128, c0:c0 + cw], wb_up[kc * 128:(kc + 1) * 128, c0:c0 + cw], cols[:, 8 + kc:9 + kc], "up"))
    for kc in range(22):
        conv.append((w_dn[kc * 128:(kc + 1) * 128, :], wb_dn[kc * 128:(kc + 1) * 128, :], None, "dn"))
    nconv = len(conv)
    cv = {"next": 0, "loaded": 0, "stored": 0}

    def conv_load(i):
        src = conv[i][0]
        w = src.shape[1]
        k.dma(SP, d_stl[i % 3], stf[i % 3][:, 0:w], src, writes=[b_stf[i % 3]])

    def conv_cast(i):
        src, dst, gc, nm = conv[i]
        w = src.shape[1]
        j = i % 2
        if i % 2 == 0:
            if gc is None:
                k.op(DVE, lambda e: e.tensor_copy(out=stb[j][:, 0:w], in_=stf[i % 3][:, 0:w]), reads=[b_stf[i % 3]], writes=[b_stb[j]])
            else:
                k.op(DVE, lambda e: e.tensor_scalar(out=stb[j][:, 0:w], in0=stf[i % 3][:, 0:w], scalar1=gc, scalar2=None, op0=ALU.mult),
                     reads=[b_stf[i % 3], b_cols], writes=[b_stb[j]])
        else:
            if gc is None:
                k.op(ACT, lambda e: e.activation(out=stb[j][:, 0:w], in_=stf[i % 3][:, 0:w], func=AF.Copy), reads=[b_stf[i % 3]], writes=[b_stb[j]])
            else:
                k.op(ACT, lambda e: e.activation(out=stb[j][:, 0:w], in_=stf[i % 3][:, 0:w], func=AF.Copy, scale=gc),
                     reads=[b_stf[i % 3], b_cols], writes=[b_stb[j]])

    def conv_store(i):
        src, dst, gc, nm = conv[i]
        w = src.shape[1]
        j = i % 2
        k.dma(ACT, d_conv[j], dst, stb[j][:, 0:w], reads=[b_stb[j]], writes=[b_wb[nm]])

    def conv_run(upto, flush=False):
        upto = min(upto, nconv)
        while cv["next"] < upto:
            i = cv["next"]
            while cv["loaded"] < min(i + 3, nconv):
                conv_load(cv["loaded"])
                cv["loaded"] += 1
            conv_cast(i)
            if i - 1 >= cv["stored"]:
                conv_store(cv["stored"])
                cv["stored"] += 1
            cv["next"] += 1
        if flush:
            while cv["stored"] < cv["next"]:
                conv_store(cv["stored"])
                cv["stored"] += 1

    conv_run(n_phase0, flush=True)

    if stop == 1:
        return finish()
    ring_ctr = [0]

    def ring_slot():
        i = ring_ctr[0] % 4
        ring_ctr[0] += 1
        return i

    def slot_view(i, kk):
        return RING[:, i * 5632:i * 5632 + kk * 512].rearrange("p (k n) -> p k n", n=512)

    def load_k8(wb, nm, c0):
        i = ring_slot()
        k.dma(SP, d_ring[i], slot_view(i, 8), wb[:, c0:c0 + 512].rearrange("(k p) n -> p k n", p=128),
              reads=[b_wb[nm]], writes=[b_ring[i]])
        return i

    def load_up(blk):
        i = ring_slot()
        v = slot_view(i, 8)
        j = 2 * blk
        k.dma(SP, d_ring[i], v[:, :, 0:256], wb_up[:, j * 128:j * 128 + 256].rearrange("(k p) n -> p k n", p=128),
              reads=[b_wb["up"]], writes=[b_ring[i]])
        k.dma(SP, d_ring[i], v[:, :, 256:512], wb_up[:, FH + j * 128:FH + j * 128 + 256].rearrange("(k p) n -> p k n", p=128),
              reads=[b_wb["up"]], writes=[b_ring[i]])
        return i

    def load_dn(kh, nh):
        i = ring_slot()
        k.dma(SP, d_ring[i], slot_view(i, 11), wb_dn[kh * 1408:(kh + 1) * 1408, nh * 512:(nh + 1) * 512].rearrange("(k p) n -> p k n", p=128),
              reads=[b_wb["dn"]], writes=[b_ring[i]])
        return i

    mm_ctr = [0, 0]
    MMt.extend([OTt[0], OTt[1], STt2[0].rearrange("p q v -> p (q v)"), STt2[1].rearrange("p q v -> p (q v)")])
    b_MM.extend([b_OT[0], b_OT[1], b_ST2[0], b_ST2[1]])

    def mm_bank(wide=False):
        if wide:
            i = mm_ctr[1] % 7
            mm_ctr[1] += 1
        else:
            i = mm_ctr[0] % 3
            mm_ctr[0] += 1
        return i

    ta_ctr = [0]

    def ta():
        i = ta_ctr[0] % 3
        ta_ctr[0] += 1
        return TA[i], b_TA[i]

    st_ctr = [0]
    misc = {"stat": 0, "hb": 0, "bt": 0, "kppT": 0, "sc": 0, "ot": 0}

    def nxt(name, n):
        v = misc[name] % n
        misc[name] += 1
        return v

    b_stats = [Buf(f"stat{i}") for i in range(32)]

    def stat_cols(n):
        i = misc["stat"] % 32
        misc["stat"] += 1
        return stat[:, 2 * i:2 * i + n], b_stats[i]

    def fm_group(slot, chunk, rhs, rbufs, wide=True):
        bi = mm_bank(wide)
        v = slot_view(slot, 8)

        def f(e):
            ins = None
            for kc in range(8):
                ins = e.matmul(MMt[bi][:], lhsT=v[:, kc, chunk * 128:(chunk + 1) * 128], rhs=rhs[:, kc, :], start=(kc == 0), stop=(kc == 7))
            return ins
        k.op(PE, f, reads=[b_ring[slot]] + rbufs, writes=[b_MM[bi]])
        return bi

    def tm_group(slot, lhs, s, rbufs, wide=True):
        bi = mm_bank(wide)
        v = slot_view(slot, 8)

        def f(e):
            ins = None
            for kc in range(8):
                ins = e.matmul(MMt[bi][:], lhsT=lhs[:, kc, s * 128:(s + 1) * 128], rhs=v[:, kc, :], start=(kc == 0), stop=(kc == 7))
            return ins
        k.op(PE, f, reads=[b_ring[slot]] + rbufs, writes=[b_MM[bi]])
        return bi

    def rstd_small(ss_ap, ss_buf, n, scale):
        m, bm = stat_cols(n)
        r, br = stat_cols(n)
        k.op(POOL, lambda e: e.tensor_scalar(out=m, in0=ss_ap, scalar1=scale, scalar2=EPS, op0=ALU.mult, op1=ALU.add),
             reads=[ss_buf], writes=[bm])
        k.op(POOL, lambda e: e.tensor_tensor(out=r, in0=m, in1=NEGH, op=ALU.pow),
             reads=[bm, b_der], writes=[br])
        return r, br

    def norm_to_hT(s):
        norm_b(norm_a(s))

    def norm_a(s, src=None, bsrc=None):
        if src is None:
            src, bsrc = X[s][:], b_X[s]
        ss, bss = stat_cols(1)
        k.op(ACT, lambda e: e.activation(out=junk[:], in_=src, func=AF.Square, accum_out=ss),
             reads=[bsrc], writes=[b_junk, bss])
        r, br = rstd_small(ss, bss, 1, 1.0 / D)
        hi = nxt("hb", 2)
        k.op(DVE, lambda e: e.tensor_scalar(out=hb[hi][:], in0=src, scalar1=r, scalar2=None, op0=ALU.mult),
             reads=[bsrc, br], writes=[b_hb[hi]])
        return (s, hi)

    def norm_b(sh):
        s, hi = sh

        def f(e):
            ins = None
            for kc in range(8):
                ins = e.transpose(TPt[:, kc, :], hb[hi][:, kc * 128:(kc + 1) * 128], identb[:])
            return ins
        k.op(PE, f, reads=[b_hb[hi], b_identb], writes=[b_TP])
        k.op(DVE, lambda e: e.tensor_copy(out=HT[:, :, s * 128:(s + 1) * 128], in_=TPt[:]), reads=[b_TP], writes=[b_HT])

    def load_x(src, t, s):
        k.dma(SP, d_xl[s], X[s][:], src[t * T + s * 128:t * T + (s + 1) * 128, :], writes=[b_X[s]])

    def sbf_slot(gt, j):
        return (8 * gt + j) % 9

    def hgrn_group(gi, gt, main, last_pre=False):
        heads = [4 * gi + i for i in range(4)]
        sl_f = load_k8(wb_in, "in", 3072 + gi * 512)
        sl_i = load_k8(wb_in, "in", 4096 + gi * 512)
        if main:
            sl_g = load_k8(wb_in, "in", 5120 + gi * 512)
            sl_q = load_k8(wb_in, "in", 2048 + gi * 512)
        for hl, h in enumerate(heads):
            bi = fm_group(sl_f, hl, HT, [b_HT])
            z = MMt[bi]
            E, bE = ta()
            k.op(ACT, lambda e: e.activation(out=E[:], in_=z[:], func=AF.Exp, scale=-1.0), reads=[b_MM[bi]], writes=[bE])
            k.op(ACT, lambda e: e.activation(out=P2[hl][:], in_=E[:], func=AF.Ln, scale=LB(h), bias=1.0), reads=[bE, b_der], writes=[b_P2[hl]])
            k.op(ACT, lambda e: e.activation(out=P3[hl][:], in_=E[:], func=AF.Ln, scale=1.0, bias=1.0), reads=[bE], writes=[b_P3[hl]])
            k.op(DVE, lambda e: e.scalar_tensor_tensor(out=P1[hl][:], in0=z[:], scalar=-1.0, in1=P3[hl][:], op0=ALU.mult, op1=ALU.subtract),
                 reads=[b_MM[bi], b_P3[hl]], writes=[b_P1[hl]])
        for hl, h in enumerate(heads):
            k.op(DVE, lambda e: e.tensor_tensor(out=P2[hl][:], in0=P2[hl][:], in1=P3[hl][:], op=ALU.subtract),
                 reads=[b_P2[hl], b_P3[hl]], writes=[b_P2[hl]])
        for hl, h in enumerate(heads):
            B, bB = Bt[hl], b_Bt[hl]
            if main:
                k.op(DVE, lambda e: e.tensor_tensor_scan(out=B[:], data0=mask_scan, data1=P2[hl][:], initial=0.0, op0=ALU.mult, op1=ALU.add),
                     reads=[b_cst, b_P2[hl]], writes=[bB])
            else:
                k.op(DVE, lambda e: e.tensor_tensor_scan(out=B[:], data0=ones512, data1=P2[hl][:], initial=0.0, op0=ALU.mult, op1=ALU.add),
                     reads=[b_onesf, b_P2[hl]], writes=[bB])
        for hl, h in enumerate(heads):
            B, bB = Bt[hl], b_Bt[hl]
            k.op(DVE, lambda e: e.tensor_tensor(out=P1[hl][:], in0=P1[hl][:], in1=B[:], op=ALU.subtract),
                 reads=[b_P1[hl], bB], writes=[b_P1[hl]])
            if main:
                B3 = B.rearrange("p (c l) -> p c l", l=64)
                k.op(POOL, lambda e: e.tensor_tensor(out=P3[hl].rearrange("p (c l) -> p c l", l=64), in0=P1[hl].rearrange("p (c l) -> p c l", l=64),
                                                     in1=B3[:, :, 63:64].to_broadcast([128, 8, 64]), op=ALU.add),
                     reads=[b_P1[hl], bB], writes=[b_P3[hl]])
            else:
                k.op(POOL, lambda e: e.tensor_tensor(out=bia[:, hl:hl + 1], in0=B[:, 511:512], in1=LNOML(h), op=ALU.add),
                     reads=[bB, b_der], writes=[b_bia[hl]])
        if main:
            for hp in range(0, 4, 2):
                st = []
                for hl in (hp, hp + 1):
                    bi = fm_group(sl_g, hl, HT, [b_HT])
                    E, bE = ta()
                    st.append((hl, heads[hl], bi, E, bE))
                    k.op(ACT, lambda e: e.activation(out=E[:], in_=MMt[bi][:], func=AF.Exp, scale=-1.0), reads=[b_MM[bi]], writes=[bE])
                for hl, h, bi, E, bE in st:
                    k.op(ACT, lambda e: e.activation(out=E[:], in_=E[:], func=AF.Ln, bias=1.0), reads=[bE], writes=[bE])
                for hl, h, bi, E, bE in st:
                    k.op(ACT, lambda e: e.activation(out=E[:], in_=E[:], func=AF.Exp, scale=-1.0), reads=[bE], writes=[bE])
                for hl, h, bi, E, bE in st:
                    k.op(DVE, lambda e: e.tensor_tensor(out=R5[:, h, :], in0=MMt[bi][:], in1=E[:], op=ALU.mult), reads=[b_MM[bi], bE], writes=[b_R5])
        for s in range(4):
            bi = tm_group(sl_i, HT, s, [b_HT])
            k.op(DVE, lambda e: e.tensor_copy(out=VI[:, s, :], in_=MMt[bi][:]), reads=[b_MM[bi]], writes=[b_VI])
        for hl, h in enumerate(heads):
            B = Bt[hl]
            if main:
                B3 = B.rearrange("p (c l) -> p c l", l=64)
                k.op(ACT, lambda e: e.activation(out=KP[:, hl, :], in_=P1[hl][:], func=AF.Exp, bias=LNOML(h)), reads=[b_P1[hl], b_der], writes=[b_KP])
                k.op(ACT, lambda e: e.activation(out=kppT[hl][:], in_=P3[hl][:], func=AF.Exp, bias=LNOML(h)), reads=[b_P3[hl], b_der], writes=[b_kppT[hl]])
                k.op(ACT, lambda e: e.activation(out=decs[:, h, :], in_=B3[:, :, 63], func=AF.Exp), reads=[b_Bt[hl]], writes=[b_decs[h]])
            else:
                k.op(ACT, lambda e: e.activation(out=kppT[hl][:], in_=P1[hl][:], func=AF.Exp, bias=bia[:, hl:hl + 1]), reads=[b_P1[hl], b_bia[hl]], writes=[b_kppT[hl]])
                k.op(ACT, lambda e: e.activation(out=decs[:, h, 0:1], in_=B[:, 511:512], func=AF.Exp), reads=[b_Bt[hl]], writes=[b_decs[h]])

        if main:
            for hp in range(0, 4, 2):
                st = []
                for hl in (hp, hp + 1):
                    bi = fm_group(sl_q, hl, HT, [b_HT])
                    E, bE = ta()
                    st.append((hl, heads[hl], bi, E, bE))
                    k.op(ACT, lambda e: e.activation(out=E[:], in_=MMt[bi][:], func=AF.Exp, scale=-1.0), reads=[b_MM[bi]], writes=[bE])
                for hl, h, bi, E, bE in st:
                    k.op(ACT, lambda e: e.activation(out=E[:], in_=E[:], func=AF.Ln, bias=1.0), reads=[bE], writes=[bE])
                    k.op(DVE, lambda e: e.tensor_tensor(out=E[:], in0=Bt[hl][:], in1=E[:], op=ALU.subtract), reads=[bE, b_Bt[hl]], writes=[bE])
                for hl, h, bi, E, bE in st:
                    k.op(ACT, lambda e: e.activation(out=E[:], in_=E[:], func=AF.Exp), reads=[bE], writes=[bE])
                for hl, h, bi, E, bE in st:
                    k.op(DVE, lambda e: e.tensor_tensor(out=QP[:, hl, :], in0=MMt[bi][:], in1=E[:], op=ALU.mult), reads=[b_MM[bi], bE], writes=[b_QP])
        for hl, h in enumerate(heads):
            def ftp(e):
                ins = None
                for s in range(4):
                    ins = e.transpose(TPt[:, s, :], kppT[hl][:, s * 128:(s + 1) * 128], identb[:])
                return ins
            k.op(PE, ftp, reads=[b_kppT[hl], b_identb], writes=[b_TP])
            k.op(DVE, lambda e: e.tensor_copy(out=KPP[:, :, hl * 128:(hl + 1) * 128], in_=TPt[:, 0:4, :]), reads=[b_TP], writes=[b_KPP])
        if not main:
            def fst(e):
                ins = None
                for hl in range(4):
                    for s in range(4):
                        ins = e.matmul(STt2[0][:, hl, :], lhsT=KPP[:, s, hl * 128:(hl + 1) * 128],
                                       rhs=VI[:, s, hl * 128:(hl + 1) * 128], start=(s == 0), stop=(s == 3))
                return ins
            k.op(PE, fst, reads=[b_KPP, b_VI], writes=[b_ST2[0]])
            for hl, h in enumerate(heads):
                k.op(DVE, lambda e: e.scalar_tensor_tensor(out=S32[:, h, :], in0=S32[:, h, :], scalar=decs[:, h, 0:1],
                                                           in1=STt2[0][:, hl, :], op0=ALU.mult, op1=ALU.add),
                     reads=[b_S32[h], b_decs[h], b_ST2[0]], writes=[b_S32[h]])
                if last_pre:
                    k.op(POOL, lambda e: e.tensor_copy(out=SBF[:, h, 0, :], in_=S32[:, h, :]), reads=[b_S32[h]], writes=[b_SBF[h]])
            return

    def hgrn_pass1(gi, gt):
        heads = [4 * gi + i for i in range(4)]
        for j in range(8):
            s, pb = j // 2, (j % 2) * 64
            sb_i = j % 2

            def fst(e):
                ins = None
                for hl in range(4):
                    ins = e.matmul(STt2[sb_i][:, hl, :], lhsT=KPP[pb:pb + 64, s, hl * 128:(hl + 1) * 128],
                                   rhs=VI[pb:pb + 64, s, hl * 128:(hl + 1) * 128], start=True, stop=True)
                return ins
            k.op(PE, fst, reads=[b_KPP, b_VI], writes=[b_ST2[sb_i]])
            for hl, h in enumerate(heads):
                k.op(DVE, lambda e: e.scalar_tensor_tensor(out=S32[:, h, :], in0=S32[:, h, :], scalar=decs[:, h, j:j + 1],
                                                           in1=STt2[sb_i][:, hl, :], op0=ALU.mult, op1=ALU.add),
                     reads=[b_S32[h], b_decs[h], b_ST2[sb_i]], writes=[b_S32[h]])
                so = sbf_slot(gt, j + 1)
                if hl % 2 == 0:
                    k.op(ACT, lambda e: e.activation(out=SBF[:, h, so, :], in_=S32[:, h, :], func=AF.Copy), reads=[b_S32[h]], writes=[b_SBF[h]])
                else:
                    k.op(POOL, lambda e: e.tensor_copy(out=SBF[:, h, so, :], in_=S32[:, h, :]), reads=[b_S32[h]], writes=[b_SBF[h]])
            yield

    def hgrn_pass2(gi, gt):
        heads = [4 * gi + i for i in range(4)]

        def sc(hl):
            ci = hl % 2
            SCv = STt2[ci].rearrange("p q v -> p (q v)")

            def fsc(e):
                ins = None
                for j in range(8):
                    pb = (j % 2) * 64
                    ins = e.matmul(SCv[pb:pb + 64, (j // 2) * 64:(j // 2 + 1) * 64], lhsT=KP[:, hl, j * 64:(j + 1) * 64],
                                   rhs=QP[:, hl, j * 64:(j + 1) * 64], start=True, stop=True)
                return ins
            k.op(PE, fsc, reads=[b_KP, b_QP], writes=[b_ST2[ci]])
            k.op(DVE, lambda e: e.tensor_tensor(out=scm[ci][:], in0=SCv[:, 0:256], in1=mask_sc, op=ALU.mult),
                 reads=[b_ST2[ci], b_cst], writes=[b_scm[ci]])

        sc(0)
        sc(1)
        yield
        for hl, h in enumerate(heads):
            ci = hl % 2
            oi = hl % 2

            def fo(e):
                ins = None
                for j in range(8):
                    s, pb = j // 2, (j % 2) * 64
                    si = sbf_slot(gt, j)
                    e.matmul(OTt[oi][:, j * 64:(j + 1) * 64], lhsT=VI[pb:pb + 64, s, hl * 128:(hl + 1) * 128],
                             rhs=scm[ci][pb:pb + 64, s * 64:(s + 1) * 64], start=True, stop=False)
                    ins = e.matmul(OTt[oi][:, j * 64:(j + 1) * 64], lhsT=SBF[:, h, si, :], rhs=QP[:, hl, j * 64:(j + 1) * 64],
                                   start=False, stop=True)
                return ins
            k.op(PE, fo, reads=[b_VI, b_scm[ci], b_SBF[h], b_QP], writes=[b_OT[oi]])
            OC, bOC = P1[hl], b_P1[hl]
            k.op(DVE, lambda e: e.tensor_copy(out=OC[:], in_=OTt[oi][:]), reads=[b_OT[oi]], writes=[bOC])
            SQ, bSQ = P2[hl], b_P2[hl]
            k.op(ACT, lambda e: e.activation(out=SQ[:], in_=OC[:], func=AF.Square), reads=[bOC], writes=[bSQ])
            if hl + 2 < 4:
                sc(hl + 2)
            yield
        for hl, h in enumerate(heads):
            OC, bOC = P1[hl], b_P1[hl]
            SQ, bSQ = P2[hl], b_P2[hl]
            bi = mm_bank()
            k.op(PE, lambda e: e.matmul(MMt[bi][:], lhsT=onesf[:], rhs=SQ[:], start=True, stop=True), reads=[b_onesf, bSQ], writes=[b_MM[bi]])
            k.op(ACT, lambda e: e.activation(out=SQ[:], in_=MMt[bi][:], func=AF.Ln, scale=1.0 / 128, bias=EPS), reads=[b_MM[bi]], writes=[bSQ])
            k.op(ACT, lambda e: e.activation(out=SQ[:], in_=SQ[:], func=AF.Exp, scale=-0.5), reads=[bSQ], writes=[bSQ])
            k.op(DVE, lambda e: e.scalar_tensor_tensor(out=SQ[:], in0=OC[:], scalar=cols[:, 24 + h:25 + h], in1=SQ[:],
                                                       op0=ALU.mult, op1=ALU.mult), reads=[bOC, b_cols, bSQ], writes=[bSQ])
            k.op(POOL, lambda e: e.tensor_tensor(out=R5[:, h, :], in0=SQ[:], in1=R5[:, h, :], op=ALU.mult), reads=[bSQ, b_R5], writes=[b_R5])
            if hl < 3:
                yield

    def interleave(gen, fillers):
        fillers = list(fillers)
        for _ in gen:
            if fillers:
                fillers.pop(0)()
        for f in fillers:
            f()

    def out_proj(loader, nk, lhs, lbufs, gain_off, final, t, after_nh0=None):
        nkh = nk // 8 if nk == 8 else 2
        kper = 8 if nk == 8 else 11
        pend = []
        ssq_ = [stat_cols(2) for _ in range(4)]
        ssq = [a for a, _ in ssq_]
        bssq = [b for _, b in ssq_]
        s_outer = (nk == 8)
        slots_all = {}
        if s_outer:
            for nh in range(2):
                slots_all[nh] = [loader(kh, nh) for kh in range(nkh)]
            order = [(s, nh) for s in range(4) for nh in range(2)]
        else:
            order = [(s, nh) for nh in range(2) for s in range(4)]
        for (s, nh) in order:
            if nh not in slots_all:
                slots_all[nh] = [loader(kh, nh) for kh in range(nkh)]
            slots = slots_all[nh]
            bi = (2 * s + nh) % 4 if s_outer else s
            if bi == 3:
                ptile, pbuf = STt.rearrange("p q v -> p (q v)"), b_ST
            else:
                ptile, pbuf = MMt[bi], [b_MM[bi]]

            def f(e):
                ins = None
                n = 0
                for kh in range(nkh):
                    v = slot_view(slots[kh], kper)
                    for kc in range(kper):
                        ins = e.matmul(ptile[:], lhsT=lhs[:, kh * kper + kc, s * 128:(s + 1) * 128], rhs=v[:, kc, :],
                                       start=(n == 0), stop=(n == nkh * kper - 1))
                        n += 1
                return ins
            k.op(PE, f, reads=[b_ring[sl] for sl in slots] + lbufs, writes=pbuf)
            k.op(ACT, lambda e: e.activation(out=junk[:, 0:512], in_=ptile[:], func=AF.Square, accum_out=ssq[s][:, nh:nh + 1]),
                 reads=pbuf, writes=[b_junk, bssq[s]])
            park = BT[s // 2][:, (s % 2) * 512:(s % 2 + 1) * 512]
            if nh == 0:
                k.op(ACT, lambda e: e.activation(out=park, in_=ptile[:], func=AF.Copy), reads=pbuf, writes=[b_BT[s // 2]])
                if final and t + 1 < ntiles_main:
                    if len(pend) == 2:
                        norm_b(pend.pop(0))
                    pend.append(norm_a(s, Xn[s], b_Xn[s]))
                    if s == 3:
                        while pend:
                            norm_b(pend.pop(0))
                if s == 3 and after_nh0 is not None:
                    if 1 not in slots_all:
                        slots_all[1] = [loader(kh, 1) for kh in range(nkh)]
                    after_nh0()
                continue
            tot, btot = stat_cols(1)
            k.op(POOL, lambda e: e.tensor_tensor(out=tot, in0=ssq[s][:, 0:1], in1=ssq[s][:, 1:2], op=ALU.add),
                 reads=[bssq[s]], writes=[btot])
            r, br = rstd_small(tot, btot, 1, 1.0 / D)
            tA, btA = ta()
            k.op(DVE, lambda e: e.scalar_tensor_tensor(out=tA[:], in0=ptile[:], scalar=r, in1=rows[:, gain_off + 512:gain_off + 1024],
                                                       op0=ALU.mult, op1=ALU.mult), reads=pbuf + [br, b_rows], writes=[btA])
            tB, btB = ta()
            k.op(DVE, lambda e: e.scalar_tensor_tensor(out=tB[:], in0=park, scalar=r, in1=rows[:, gain_off:gain_off + 512],
                                                       op0=ALU.mult, op1=ALU.mult), reads=[b_BT[s // 2], br, b_rows], writes=[btB])
            if final:
                k.op(POOL, lambda e: e.tensor_tensor(out=Y[:, 512:1024], in0=X[s][:, 512:1024], in1=tA[:], op=ALU.add),
                     reads=[btA, b_X[s]], writes=[b_Y])
                k.op(POOL, lambda e: e.tensor_tensor(out=Y[:, 0:512], in0=X[s][:, 0:512], in1=tB[:], op=ALU.add),
                     reads=[btB, b_X[s]], writes=[b_Y])
                k.dma(SP, d_y, y_out[t * T + s * 128:t * T + (s + 1) * 128, :], Y, reads=[b_Y])
                if t + 1 < ntiles_main:
                    k.op(DVE, lambda e: e.tensor_copy(out=X[s][:], in_=Xn[s]), reads=[b_Xn[s]], writes=[b_X[s]])
            else:
                k.op(POOL, lambda e: e.tensor_tensor(out=X[s][:, 512:1024], in0=X[s][:, 512:1024], in1=tA[:], op=ALU.add),
                     reads=[btA, b_X[s]], writes=[b_X[s]])
                k.op(POOL, lambda e: e.tensor_tensor(out=X[s][:, 0:512], in0=X[s][:, 0:512], in1=tB[:], op=ALU.add),
                     reads=[btB, b_X[s]], writes=[b_X[s]])
                if len(pend) == 2:
                    norm_b(pend.pop(0))
                pend.append(norm_a(s))
        while pend:
            norm_b(pend.pop(0))

    gt = 0
    if ntiles_pre:
        for s in range(4):
            load_x(x_prev, 0, s)
    else:
        for s in range(4):
            load_x(x_main, 0, s)
    for t in range(ntiles_pre):
        for s in range(4):
            norm_to_hT(s)
        for s in range(4):
            if t + 1 < ntiles_pre:
                load_x(x_prev, t + 1, s)
            else:
                load_x(x_main, 0, s)
        for gi in range(2):
            hgrn_group(gi, 0, False, last_pre=(t == ntiles_pre - 1))
        conv_run(n_phase0 + (t + 1) * (-(-(nconv - n_phase0) // max(ntiles_pre, 1))))
    conv_run(nconv, flush=True)

    pre_v = [None]
    for t in range(ntiles_main):
        if t == 0:
            for s in range(4):
                norm_to_hT(s)
        if stop == 2:
            return finish()
        if pre_v[0] is not None:
            sl_v = pre_v[0]
            pre_v[0] = None
        else:
            sl_v = [load_k8(wb_in, "in", c0) for c0 in (1024, 1536)]
        sl_u = [load_k8(wb_in, "in", c0) for c0 in (0, 512)]
        for s in range(4):
            gi_ = nxt("bt", 2)
            gv, bgv = BT[gi_], b_BT[gi_]
            sm, bsm = stat_cols(2)
            for hf in range(2):
                bi = tm_group(sl_v[hf], HT, s, [b_HT])
                k.op(ACT, lambda e: e.activation(out=gv[:, hf * 512:(hf + 1) * 512], in_=MMt[bi][:], func=AF.Gelu, accum_out=sm[:, hf:hf + 1]),
                     reads=[b_MM[bi]], writes=[bgv, bsm])
            nm0, bnm0 = stat_cols(1)
            nm_, bnm = stat_cols(1)
            k.op(POOL, lambda e: e.tensor_tensor(out=nm0, in0=sm[:, 0:1], in1=sm[:, 1:2], op=ALU.add), reads=[bsm], writes=[bnm0])
            k.op(POOL, lambda e: e.tensor_scalar(out=nm_, in0=nm0, scalar1=-1.0 / D, scalar2=1.0, op0=ALU.mult, op1=ALU.mult),
                 reads=[bnm0], writes=[bnm])
            vs, bvs = stat_cols(1)
            k.op(ACT, lambda e: e.activation(out=junk[:], in_=gv[:], func=AF.Square, bias=nm_, accum_out=vs),
                 reads=[bgv, bnm], writes=[b_junk, bvs])
            r, br = rstd_small(vs, bvs, 1, 1.0 / D)
            k.op(DVE, lambda e: e.tensor_scalar(out=VLN[:, s, :], in0=gv[:], scalar1=nm_, scalar2=r, op0=ALU.add, op1=ALU.mult),
                 reads=[bgv, bnm, br], writes=[b_VLN])
        for c in range(8):
            bi = fm_group(sl_u[c // 4], c % 4, HT, [b_HT])
            k.op(ACT, lambda e: e.activation(out=U[:, c, :], in_=MMt[bi][:], func=AF.Gelu), reads=[b_MM[bi]], writes=[b_U])
        for g in range(8):
            bi = mm_bank()

            def fs(e):
                ins = None
                for n in range(4):
                    ins = e.matmul(MMt[bi][:, n * 128:(n + 1) * 128], lhsT=VLN[:, n, g * 128:(g + 1) * 128], rhs=wspb[:, g, :], start=True, stop=True)
                return ins
            k.op(PE, fs, reads=[b_VLN, b_wspb], writes=[b_MM[bi]])
            tA, btA = ta()
            k.op(DVE, lambda e: e.scalar_tensor_tensor(out=tA.rearrange("p (n t) -> p n t", t=128), in0=MMt[bi].rearrange("p (n t) -> p n t", t=128),
                                                       scalar=cols[:, 16 + g:17 + g], in1=cg[:, g:g + 1, :].to_broadcast([128, 4, 128]),
                                                       op0=ALU.mult, op1=ALU.add), reads=[b_MM[bi], b_cols, b_cg], writes=[btA])
            k.op(POOL, lambda e: e.tensor_tensor(out=U[:, g, :], in0=tA[:], in1=U[:, g, :], op=ALU.mult), reads=[btA, b_U], writes=[b_U])
        if stop == 3:
            return finish()
        hgrn_group(0, gt, True)
        sl_ga = [load_k8(wb_in, "in", c0) for c0 in (6144, 6656)]

        def ga_item(c):
            bi = fm_group(sl_ga[c // 4], c % 4, HT, [b_HT], wide=False)
            k.op(ACT, lambda e: e.activation(out=R4[:, c, :], in_=MMt[bi][:], func=AF.Sigmoid), reads=[b_MM[bi]], writes=[b_R4])
        interleave(hgrn_pass1(0, gt), [(lambda c=c: ga_item(c)) for c in range(8)])
        sl_pa = [load_k8(wb_pa, "pa", c0) for c0 in (0, 512)]

        def pa_item(c):
            bi = fm_group(sl_pa[c // 4], c % 4, U, [b_U], wide=False)
            k.op(DVE, lambda e: e.tensor_tensor(out=R4[:, c, :], in0=MMt[bi][:], in1=R4[:, c, :], op=ALU.mult), reads=[b_MM[bi], b_R4], writes=[b_R4])
        interleave(hgrn_pass2(0, gt), [(lambda c=c: pa_item(c)) for c in range(8)])
        hgrn_group(1, gt, True)
        sl_gb = [load_k8(wb_in, "in", c0) for c0 in (7168, 7680)]

        def gb_item(c):
            bi = fm_group(sl_gb[c // 4], c % 4, HT, [b_HT], wide=False)
            k.op(ACT, lambda e: e.activation(out=GB[:, c, :], in_=MMt[bi][:], func=AF.Sigmoid), reads=[b_MM[bi]], writes=[b_U])
        interleave(hgrn_pass1(1, gt), [(lambda c=c: gb_item(c)) for c in range(8)])
        interleave(hgrn_pass2(1, gt), [])
        gt += 1
        sl_pb = [load_k8(wb_pb, "pb", c0) for c0 in (0, 512)]
        for c in range(8):
            bi = fm_group(sl_pb[c // 4], c % 4, R5, [b_R5])
            tA, btA = ta()
            k.op(DVE, lambda e: e.tensor_tensor(out=tA[:], in0=MMt[bi][:], in1=GB[:, c, :], op=ALU.mult), reads=[b_MM[bi], b_U], writes=[btA])
            k.op(POOL, lambda e: e.tensor_tensor(out=R4[:, c, :], in0=tA[:], in1=R4[:, c, :], op=ALU.add), reads=[btA, b_R4], writes=[b_R4])
        if stop == 6:
            return finish()
        out_proj(lambda kh, nh: load_k8(wb_out, "out", nh * 512), 8, R4, [b_R4], 0, False, t)
        if stop == 7:
            return finish()
        if t + 1 < ntiles_main:
            for s in range(4):
                k.dma(SP, d_xl[s], Xn[s], x_main[(t + 1) * T + s * 128:(t + 1) * T + (s + 1) * 128, :], writes=[b_Xn[s]])
        for blk in range(11):
            sl = load_up(blk)
            for jj in range(2):
                bg = fm_group(sl, jj, HT, [b_HT])
                bu = fm_group(sl, 2 + jj, HT, [b_HT])
                tA, btA = ta()
                k.op(ACT, lambda e: e.activation(out=tA[:], in_=MMt[bg][:], func=AF.Silu), reads=[b_MM[bg]], writes=[btA])
                k.op(DVE, lambda e: e.tensor_tensor(out=FACT[:, 2 * blk + jj, :], in0=MMt[bu][:], in1=tA[:], op=ALU.mult),
                     reads=[b_MM[bu], btA], writes=[b_FACT])
        def prefetch_v():
            if t + 1 < ntiles_main:
                pre_v[0] = [load_k8(wb_in, "in", c0) for c0 in (1024, 1536)]
        out_proj(load_dn, 22, FACT, [b_FACT], D, True, t, after_nh0=prefetch_v)

    return finish()


def _unused():
    for s in range(4):
        if d_xs[s].cnt:
            nc.sync.wait_ge(d_xs[s].sem, d_xs[s].cnt)
    nc.all_engine_barrier()
    for sm in k.sems:
        nc.gpsimd.sem_clear(sm)
    return nc


def _host_inputs(inp):
    f = lambda a: np.ascontiguousarray(np.asarray(a, dtype=np.float32))
    x = f(inp["x"])
    col = lambda v: f(v).reshape(8, 128).T
    cols = np.concatenate([col(inp["pre_mix_gain"][0]), col(inp["pre_ffn_gain"][0]), col(inp["sgu_norm_gain"][0]),
                           col(inp["hgrn_norm_gain"][0]), col(inp["lb_logits"][0]), col(inp["lb_logits"][1])], axis=1)
    rows = np.concatenate([np.broadcast_to(f(inp["post_mix_gain"][0])[None, :], (128, D)),
                           np.broadcast_to(f(inp["post_ffn_gain"][0])[None, :], (128, D))], axis=1)
    wspT = f(inp["w_spatial"][0]).transpose(2, 0, 1).reshape(128, 1024)
    ident = np.eye(128, dtype=np.float32)
    p = np.arange(128)
    tt = np.arange(64)
    msc = (p[:, None] % 64 <= tt[None, :]).astype(np.float32)
    mask_sc = np.tile(msc, (1, 4))
    mask_scan = np.ones((128, 512), np.float32)
    mask_scan[:, 0::64] = 0.0
    mask_sguT = (p[:, None] // 64 <= p[None, :] // 64).astype(np.float32)
    consts = np.concatenate([ident, mask_sc, mask_scan, mask_sguT], axis=1)
    sguA = np.stack([f(inp["sgu_norm_bias"][0]), np.ones(D, np.float32)], axis=0)
    bsp = f(inp["b_spatial"][0]).reshape(1, D)
    shared = {
        "w_in": f(inp["w_in"][0]), "w_pa": f(inp["w_proj_sgu"][0]), "w_pb": f(inp["w_proj_hgrn"][0]),
        "w_out": f(inp["w_out"][0]), "w_up": f(inp["w_ffn_up"][0]), "w_dn": f(inp["w_ffn_down"][0]),
        "cols": f(cols), "rows_bc": f(rows), "wspT": f(wspT), "consts": f(consts), "sguA": f(sguA), "bsp": bsp,
    }
    maps = []
    zeros = np.zeros((NTOK, D), np.float32)
    for c in range(8):
        b, half = c // 2, c % 2
        m = dict(shared)
        m["x_main"] = np.ascontiguousarray(x[b, half * NTOK:(half + 1) * NTOK])
        m["x_prev"] = np.ascontiguousarray(x[b, 0:NTOK]) if half == 1 else zeros
        maps.append(m)
    return maps


def kernel(**inputs):
    maps = _host_inputs(inputs)
    nc = build_nc()
    res = run_bass_kernel_spmd(nc, maps, core_ids=list(range(8)))
    out = np.empty((4, 8192, D), np.float32)
    for c in range(8):
        b, half = c // 2, c % 2
        out[b, half * NTOK:(half + 1) * NTOK] = np.asarray(res.results[c]["y_out"], dtype=np.float32)
    return out
```

```python
import numpy as np
import concourse.bass as bass
import concourse.mybir as mybir
from concourse.bass_utils import run_bass_kernel_spmd

F32 = mybir.dt.float32
BF16 = mybir.dt.bfloat16
AF = mybir.ActivationFunctionType
ALU = mybir.AluOpType

D = 1024
NTOK = 4096
T = 512
NT = NTOK // T
FH = 2816
EPS = 1e-6
NCOL = 48


class Eng:
    def __init__(self, nc, name, h, kind):
        self.h = h
        self.kind = kind
        self.sem = nc.alloc_semaphore("se_" + name)
        self.cnt = 0
        self.waited = {}


class DSem:
    def __init__(self, nc, name):
        self.sem = nc.alloc_semaphore("sd_" + name)
        self.cnt = 0


class Buf:
    def __init__(self, name):
        self.name = name
        self.wr = {}
        self.rd = {}
        self.ov = []


def overlap(*bufs):
    for a in bufs:
        for b in bufs:
            if a is not b and b not in a.ov:
                a.ov.append(b)


class K:
    def __init__(self, nc):
        self.nc = nc
        self.pe = Eng(nc, "pe", nc.tensor, "pe")
        self.act = Eng(nc, "act", nc.scalar, "c")
        self.dve = Eng(nc, "dve", nc.vector, "c")
        self.pool = Eng(nc, "pool", nc.gpsimd, "c")
        self.sp = Eng(nc, "sp", nc.sync, "q")
        self.sems = [e.sem for e in (self.pe, self.act, self.dve, self.pool, self.sp)]
        self.semobj = {}

    def dsem(self, name):
        d = DSem(self.nc, name)
        self.sems.append(d.sem)
        return d

    def _deps(self, eng, reads, writes):
        need = {}

        def add(dct, same_ok):
            for key, (sem, val, src) in dct.items():
                if src is eng:
                    if eng.kind == "pe":
                        continue
                if need.get(key, (None, 0))[1] < val:
                    need[key] = (sem, val)

        for b in reads:
            for o in [b] + b.ov:
                add(o.wr, True)
        for b in writes:
            for o in [b] + b.ov:
                add(o.wr, False)
                add(o.rd, False)
        for key, (sem, val) in need.items():
            if eng.waited.get(key, 0) < val:
                eng.h.wait_ge(sem, val)
                eng.waited[key] = val

    def _mark(self, tok, reads, writes):
        key = id(tok[0])
        for b in writes:
            b.wr[key] = tok
        for b in reads:
            b.rd[key] = tok

    def op(self, eng, fn, reads=(), writes=()):
        self._deps(eng, reads, writes)
        ins = fn(eng.h)
        eng.cnt += 1
        ins.then_inc(eng.sem, 1)
        self._mark((eng.sem, eng.cnt, eng), reads, writes)

    def dma(self, q, ds, out, in_, reads=(), writes=()):
        self._deps(q, reads, writes)
        ins = q.h.dma_start(out=out, in_=in_)
        ds.cnt += 16
        ins.then_inc(ds.sem, 16)
        self._mark((ds.sem, ds.cnt, None), reads, writes)


def build_nc(ntiles_main=NT, ntiles_pre=NT, stop=99):
    nc = bass.Bass("TRN2", target_bir_lowering=False)

    def din(name, shape, dt=F32):
        return nc.dram_tensor(name, shape, dt, kind="ExternalInput").ap()

    x_main = din("x_main", [NTOK, D])
    x_prev = din("x_prev", [NTOK, D])
    w_in = din("w_in", [D, 8192])
    w_pa = din("w_pa", [D, D])
    w_pb = din("w_pb", [D, D])
    w_out = din("w_out", [D, D])
    w_up = din("w_up", [D, 2 * FH])
    w_dn = din("w_dn", [FH, D])
    cols_d = din("cols", [128, NCOL])
    rows_d = din("rows_bc", [128, 2 * D])
    wspT_d = din("wspT", [128, 8 * 128])
    cst_d = din("consts", [128, 128 + 256 + 512 + 128])
    sguA_d = din("sguA", [2, D])
    bsp_d = din("bsp", [1, D])
    y_out = nc.dram_tensor("y_out", [NTOK, D], F32, kind="ExternalOutput").ap()

    def dscr(name, shape):
        return nc.dram_tensor(name, shape, BF16, kind="Internal").ap()

    wb_in = dscr("wb_in", [D, 8192])
    wb_pa = dscr("wb_pa", [D, D])
    wb_pb = dscr("wb_pb", [D, D])
    wb_out = dscr("wb_out", [D, D])
    wb_up = dscr("wb_up", [D, 2 * FH])
    wb_dn = dscr("wb_dn", [FH, D])

    k = K(nc)
    PE, ACT, DVE, POOL, SP = k.pe, k.act, k.dve, k.pool, k.sp

    def sb(name, shape, dt):
        return nc.alloc_sbuf_tensor("s_" + name, shape, dt)

    cols = sb("cols", [128, NCOL], F32)
    der = sb("der", [128, 64], F32)
    rows = sb("rows", [128, 2 * D], F32)
    cst = sb("cst", [128, 1024], F32)
    identb = sb("identb", [128, 128], BF16)
    onesf = sb("onesf", [128, 128], F32)
    wspb = sb("wspb", [128, 8, 128], BF16)
    cg = sb("cg", [128, 8, 128], F32)
    X = [sb(f"x{s}", [128, D], F32) for s in range(4)]
    wspf = X[1].rearrange("p (g t) -> p g t", t=128)
    sguA = X[2][0:2, :]
    sguB = X[3][0:2, :]
    HT = sb("hT", [128, 8, T], BF16)
    RR = sb("RR", [128, 12288], BF16)
    R4 = sb("R4", [128, 8, T], BF16)
    R5 = sb("R5", [128, 8, T], BF16)
    S32 = sb("S32", [128, 8, 128], F32)
    SBF = sb("SBF", [128, 8, 9, 128], BF16)
    decs = sb("decs", [128, 8, 8], F32)
    scm = [sb(f"scm{i}", [128, 256], BF16) for i in range(2)]
    Bt = [sb(f"Bt{i}", [128, T], F32) for i in range(4)]
    TA = [sb(f"ta{i}", [128, T], F32) for i in range(3)]
    P1all = sb("P1all", [128, 4, T], F32)
    P1 = [P1all[:, i, :] for i in range(4)]
    Y = P1all[:, 0:2, :].rearrange("p a n -> p (a n)")
    P2all = sb("P2all", [128, 4, T], F32)
    P3all = sb("P3all", [128, 4, T], F32)
    P2 = [P2all[:, i, :] for i in range(4)]
    P3 = [P3all[:, i, :] for i in range(4)]
    Xn = [P2all[:, 0:2, :].rearrange("p a n -> p (a n)"), P2all[:, 2:4, :].rearrange("p a n -> p (a n)"),
          P3all[:, 0:2, :].rearrange("p a n -> p (a n)"), P3all[:, 2:4, :].rearrange("p a n -> p (a n)")]
    bia = sb("bia", [128, 4], F32)
    kppT = [sb(f"kppT{i}", [128, T], BF16) for i in range(4)]
    junk = sb("junk", [128, D], BF16)
    hb = [sb(f"hb{i}", [128, D], BF16) for i in range(2)]
    BT = [sb(f"BT{i}", [128, D], F32) for i in range(2)]
    stat = sb("stat", [128, 64], F32)
    RING = sb("RING", [128, 4 * 5632], BF16)

    U = RR[:, 0:4096].rearrange("p (k n) -> p k n", n=T)
    VLN = RR[:, 4096:8192].rearrange("p (s n) -> p s n", n=D)
    QP = RR[:, 4096:6144].rearrange("p (k n) -> p k n", n=T)
    KP = RR[:, 6144:8192].rearrange("p (k n) -> p k n", n=T)
    KPP = RR[:, 8192:10240].rearrange("p (s n) -> p s n", n=T)
    VI = RR[:, 10240:12288].rearrange("p (s n) -> p s n", n=T)
    GB = U
    FACT = RR[:, 0:11264].rearrange("p (k n) -> p k n", n=T)

    MMt = [nc.alloc_psum_tensor(f"mm{i}", [128, T], F32) for i in range(3)]
    STt2 = [nc.alloc_psum_tensor(f"st{i}", [128, 4, 128], F32) for i in range(2)]
    STt = STt2[0]
    TPt = nc.alloc_psum_tensor("tp", [128, 8, 128], BF16)
    OTt = [nc.alloc_psum_tensor(f"ot{i}", [128, T], F32) for i in range(2)]

    b_cols, b_der, b_rows, b_cst, b_identb, b_onesf = (Buf(n) for n in ("cols", "der", "rows", "cst", "identb", "onesf"))
    b_wspf, b_wspb, b_cg, b_sguA, b_sguB = (Buf(n) for n in ("wspf", "wspb", "cg", "sguA", "sguB"))
    b_X = [Buf(f"x{s}") for s in range(4)]
    overlap(b_X[1], b_wspf)
    overlap(b_X[2], b_sguA)
    overlap(b_X[3], b_sguB)
    b_HT = Buf("hT")
    b_U, b_VLN, b_QP, b_KP, b_KPP, b_VI, b_FACT = (Buf(n) for n in ("U", "VLN", "QP", "KP", "KPP", "VI", "FACT"))
    overlap(b_U, b_FACT)
    overlap(b_VLN, b_QP, b_FACT)
    overlap(b_VLN, b_KP, b_FACT)
    overlap(b_KPP, b_FACT)
    overlap(b_VI, b_FACT)
    b_R4, b_R5 = Buf("R4"), Buf("R5")
    b_S32 = [Buf(f"S32_{h}") for h in range(8)]
    b_SBF = [Buf(f"SBF_{h}") for h in range(8)]
    b_decs = [Buf(f"decs{h}") for h in range(8)]
    b_scm = [Buf("scm0"), Buf("scm1")]
    b_Bt = [Buf(f"Bt{i}") for i in range(4)]
    b_TA = [Buf(f"ta{i}") for i in range(3)]
    b_P1 = [Buf(f"P1_{i}") for i in range(4)]
    b_P2 = [Buf(f"P2_{i}") for i in range(4)]
    b_P3 = [Buf(f"P3_{i}") for i in range(4)]
    b_Xn = [Buf(f"xn{i}") for i in range(4)]
    overlap(b_Xn[0], b_P2[0])
    overlap(b_Xn[0], b_P2[1])
    overlap(b_Xn[1], b_P2[2])
    overlap(b_Xn[1], b_P2[3])
    overlap(b_Xn[2], b_P3[0])
    overlap(b_Xn[2], b_P3[1])
    overlap(b_Xn[3], b_P3[2])
    overlap(b_Xn[3], b_P3[3])
    b_bia = [Buf(f"bia{i}") for i in range(4)]
    b_Y = Buf("Y")
    overlap(b_Y, b_P1[0])
    overlap(b_Y, b_P1[1])
    b_kppT = [Buf(f"kppT{i}") for i in range(4)]
    b_junk = Buf("junk")
    b_hb = [Buf("hb0"), Buf("hb1")]
    b_BT = [Buf("BT0"), Buf("BT1")]
    b_stat = Buf("stat")
    b_ring = [Buf(f"ring{i}") for i in range(4)]
    b_stb = [Buf("stb0"), Buf("stb1")]
    b_MM = [Buf(f"mm{i}") for i in range(3)]
    b_ST2 = [Buf(f"st{i}") for i in range(2)]
    b_ST = [b_ST2[0]]
    b_TP = Buf("tp")
    b_OT = [Buf("ot0"), Buf("ot1")]
    b_wb = {n: Buf("wb_" + n) for n in ("in", "pa", "pb", "out", "up", "dn")}

    d_const = k.dsem("const")
    d_ring = [k.dsem(f"ring{i}") for i in range(4)]
    d_stl = [k.dsem(f"stl{i}") for i in range(3)]
    d_conv = [k.dsem("conv0"), k.dsem("conv1")]
    d_xl = [k.dsem(f"xl{s}") for s in range(4)]
    d_xs = [k.dsem(f"xs{s}") for s in range(4)]
    d_y = k.dsem("y")

    def finish():
        for s in range(4):
            if d_xs[s].cnt:
                nc.sync.wait_ge(d_xs[s].sem, d_xs[s].cnt)
        if d_y.cnt:
            nc.sync.wait_ge(d_y.sem, d_y.cnt)
        nc.all_engine_barrier()
        for sm in k.sems:
            nc.gpsimd.sem_clear(sm)
        return nc

    for sm in k.sems:
        nc.gpsimd.sem_clear(sm)
    nc.all_engine_barrier()

    k.dma(SP, d_const, cols[:], cols_d[:, :], writes=[b_cols])
    k.dma(SP, d_const, rows[:], rows_d[:, :], writes=[b_rows])
    k.dma(SP, d_const, cst[:], cst_d[:, :], writes=[b_cst])
    k.dma(SP, d_const, X[1][:], wspT_d[:, :], writes=[b_wspf])
    k.dma(SP, d_const, sguA, sguA_d[:, :], writes=[b_sguA])
    k.dma(SP, d_const, sguB[1:2, :], bsp_d[:, :], writes=[b_sguB])
    for b_ in (b_cols, b_rows, b_cst, b_wspf, b_sguA, b_sguB):
        b_.wr[id(d_const.sem)] = (d_const.sem, d_const.cnt, None)
    ident_f = cst[:, 0:128]
    mask_sc = cst[:, 128:384]
    mask_scan = cst[:, 384:896]
    mask_sgu = cst[:, 896:1024]

    k.op(DVE, lambda e: e.tensor_copy(out=identb[:], in_=ident_f), reads=[b_cst], writes=[b_identb])
    k.op(POOL, lambda e: e.memset(onesf[:], 1.0), writes=[b_onesf])
    k.op(POOL, lambda e: e.memset(der[:, 40:48], -0.5), writes=[b_der])
    k.op(POOL, lambda e: e.memset(S32.rearrange("p h v -> p (h v)"), 0.0), writes=b_S32)
    k.op(POOL, lambda e: e.memset(SBF.rearrange("p h j v -> p (h j v)"), 0.0), writes=b_SBF)
    k.op(DVE, lambda e: e.tensor_tensor(out=der[:, 24:32], in0=cols[:, 40:48], in1=cols[:, 32:40], op=ALU.subtract),
         reads=[b_cols], writes=[b_der])
    k.op(ACT, lambda e: e.activation(out=der[:, 24:32], in_=der[:, 24:32], func=AF.Exp), reads=[b_der], writes=[b_der])
    k.op(DVE, lambda e: e.tensor_scalar(out=der[:, 32:40], in0=der[:, 24:32], scalar1=1.0, scalar2=None, op0=ALU.add),
         reads=[b_der], writes=[b_der])
    k.op(DVE, lambda e: e.reciprocal(out=der[:, 0:8], in_=der[:, 32:40]), reads=[b_der], writes=[b_der])
    k.op(DVE, lambda e: e.tensor_tensor(out=der[:, 8:16], in0=der[:, 24:32], in1=der[:, 0:8], op=ALU.mult),
         reads=[b_der], writes=[b_der])
    k.op(ACT, lambda e: e.activation(out=der[:, 16:24], in_=der[:, 8:16], func=AF.Ln), reads=[b_der], writes=[b_der])
    LB = lambda h: der[:, h:h + 1]
    LNOML = lambda h: der[:, 16 + h:17 + h]
    NEGH = der[:, 40:41]

    w3 = wspf
    ones512 = onesf[:, 0:1].to_broadcast([128, T])
    k.op(DVE, lambda e: e.tensor_tensor(out=w3, in0=w3, in1=mask_sgu.unsqueeze(1).to_broadcast([128, 8, 128]), op=ALU.mult),
         reads=[b_wspf, b_cst], writes=[b_wspf])
    k.op(POOL, lambda e: e.tensor_copy(out=wspb[:], in_=wspf), reads=[b_wspf], writes=[b_wspb])

    def f_wsum2(e):
        ins = None
        for g in range(8):
            dst = MMt[g // 4][0:1, (g % 4) * 128:(g % 4 + 1) * 128]
            ins = e.matmul(dst, lhsT=onesf[:, 0:1], rhs=wspf[:, g, :], start=True, stop=True)
        return ins
    k.op(PE, f_wsum2, reads=[b_onesf, b_wspf], writes=[b_MM[0], b_MM[1]])
    k.op(DVE, lambda e: e.tensor_copy(out=sguB[0:1, 0:512], in_=MMt[0][0:1, :]), reads=[b_MM[0]], writes=[b_sguB])
    k.op(DVE, lambda e: e.tensor_copy(out=sguB[0:1, 512:1024], in_=MMt[1][0:1, :]), reads=[b_MM[1]], writes=[b_sguB])

    def f_cg(e):
        ins = None
        for g in range(8):
            dst = MMt[g // 4][:, (g % 4) * 128:(g % 4 + 1) * 128]
            ins = e.matmul(dst, lhsT=sguA[:, g * 128:(g + 1) * 128], rhs=sguB[:, g * 128:(g + 1) * 128], start=True, stop=True)
        return ins
    k.op(PE, f_cg, reads=[b_sguA, b_sguB], writes=[b_MM[0], b_MM[1]])
    k.op(DVE, lambda e: e.tensor_copy(out=cg[:, 0:4, :].rearrange("p g t -> p (g t)"), in_=MMt[0][:]), reads=[b_MM[0]], writes=[b_cg])
    k.op(DVE, lambda e: e.tensor_copy(out=cg[:, 4:8, :].rearrange("p g t -> p (g t)"), in_=MMt[1][:]), reads=[b_MM[1]], writes=[b_cg])

    if stop == 0:
        return finish()
    stf = [R4.rearrange("p k n -> p (k n)").bitcast(F32), R5.rearrange("p k n -> p (k n)").bitcast(F32), RR[:, 0:4096].bitcast(F32)]
    stb = [RR[:, 4096:6144], RR[:, 6144:8192]]
    b_stf = [Buf(f"stf{i}") for i in range(3)]
    overlap(b_stf[0], b_R4)
    overlap(b_stf[1], b_R5)
    overlap(b_stf[2], b_U, b_FACT)
    overlap(b_stb[0], b_VLN, b_QP, b_FACT)
    overlap(b_stb[1], b_VLN, b_KP, b_FACT)
    conv = []

    def conv_win(c0):
        for kc in range(8):
            conv.append((w_in[kc * 128:(kc + 1) * 128, c0:c0 + 2048], wb_in[kc * 128:(kc + 1) * 128, c0:c0 + 2048], cols[:, kc:kc + 1], "in"))
    conv_win(2048)
    conv_win(4096)
    n_phase0 = len(conv)
    conv_win(0)
    conv_win(6144)
    for nm, src, dst in (("pa", w_pa, wb_pa), ("pb", w_pb, wb_pb), ("out", w_out, wb_out)):
        for kc in range(8):
            conv.append((src[kc * 128:(kc + 1) * 128, :], dst[kc * 128:(kc + 1) * 128, :], None, nm))
    for kc in range(8):
        for c0, cw in ((0, 2048), (2048, 2048), (4096, 1536)):
            conv.append((w_up[kc * 128:(kc + 1) * 128, c0:c0 + cw], wb_up[kc * 128:(kc + 1) * 128, c0:c0 + cw], cols[:, 8 + kc:9 + kc], "up"))
    for kc in range(22):
        conv.append((w_dn[kc * 128:(kc + 1) * 128, :], wb_dn[kc * 128:(kc + 1) * 128, :], None, "dn"))
    nconv = len(conv)
    cv = {"next": 0, "loaded": 0, "stored": 0}

    def conv_load(i):
        src = conv[i][0]
        w = src.shape[1]
        k.dma(SP, d_stl[i % 3], stf[i % 3][:, 0:w], src, writes=[b_stf[i % 3]])

    def conv_cast(i):
        src, dst, gc, nm = conv[i]
        w = src.shape[1]
        j = i % 2
        if gc is None:
            k.op(DVE, lambda e: e.tensor_copy(out=stb[j][:, 0:w], in_=stf[i % 3][:, 0:w]), reads=[b_stf[i % 3]], writes=[b_stb[j]])
        else:
            k.op(DVE, lambda e: e.tensor_scalar(out=stb[j][:, 0:w], in0=stf[i % 3][:, 0:w], scalar1=gc, scalar2=None, op0=ALU.mult),
                 reads=[b_stf[i % 3], b_cols], writes=[b_stb[j]])

    def conv_store(i):
        src, dst, gc, nm = conv[i]
        w = src.shape[1]
        j = i % 2
        k.dma(ACT, d_conv[j], dst, stb[j][:, 0:w], reads=[b_stb[j]], writes=[b_wb[nm]])

    def conv_run(upto, flush=False):
        upto = min(upto, nconv)
        while cv["next"] < upto:
            i = cv["next"]
            while cv["loaded"] < min(i + 3, nconv):
                conv_load(cv["loaded"])
                cv["loaded"] += 1
            conv_cast(i)
            if i - 1 >= cv["stored"]:
                conv_store(cv["stored"])
                cv["stored"] += 1
            cv["next"] += 1
        if flush:
            while cv["stored"] < cv["next"]:
                conv_store(cv["stored"])
                cv["stored"] += 1

    conv_run(n_phase0, flush=True)
    tick_budget = [0]

    def tick():
        if tick_budget[0] > 0 and cv["next"] < nconv:
            conv_run(cv["next"] + 1)
            tick_budget[0] -= 1

    if stop == 1:
        return finish()
    ring_ctr = [0]

    def ring_slot():
        i = ring_ctr[0] % 4
        ring_ctr[0] += 1
        return i

    def slot_view(i, kk):
        return RING[:, i * 5632:i * 5632 + kk * 512].rearrange("p (k n) -> p k n", n=512)

    def load_k8(wb, nm, c0):
        i = ring_slot()
        k.dma(SP, d_ring[i], slot_view(i, 8), wb[:, c0:c0 + 512].rearrange("(k p) n -> p k n", p=128),
              reads=[b_wb[nm]], writes=[b_ring[i]])
        return i

    def load_up(blk):
        i = ring_slot()
        v = slot_view(i, 8)
        j = 2 * blk
        k.dma(SP, d_ring[i], v[:, :, 0:256], wb_up[:, j * 128:j * 128 + 256].rearrange("(k p) n -> p k n", p=128),
              reads=[b_wb["up"]], writes=[b_ring[i]])
        k.dma(SP, d_ring[i], v[:, :, 256:512], wb_up[:, FH + j * 128:FH + j * 128 + 256].rearrange("(k p) n -> p k n", p=128),
              reads=[b_wb["up"]], writes=[b_ring[i]])
        return i

    def load_dn(kh, nh):
        i = ring_slot()
        k.dma(SP, d_ring[i], slot_view(i, 11), wb_dn[kh * 1408:(kh + 1) * 1408, nh * 512:(nh + 1) * 512].rearrange("(k p) n -> p k n", p=128),
              reads=[b_wb["dn"]], writes=[b_ring[i]])
        return i

    mm_ctr = [0, 0]
    MMt.extend([OTt[0], OTt[1], STt2[0].rearrange("p q v -> p (q v)"), STt2[1].rearrange("p q v -> p (q v)")])
    b_MM.extend([b_OT[0], b_OT[1], b_ST2[0], b_ST2[1]])

    def mm_bank(wide=False):
        if wide:
            i = mm_ctr[1] % 7
            mm_ctr[1] += 1
        else:
            i = mm_ctr[0] % 3
            mm_ctr[0] += 1
        return i

    ta_ctr = [0]

    def ta():
        i = ta_ctr[0] % 3
        ta_ctr[0] += 1
        return TA[i], b_TA[i]

    st_ctr = [0]
    misc = {"stat": 0, "hb": 0, "bt": 0, "kppT": 0, "sc": 0, "ot": 0}

    def nxt(name, n):
        v = misc[name] % n
        misc[name] += 1
        return v

    b_stats = [Buf(f"stat{i}") for i in range(32)]

    def stat_cols(n):
        i = misc["stat"] % 32
        misc["stat"] += 1
        return stat[:, 2 * i:2 * i + n], b_stats[i]

    def fm_group(slot, chunk, rhs, rbufs, wide=True):
        bi = mm_bank(wide)
        v = slot_view(slot, 8)

        def f(e):
            ins = None
            for kc in range(8):
                ins = e.matmul(MMt[bi][:], lhsT=v[:, kc, chunk * 128:(chunk + 1) * 128], rhs=rhs[:, kc, :], start=(kc == 0), stop=(kc == 7))
            return ins
        k.op(PE, f, reads=[b_ring[slot]] + rbufs, writes=[b_MM[bi]])
        return bi

    def tm_group(slot, lhs, s, rbufs, wide=True):
        bi = mm_bank(wide)
        v = slot_view(slot, 8)

        def f(e):
            ins = None
            for kc in range(8):
                ins = e.matmul(MMt[bi][:], lhsT=lhs[:, kc, s * 128:(s + 1) * 128], rhs=v[:, kc, :], start=(kc == 0), stop=(kc == 7))
            return ins
        k.op(PE, f, reads=[b_ring[slot]] + rbufs, writes=[b_MM[bi]])
        return bi

    def rstd_small(ss_ap, ss_buf, n, scale):
        m, bm = stat_cols(n)
        r, br = stat_cols(n)
        k.op(POOL, lambda e: e.tensor_scalar(out=m, in0=ss_ap, scalar1=scale, scalar2=EPS, op0=ALU.mult, op1=ALU.add),
             reads=[ss_buf], writes=[bm])
        k.op(POOL, lambda e: e.tensor_tensor(out=r, in0=m, in1=NEGH, op=ALU.pow),
             reads=[bm, b_der], writes=[br])
        return r, br

    def norm_to_hT(s):
        norm_b(norm_a(s))

    def norm_a(s, src=None, bsrc=None):
        if src is None:
            src, bsrc = X[s][:], b_X[s]
        ss, bss = stat_cols(1)
        k.op(ACT, lambda e: e.activation(out=junk[:], in_=src, func=AF.Square, accum_out=ss),
             reads=[bsrc], writes=[b_junk, bss])
        r, br = rstd_small(ss, bss, 1, 1.0 / D)
        hi = nxt("hb", 2)
        k.op(DVE, lambda e: e.tensor_scalar(out=hb[hi][:], in0=src, scalar1=r, scalar2=None, op0=ALU.mult),
             reads=[bsrc, br], writes=[b_hb[hi]])
        return (s, hi)

    def norm_b(sh):
        s, hi = sh

        def f(e):
            ins = None
            for kc in range(8):
                ins = e.transpose(TPt[:, kc, :], hb[hi][:, kc * 128:(kc + 1) * 128], identb[:])
            return ins
        k.op(PE, f, reads=[b_hb[hi], b_identb], writes=[b_TP])
        k.op(DVE, lambda e: e.tensor_copy(out=HT[:, :, s * 128:(s + 1) * 128], in_=TPt[:]), reads=[b_TP], writes=[b_HT])

    def load_x(src, t, s):
        k.dma(SP, d_xl[s], X[s][:], src[t * T + s * 128:t * T + (s + 1) * 128, :], writes=[b_X[s]])

    def sbf_slot(gt, j):
        return (8 * gt + j) % 9

    def hgrn_group(gi, gt, main, last_pre=False):
        heads = [4 * gi + i for i in range(4)]
        sl_f = load_k8(wb_in, "in", 3072 + gi * 512)
        sl_i = load_k8(wb_in, "in", 4096 + gi * 512)
        if main:
            sl_g = load_k8(wb_in, "in", 5120 + gi * 512)
            sl_q = load_k8(wb_in, "in", 2048 + gi * 512)
        for hl, h in enumerate(heads):
            bi = fm_group(sl_f, hl, HT, [b_HT])
            z = MMt[bi]
            E, bE = ta()
            k.op(ACT, lambda e: e.activation(out=E[:], in_=z[:], func=AF.Exp, scale=-1.0), reads=[b_MM[bi]], writes=[bE])
            k.op(ACT, lambda e: e.activation(out=P2[hl][:], in_=E[:], func=AF.Ln, scale=LB(h), bias=1.0), reads=[bE, b_der], writes=[b_P2[hl]])
            k.op(ACT, lambda e: e.activation(out=P3[hl][:], in_=E[:], func=AF.Ln, scale=1.0, bias=1.0), reads=[bE], writes=[b_P3[hl]])
            k.op(DVE, lambda e: e.scalar_tensor_tensor(out=P1[hl][:], in0=z[:], scalar=-1.0, in1=P3[hl][:], op0=ALU.mult, op1=ALU.subtract),
                 reads=[b_MM[bi], b_P3[hl]], writes=[b_P1[hl]])
            if not main:
                tick()
        for hl, h in enumerate(heads):
            k.op(DVE, lambda e: e.tensor_tensor(out=P2[hl][:], in0=P2[hl][:], in1=P3[hl][:], op=ALU.subtract),
                 reads=[b_P2[hl], b_P3[hl]], writes=[b_P2[hl]])
        for hl, h in enumerate(heads):
            B, bB = Bt[hl], b_Bt[hl]
            if main:
                k.op(DVE, lambda e: e.tensor_tensor_scan(out=B[:], data0=mask_scan, data1=P2[hl][:], initial=0.0, op0=ALU.mult, op1=ALU.add),
                     reads=[b_cst, b_P2[hl]], writes=[bB])
            else:
                k.op(DVE, lambda e: e.tensor_tensor_scan(out=B[:], data0=ones512, data1=P2[hl][:], initial=0.0, op0=ALU.mult, op1=ALU.add),
                     reads=[b_onesf, b_P2[hl]], writes=[bB])
        for hl, h in enumerate(heads):
            B, bB = Bt[hl], b_Bt[hl]
            k.op(DVE, lambda e: e.tensor_tensor(out=P1[hl][:], in0=P1[hl][:], in1=B[:], op=ALU.subtract),
                 reads=[b_P1[hl], bB], writes=[b_P1[hl]])
            if main:
                B3 = B.rearrange("p (c l) -> p c l", l=64)
                k.op(POOL, lambda e: e.tensor_tensor(out=P3[hl].rearrange("p (c l) -> p c l", l=64), in0=P1[hl].rearrange("p (c l) -> p c l", l=64),
                                                     in1=B3[:, :, 63:64].to_broadcast([128, 8, 64]), op=ALU.add),
                     reads=[b_P1[hl], bB], writes=[b_P3[hl]])
            else:
                k.op(POOL, lambda e: e.tensor_tensor(out=bia[:, hl:hl + 1], in0=B[:, 511:512], in1=LNOML(h), op=ALU.add),
                     reads=[bB, b_der], writes=[b_bia[hl]])
        if main:
            for hp in range(0, 4, 2):
                st = []
                for hl in (hp, hp + 1):
                    bi = fm_group(sl_g, hl, HT, [b_HT])
                    E, bE = ta()
                    st.append((hl, heads[hl], bi, E, bE))
                    k.op(ACT, lambda e: e.activation(out=E[:], in_=MMt[bi][:], func=AF.Exp, scale=-1.0), reads=[b_MM[bi]], writes=[bE])
                for hl, h, bi, E, bE in st:
                    k.op(ACT, lambda e: e.activation(out=E[:], in_=E[:], func=AF.Ln, bias=1.0), reads=[bE], writes=[bE])
                for hl, h, bi, E, bE in st:
                    k.op(ACT, lambda e: e.activation(out=E[:], in_=E[:], func=AF.Exp, scale=-1.0), reads=[bE], writes=[bE])
                for hl, h, bi, E, bE in st:
                    k.op(DVE, lambda e: e.tensor_tensor(out=R5[:, h, :], in0=MMt[bi][:], in1=E[:], op=ALU.mult), reads=[b_MM[bi], bE], writes=[b_R5])
        for s in range(4):
            bi = tm_group(sl_i, HT, s, [b_HT])
            k.op(DVE, lambda e: e.tensor_copy(out=VI[:, s, :], in_=MMt[bi][:]), reads=[b_MM[bi]], writes=[b_VI])
        if not main:
            tick()
        for hl, h in enumerate(heads):
            B = Bt[hl]
            if main:
                B3 = B.rearrange("p (c l) -> p c l", l=64)
                k.op(ACT, lambda e: e.activation(out=KP[:, hl, :], in_=P1[hl][:], func=AF.Exp, bias=LNOML(h)), reads=[b_P1[hl], b_der], writes=[b_KP])
                k.op(ACT, lambda e: e.activation(out=kppT[hl][:], in_=P3[hl][:], func=AF.Exp, bias=LNOML(h)), reads=[b_P3[hl], b_der], writes=[b_kppT[hl]])
                k.op(ACT, lambda e: e.activation(out=decs[:, h, :], in_=B3[:, :, 63], func=AF.Exp), reads=[b_Bt[hl]], writes=[b_decs[h]])
            else:
                k.op(ACT, lambda e: e.activation(out=kppT[hl][:], in_=P1[hl][:], func=AF.Exp, bias=bia[:, hl:hl + 1]), reads=[b_P1[hl], b_bia[hl]], writes=[b_kppT[hl]])
                k.op(ACT, lambda e: e.activation(out=decs[:, h, 0:1], in_=B[:, 511:512], func=AF.Exp), reads=[b_Bt[hl]], writes=[b_decs[h]])

        if main:
            for hp in range(0, 4, 2):
                st = []
                for hl in (hp, hp + 1):
                    bi = fm_group(sl_q, hl, HT, [b_HT])
                    E, bE = ta()
                    st.append((hl, heads[hl], bi, E, bE))
                    k.op(ACT, lambda e: e.activation(out=E[:], in_=MMt[bi][:], func=AF.Exp, scale=-1.0), reads=[b_MM[bi]], writes=[bE])
                for hl, h, bi, E, bE in st:
                    k.op(ACT, lambda e: e.activation(out=E[:], in_=E[:], func=AF.Ln, bias=1.0), reads=[bE], writes=[bE])
                    k.op(DVE, lambda e: e.tensor_tensor(out=E[:], in0=Bt[hl][:], in1=E[:], op=ALU.subtract), reads=[bE, b_Bt[hl]], writes=[bE])
                for hl, h, bi, E, bE in st:
                    k.op(ACT, lambda e: e.activation(out=E[:], in_=E[:], func=AF.Exp), reads=[bE], writes=[bE])
                for hl, h, bi, E, bE in st:
                    k.op(DVE, lambda e: e.tensor_tensor(out=QP[:, hl, :], in0=MMt[bi][:], in1=E[:], op=ALU.mult), reads=[b_MM[bi], bE], writes=[b_QP])
        for hl, h in enumerate(heads):
            def ftp(e):
                ins = None
                for s in range(4):
                    ins = e.transpose(TPt[:, s, :], kppT[hl][:, s * 128:(s + 1) * 128], identb[:])
                return ins
            k.op(PE, ftp, reads=[b_kppT[hl], b_identb], writes=[b_TP])
            k.op(DVE, lambda e: e.tensor_copy(out=KPP[:, :, hl * 128:(hl + 1) * 128], in_=TPt[:, 0:4, :]), reads=[b_TP], writes=[b_KPP])
        if not main:
            tick()
            def fst(e):
                ins = None
                for hl in range(4):
                    for s in range(4):
                        ins = e.matmul(STt2[0][:, hl, :], lhsT=KPP[:, s, hl * 128:(hl + 1) * 128],
                                       rhs=VI[:, s, hl * 128:(hl + 1) * 128], start=(s == 0), stop=(s == 3))
                return ins
            k.op(PE, fst, reads=[b_KPP, b_VI], writes=[b_ST2[0]])
            for hl, h in enumerate(heads):
                k.op(DVE, lambda e: e.scalar_tensor_tensor(out=S32[:, h, :], in0=S32[:, h, :], scalar=decs[:, h, 0:1],
                                                           in1=STt2[0][:, hl, :], op0=ALU.mult, op1=ALU.add),
                     reads=[b_S32[h], b_decs[h], b_ST2[0]], writes=[b_S32[h]])
                if last_pre:
                    k.op(POOL, lambda e: e.tensor_copy(out=SBF[:, h, 0, :], in_=S32[:, h, :]), reads=[b_S32[h]], writes=[b_SBF[h]])
            return

    def hgrn_pass1(gi, gt):
        heads = [4 * gi + i for i in range(4)]
        for j in range(8):
            s, pb = j // 2, (j % 2) * 64
            sb_i = j % 2

            def fst(e):
                ins = None
                for hl in range(4):
                    ins = e.matmul(STt2[sb_i][:, hl, :], lhsT=KPP[pb:pb + 64, s, hl * 128:(hl + 1) * 128],
                                   rhs=VI[pb:pb + 64, s, hl * 128:(hl + 1) * 128], start=True, stop=True)
                return ins
            k.op(PE, fst, reads=[b_KPP, b_VI], writes=[b_ST2[sb_i]])
            for hl, h in enumerate(heads):
                k.op(DVE, lambda e: e.scalar_tensor_tensor(out=S32[:, h, :], in0=S32[:, h, :], scalar=decs[:, h, j:j + 1],
                                                           in1=STt2[sb_i][:, hl, :], op0=ALU.mult, op1=ALU.add),
                     reads=[b_S32[h], b_decs[h], b_ST2[sb_i]], writes=[b_S32[h]])
                so = sbf_slot(gt, j + 1)
                if hl % 2 == 0:
                    k.op(ACT, lambda e: e.activation(out=SBF[:, h, so, :], in_=S32[:, h, :], func=AF.Copy), reads=[b_S32[h]], writes=[b_SBF[h]])
                else:
                    k.op(POOL, lambda e: e.tensor_copy(out=SBF[:, h, so, :], in_=S32[:, h, :]), reads=[b_S32[h]], writes=[b_SBF[h]])
            yield

    def hgrn_pass2(gi, gt):
        heads = [4 * gi + i for i in range(4)]

        def sc(hl):
            ci = hl % 2
            SCv = STt2[ci].rearrange("p q v -> p (q v)")

            def fsc(e):
                ins = None
                for j in range(8):
                    pb = (j % 2) * 64
                    ins = e.matmul(SCv[pb:pb + 64, (j // 2) * 64:(j // 2 + 1) * 64], lhsT=KP[:, hl, j * 64:(j + 1) * 64],
                                   rhs=QP[:, hl, j * 64:(j + 1) * 64], start=True, stop=True)
                return ins
            k.op(PE, fsc, reads=[b_KP, b_QP], writes=[b_ST2[ci]])
            k.op(DVE, lambda e: e.tensor_tensor(out=scm[ci][:], in0=SCv[:, 0:256], in1=mask_sc, op=ALU.mult),
                 reads=[b_ST2[ci], b_cst], writes=[b_scm[ci]])

        sc(0)
        sc(1)
        yield
        for hl, h in enumerate(heads):
            ci = hl % 2
            oi = hl % 2

            def fo(e):
                ins = None
                for j in range(8):
                    s, pb = j // 2, (j % 2) * 64
                    si = sbf_slot(gt, j)
                    e.matmul(OTt[oi][:, j * 64:(j + 1) * 64], lhsT=VI[pb:pb + 64, s, hl * 128:(hl + 1) * 128],
                             rhs=scm[ci][pb:pb + 64, s * 64:(s + 1) * 64], start=True, stop=False)
                    ins = e.matmul(OTt[oi][:, j * 64:(j + 1) * 64], lhsT=SBF[:, h, si, :], rhs=QP[:, hl, j * 64:(j + 1) * 64],
                                   start=False, stop=True)
                return ins
            k.op(PE, fo, reads=[b_VI, b_scm[ci], b_SBF[h], b_QP], writes=[b_OT[oi]])
            OC, bOC = P1[hl], b_P1[hl]
            k.op(DVE, lambda e: e.tensor_copy(out=OC[:], in_=OTt[oi][:]), reads=[b_OT[oi]], writes=[bOC])
            SQ, bSQ = P2[hl], b_P2[hl]
            k.op(ACT, lambda e: e.activation(out=SQ[:], in_=OC[:], func=AF.Square), reads=[bOC], writes=[bSQ])
            if hl + 2 < 4:
                sc(hl + 2)
            yield
        for hl, h in enumerate(heads):
            OC, bOC = P1[hl], b_P1[hl]
            SQ, bSQ = P2[hl], b_P2[hl]
            bi = mm_bank()
            k.op(PE, lambda e: e.matmul(MMt[bi][:], lhsT=onesf[:], rhs=SQ[:], start=True, stop=True), reads=[b_onesf, bSQ], writes=[b_MM[bi]])
            k.op(ACT, lambda e: e.activation(out=SQ[:], in_=MMt[bi][:], func=AF.Ln, scale=1.0 / 128, bias=EPS), reads=[b_MM[bi]], writes=[bSQ])
            k.op(ACT, lambda e: e.activation(out=SQ[:], in_=SQ[:], func=AF.Exp, scale=-0.5), reads=[bSQ], writes=[bSQ])
            k.op(DVE, lambda e: e.scalar_tensor_tensor(out=SQ[:], in0=OC[:], scalar=cols[:, 24 + h:25 + h], in1=SQ[:],
                                                       op0=ALU.mult, op1=ALU.mult), reads=[bOC, b_cols, bSQ], writes=[bSQ])
            k.op(POOL, lambda e: e.tensor_tensor(out=R5[:, h, :], in0=SQ[:], in1=R5[:, h, :], op=ALU.mult), reads=[bSQ, b_R5], writes=[b_R5])
            if hl < 3:
                yield

    def interleave(gen, fillers):
        fillers = list(fillers)
        for _ in gen:
            if fillers:
                fillers.pop(0)()
        for f in fillers:
            f()

    def out_proj(loader, nk, lhs, lbufs, gain_off, final, t, after_nh0=None):
        nkh = nk // 8 if nk == 8 else 2
        kper = 8 if nk == 8 else 11
        pend = []
        ssq_ = [stat_cols(2) for _ in range(4)]
        ssq = [a for a, _ in ssq_]
        bssq = [b for _, b in ssq_]
        s_outer = (nk == 8)
        slots_all = {}
        if s_outer:
            for nh in range(2):
                slots_all[nh] = [loader(kh, nh) for kh in range(nkh)]
            order = [(s, nh) for s in range(4) for nh in range(2)]
        else:
            order = [(s, nh) for nh in range(2) for s in range(4)]
        for (s, nh) in order:
            if nh not in slots_all:
                slots_all[nh] = [loader(kh, nh) for kh in range(nkh)]
            slots = slots_all[nh]
            bi = (2 * s + nh) % 4 if s_outer else s
            if bi == 3:
                ptile, pbuf = STt.rearrange("p q v -> p (q v)"), b_ST
            else:
                ptile, pbuf = MMt[bi], [b_MM[bi]]

            def f(e):
                ins = None
                n = 0
                for kh in range(nkh):
                    v = slot_view(slots[kh], kper)
                    for kc in range(kper):
                        ins = e.matmul(ptile[:], lhsT=lhs[:, kh * kper + kc, s * 128:(s + 1) * 128], rhs=v[:, kc, :],
                                       start=(n == 0), stop=(n == nkh * kper - 1))
                        n += 1
                return ins
            k.op(PE, f, reads=[b_ring[sl] for sl in slots] + lbufs, writes=pbuf)
            k.op(ACT, lambda e: e.activation(out=junk[:, 0:512], in_=ptile[:], func=AF.Square, accum_out=ssq[s][:, nh:nh + 1]),
                 reads=pbuf, writes=[b_junk, bssq[s]])
            park = BT[s // 2][:, (s % 2) * 512:(s % 2 + 1) * 512]
            if nh == 0:
                k.op(ACT, lambda e: e.activation(out=park, in_=ptile[:], func=AF.Copy), reads=pbuf, writes=[b_BT[s // 2]])
                if final and t + 1 < ntiles_main:
                    if len(pend) == 2:
                        norm_b(pend.pop(0))
                    pend.append(norm_a(s, Xn[s], b_Xn[s]))
                    if s == 3:
                        while pend:
                            norm_b(pend.pop(0))
                if s == 3 and after_nh0 is not None:
                    if 1 not in slots_all:
                        slots_all[1] = [loader(kh, 1) for kh in range(nkh)]
                    after_nh0()
                continue
            tot, btot = stat_cols(1)
            k.op(POOL, lambda e: e.tensor_tensor(out=tot, in0=ssq[s][:, 0:1], in1=ssq[s][:, 1:2], op=ALU.add),
                 reads=[bssq[s]], writes=[btot])
            r, br = rstd_small(tot, btot, 1, 1.0 / D)
            tA, btA = ta()
            k.op(DVE, lambda e: e.scalar_tensor_tensor(out=tA[:], in0=ptile[:], scalar=r, in1=rows[:, gain_off + 512:gain_off + 1024],
                                                       op0=ALU.mult, op1=ALU.mult), reads=pbuf + [br, b_rows], writes=[btA])
            tB, btB = ta()
            k.op(DVE, lambda e: e.scalar_tensor_tensor(out=tB[:], in0=park, scalar=r, in1=rows[:, gain_off:gain_off + 512],
                                                       op0=ALU.mult, op1=ALU.mult), reads=[b_BT[s // 2], br, b_rows], writes=[btB])
            if final:
                k.op(POOL, lambda e: e.tensor_tensor(out=Y[:, 512:1024], in0=X[s][:, 512:1024], in1=tA[:], op=ALU.add),
                     reads=[btA, b_X[s]], writes=[b_Y])
                k.op(POOL, lambda e: e.tensor_tensor(out=Y[:, 0:512], in0=X[s][:, 0:512], in1=tB[:], op=ALU.add),
                     reads=[btB, b_X[s]], writes=[b_Y])
                k.dma(SP, d_y, y_out[t * T + s * 128:t * T + (s + 1) * 128, :], Y, reads=[b_Y])
                if t + 1 < ntiles_main:
                    k.op(DVE, lambda e: e.tensor_copy(out=X[s][:], in_=Xn[s]), reads=[b_Xn[s]], writes=[b_X[s]])
            else:
                k.op(POOL, lambda e: e.tensor_tensor(out=X[s][:, 512:1024], in0=X[s][:, 512:1024], in1=tA[:], op=ALU.add),
                     reads=[btA, b_X[s]], writes=[b_X[s]])
                k.op(POOL, lambda e: e.tensor_tensor(out=X[s][:, 0:512], in0=X[s][:, 0:512], in1=tB[:], op=ALU.add),
                     reads=[btB, b_X[s]], writes=[b_X[s]])
                if len(pend) == 2:
                    norm_b(pend.pop(0))
                pend.append(norm_a(s))
        while pend:
            norm_b(pend.pop(0))

    gt = 0
    if ntiles_pre:
        for s in range(4):
            load_x(x_prev, 0, s)
    else:
        for s in range(4):
            load_x(x_main, 0, s)
    for t in range(ntiles_pre):
        for s in range(4):
            norm_to_hT(s)
        for s in range(4):
            if t + 1 < ntiles_pre:
                load_x(x_prev, t + 1, s)
            else:
                load_x(x_main, 0, s)
        tick_budget[0] = n_phase0 + (t + 1) * (-(-(nconv - n_phase0) // max(ntiles_pre, 1))) - cv["next"]
        for gi in range(2):
            hgrn_group(gi, 0, False, last_pre=(t == ntiles_pre - 1))
        conv_run(n_phase0 + (t + 1) * (-(-(nconv - n_phase0) // max(ntiles_pre, 1))))
    conv_run(nconv, flush=True)

    pre_v = [None]
    for t in range(ntiles_main):
        if t == 0:
            for s in range(4):
                norm_to_hT(s)
        if stop == 2:
            return finish()
        if pre_v[0] is not None:
            sl_v = pre_v[0]
            pre_v[0] = None
        else:
            sl_v = [load_k8(wb_in, "in", c0) for c0 in (1024, 1536)]
        sl_u = [load_k8(wb_in, "in", c0) for c0 in (0, 512)]
        for s in range(4):
            gi_ = nxt("bt", 2)
            gv, bgv = BT[gi_], b_BT[gi_]
            sm, bsm = stat_cols(2)
            for hf in range(2):
                bi = tm_group(sl_v[hf], HT, s, [b_HT])
                k.op(ACT, lambda e: e.activation(out=gv[:, hf * 512:(hf + 1) * 512], in_=MMt[bi][:], func=AF.Gelu, accum_out=sm[:, hf:hf + 1]),
                     reads=[b_MM[bi]], writes=[bgv, bsm])
            nm0, bnm0 = stat_cols(1)
            nm_, bnm = stat_cols(1)
            k.op(POOL, lambda e: e.tensor_tensor(out=nm0, in0=sm[:, 0:1], in1=sm[:, 1:2], op=ALU.add), reads=[bsm], writes=[bnm0])
            k.op(POOL, lambda e: e.tensor_scalar(out=nm_, in0=nm0, scalar1=-1.0 / D, scalar2=1.0, op0=ALU.mult, op1=ALU.mult),
                 reads=[bnm0], writes=[bnm])
            vs, bvs = stat_cols(1)
            k.op(ACT, lambda e: e.activation(out=junk[:], in_=gv[:], func=AF.Square, bias=nm_, accum_out=vs),
                 reads=[bgv, bnm], writes=[b_junk, bvs])
            r, br = rstd_small(vs, bvs, 1, 1.0 / D)
            k.op(DVE, lambda e: e.tensor_scalar(out=VLN[:, s, :], in0=gv[:], scalar1=nm_, scalar2=r, op0=ALU.add, op1=ALU.mult),
                 reads=[bgv, bnm, br], writes=[b_VLN])
        for c in range(8):
            bi = fm_group(sl_u[c // 4], c % 4, HT, [b_HT])
            k.op(ACT, lambda e: e.activation(out=U[:, c, :], in_=MMt[bi][:], func=AF.Gelu), reads=[b_MM[bi]], writes=[b_U])
        for g in range(8):
            bi = mm_bank()

            def fs(e):
                ins = None
                for n in range(4):
                    ins = e.matmul(MMt[bi][:, n * 128:(n + 1) * 128], lhsT=VLN[:, n, g * 128:(g + 1) * 128], rhs=wspb[:, g, :], start=True, stop=True)
                return ins
            k.op(PE, fs, reads=[b_VLN, b_wspb], writes=[b_MM[bi]])
            tA, btA = ta()
            k.op(DVE, lambda e: e.scalar_tensor_tensor(out=tA.rearrange("p (n t) -> p n t", t=128), in0=MMt[bi].rearrange("p (n t) -> p n t", t=128),
                                                       scalar=cols[:, 16 + g:17 + g], in1=cg[:, g:g + 1, :].to_broadcast([128, 4, 128]),
                                                       op0=ALU.mult, op1=ALU.add), reads=[b_MM[bi], b_cols, b_cg], writes=[btA])
            k.op(POOL, lambda e: e.tensor_tensor(out=U[:, g, :], in0=tA[:], in1=U[:, g, :], op=ALU.mult), reads=[btA, b_U], writes=[b_U])
        if stop == 3:
            return finish()
        hgrn_group(0, gt, True)
        sl_ga = [load_k8(wb_in, "in", c0) for c0 in (6144, 6656)]

        def ga_item(c):
            bi = fm_group(sl_ga[c // 4], c % 4, HT, [b_HT], wide=False)
            k.op(ACT, lambda e: e.activation(out=R4[:, c, :], in_=MMt[bi][:], func=AF.Sigmoid), reads=[b_MM[bi]], writes=[b_R4])
        interleave(hgrn_pass1(0, gt), [(lambda c=c: ga_item(c)) for c in range(8)])
        sl_pa = [load_k8(wb_pa, "pa", c0) for c0 in (0, 512)]

        def pa_item(c):
            bi = fm_group(sl_pa[c // 4], c % 4, U, [b_U], wide=False)
            k.op(DVE, lambda e: e.tensor_tensor(out=R4[:, c, :], in0=MMt[bi][:], in1=R4[:, c, :], op=ALU.mult), reads=[b_MM[bi], b_R4], writes=[b_R4])
        interleave(hgrn_pass2(0, gt), [(lambda c=c: pa_item(c)) for c in range(8)])
        hgrn_group(1, gt, True)
        sl_gb = [load_k8(wb_in, "in", c0) for c0 in (7168, 7680)]

        def gb_item(c):
            bi = fm_group(sl_gb[c // 4], c % 4, HT, [b_HT], wide=False)
            k.op(ACT, lambda e: e.activation(out=GB[:, c, :], in_=MMt[bi][:], func=AF.Sigmoid), reads=[b_MM[bi]], writes=[b_U])
        interleave(hgrn_pass1(1, gt), [(lambda c=c: gb_item(c)) for c in range(8)])
        interleave(hgrn_pass2(1, gt), [])
        gt += 1
        sl_pb = [load_k8(wb_pb, "pb", c0) for c0 in (0, 512)]
        for c in range(8):
            bi = fm_group(sl_pb[c // 4], c % 4, R5, [b_R5])
            tA, btA = ta()
            k.op(DVE, lambda e: e.tensor_tensor(out=tA[:], in0=MMt[bi][:], in1=GB[:, c, :], op=ALU.mult), reads=[b_MM[bi], b_U], writes=[btA])
            k.op(POOL, lambda e: e.tensor_tensor(out=R4[:, c, :], in0=tA[:], in1=R4[:, c, :], op=ALU.add), reads=[btA, b_R4], writes=[b_R4])
        if stop == 6:
            return finish()
        out_proj(lambda kh, nh: load_k8(wb_out, "out", nh * 512), 8, R4, [b_R4], 0, False, t)
        if stop == 7:
            return finish()
        if t + 1 < ntiles_main:
            for s in range(4):
                k.dma(SP, d_xl[s], Xn[s], x_main[(t + 1) * T + s * 128:(t + 1) * T + (s + 1) * 128, :], writes=[b_Xn[s]])
        for blk in range(11):
            sl = load_up(blk)
            for jj in range(2):
                bg = fm_group(sl, jj, HT, [b_HT])
                bu = fm_group(sl, 2 + jj, HT, [b_HT])
                tA, btA = ta()
                k.op(ACT, lambda e: e.activation(out=tA[:], in_=MMt[bg][:], func=AF.Silu), reads=[b_MM[bg]], writes=[btA])
                k.op(DVE, lambda e: e.tensor_tensor(out=FACT[:, 2 * blk + jj, :], in0=MMt[bu][:], in1=tA[:], op=ALU.mult),
                     reads=[b_MM[bu], btA], writes=[b_FACT])
        def prefetch_v():
            if t + 1 < ntiles_main:
                pre_v[0] = [load_k8(wb_in, "in", c0) for c0 in (1024, 1536)]
        out_proj(load_dn, 22, FACT, [b_FACT], D, True, t, after_nh0=prefetch_v)

    return finish()


def _unused():
    for s in range(4):
        if d_xs[s].cnt:
            nc.sync.wait_ge(d_xs[s].sem, d_xs[s].cnt)
    nc.all_engine_barrier()
    for sm in k.sems:
        nc.gpsimd.sem_clear(sm)
    return nc


def _host_inputs(inp):
    f = lambda a: np.ascontiguousarray(np.asarray(a, dtype=np.float32))
    x = f(inp["x"])
    col = lambda v: f(v).reshape(8, 128).T
    cols = np.concatenate([col(inp["pre_mix_gain"][0]), col(inp["pre_ffn_gain"][0]), col(inp["sgu_norm_gain"][0]),
                           col(inp["hgrn_norm_gain"][0]), col(inp["lb_logits"][0]), col(inp["lb_logits"][1])], axis=1)
    rows = np.concatenate([np.broadcast_to(f(inp["post_mix_gain"][0])[None, :], (128, D)),
                           np.broadcast_to(f(inp["post_ffn_gain"][0])[None, :], (128, D))], axis=1)
    wspT = f(inp["w_spatial"][0]).transpose(2, 0, 1).reshape(128, 1024)
    ident = np.eye(128, dtype=np.float32)
    p = np.arange(128)
    tt = np.arange(64)
    msc = (p[:, None] % 64 <= tt[None, :]).astype(np.float32)
    mask_sc = np.tile(msc, (1, 4))
    mask_scan = np.ones((128, 512), np.float32)
    mask_scan[:, 0::64] = 0.0
    mask_sguT = (p[:, None] // 64 <= p[None, :] // 64).astype(np.float32)
    consts = np.concatenate([ident, mask_sc, mask_scan, mask_sguT], axis=1)
    sguA = np.stack([f(inp["sgu_norm_bias"][0]), np.ones(D, np.float32)], axis=0)
    bsp = f(inp["b_spatial"][0]).reshape(1, D)
    shared = {
        "w_in": f(inp["w_in"][0]), "w_pa": f(inp["w_proj_sgu"][0]), "w_pb": f(inp["w_proj_hgrn"][0]),
        "w_out": f(inp["w_out"][0]), "w_up": f(inp["w_ffn_up"][0]), "w_dn": f(inp["w_ffn_down"][0]),
        "cols": f(cols), "rows_bc": f(rows), "wspT": f(wspT), "consts": f(consts), "sguA": f(sguA), "bsp": bsp,
    }
    maps = []
    zeros = np.zeros((NTOK, D), np.float32)
    for c in range(8):
        b, half = c // 2, c % 2
        m = dict(shared)
        m["x_main"] = np.ascontiguousarray(x[b, half * NTOK:(half + 1) * NTOK])
        m["x_prev"] = np.ascontiguousarray(x[b, 0:NTOK]) if half == 1 else zeros
        maps.append(m)
    return maps


def kernel(**inputs):
    maps = _host_inputs(inputs)
    nc = build_nc()
    res = run_bass_kernel_spmd(nc, maps, core_ids=list(range(8)))
    out = np.empty((4, 8192, D), np.float32)
    for c in range(8):
        b, half = c // 2, c % 2
        out[b, half * NTOK:(half + 1) * NTOK] = np.asarray(res.results[c]["y_out"], dtype=np.float32)
    return out
```
